# Optimizing a Trainium2 kernel written in Bass

```python
import math
import jax, jax.numpy as jnp
from jax import lax
import numpy as np

D_MODEL = 1024
BATCH = 32
SEQ = 256
DEPTH = 4
DEC_BATCH = 8
DEC_SEQ = 4096
PAST_LEN = 512

GRID_W = 64
EPS = 1e-6
CHUNK = 64
SHORT_CONV = 3
N_EVEN = (DEPTH + 1) // 2
N_ODD = DEPTH // 2
HY_W = D_MODEL // 2
HY_ORDER = 2
HY_EMB = 33
HY_BANDS = (HY_EMB - 1) // 2
HY_FILTER_HIDDEN = 64
HY_TARGET = 1e-2
HY_FAST = 0.3
HY_SLOW = 1.5
M_HEADS = 4
M_DH = (D_MODEL // 2) // M_HEADS
M_W = M_HEADS * M_DH
EV_CONV = 3 * HY_W + 2 * M_W
EV_PROJ = EV_CONV + 2 * M_W + 4 * M_HEADS
EV_OUT_IN = HY_W + M_W
SSD_INNER = 2 * D_MODEL
SSD_HEADDIM = 64
SSD_HEADS = SSD_INNER // SSD_HEADDIM
SSD_GROUPS = 8
SSD_HPG = SSD_HEADS // SSD_GROUPS
SSD_STATE = 128
SSD_CONV_DIM = SSD_INNER + 2 * SSD_GROUPS * SSD_STATE
OD_PROJ = SSD_INNER + SSD_CONV_DIM + 2 * SSD_HEADS
FFN_HIDDEN = -(-8 * D_MODEL // (3 * 256)) * 256

kernel_name = "hybrid_hyena_mlstm_ssd_prefix_diffusion_step"


def _rmsnorm(x, g):
    xf = x.astype(jnp.float32)
    y = xf * lax.rsqrt(jnp.mean(xf * xf, axis=-1, keepdims=True) + EPS)
    return (y * g.astype(jnp.float32)).astype(x.dtype)


def _short_conv(u, w, b):
    L = u.shape[1]
    pad = SHORT_CONV // 2
    up = jnp.pad(u, ((0, 0), (pad, SHORT_CONV - 1 - pad), (0, 0)))
    y = b
    for j in range(SHORT_CONV):
        y = y + up[:, j:j + L] * w[j]
    return y


def _to_colmajor(x):
    B, L, C = x.shape
    rows = L // GRID_W
    return x.reshape(B, rows, GRID_W, C).swapaxes(1, 2).reshape(B, L, C)


def _from_colmajor(x):
    B, L, C = x.shape
    rows = L // GRID_W
    return x.reshape(B, GRID_W, rows, C).swapaxes(1, 2).reshape(B, L, C)


def _chunks(t):
    B, L = t.shape[:2]
    return t.reshape((B, L // CHUNK, CHUNK) + t.shape[2:]).swapaxes(0, 1)


def _unchunk(t):
    t = t.swapaxes(0, 1)
    return t.reshape((t.shape[0], t.shape[1] * t.shape[2]) + t.shape[3:])


def _hyena_spectra(L, w1, b1, w2, b2, w3, freq):
    f32 = jnp.float32
    t = jnp.linspace(0.0, 1.0, L, dtype=f32)[:, None]
    ang = (2.0 * math.pi / L) * jnp.arange(L, dtype=f32)[:, None] * \
        jnp.linspace(1e-4, HY_BANDS - 1, HY_BANDS, dtype=f32)[None]
    feats = jnp.concatenate([t, jnp.cos(ang), -jnp.sin(ang)], axis=-1)
    fr = freq.astype(f32)
    h = jnp.sin(fr * (feats @ w1.astype(f32) + b1.astype(f32)))
    h = jnp.sin(fr * (h @ w2.astype(f32) + b2.astype(f32)))
    h = (h @ w3.astype(f32)).reshape(L, 2, HY_ORDER, HY_W)
    deltas = jnp.abs(jnp.linspace(math.log(HY_TARGET) / HY_SLOW, math.log(HY_TARGET) / HY_FAST, HY_W, dtype=f32))
    h = h * jnp.exp(-t * deltas)[:, None, None, :]
    k = jnp.concatenate([h[:, 0], jnp.zeros((1, HY_ORDER, HY_W), f32), h[:0:-1, 1]], axis=0)
    k = k / jnp.sum(jnp.abs(k), axis=0, keepdims=True)
    return jnp.fft.rfft(k, axis=0)


def _hyena(v, x1, x2, w1, b1, w2, b2, w3, freq, bias):
    L = v.shape[1]
    K = _hyena_spectra(L, w1, b1, w2, b2, w3, freq)
    bias = bias.astype(jnp.float32)
    z = v.astype(jnp.float32)
    for o, gate in enumerate((x1, x2)):
        Z = jnp.fft.rfft(z, n=2 * L, axis=1)
        y = jnp.fft.irfft(Z * K[:, o], n=2 * L, axis=1)[:, :L]
        z = gate.astype(jnp.float32) * (y + z * bias[o])
    return z.astype(v.dtype)


def _mlstm_scan(q, k, v, ig, lf, C0, n0, m0):
    mask = jnp.tril(jnp.ones((CHUNK, CHUNK), bool))[None, :, :, None]

    def step(carry, inp):
        C, n, m = carry
        qc, kc, vc, ic, fc = inp
        b = jnp.cumsum(fc, axis=1)
        dlog = jnp.where(mask, b[:, :, None] - b[:, None, :] + ic[:, None, :], -jnp.inf)
        inter = b + m[:, None]
        mt = jnp.maximum(inter, jnp.max(dlog, axis=2))
        w = jnp.exp(dlog - mt[:, :, None])
        a_in = jnp.exp(inter - mt)
        s = jnp.einsum('blhd,bshd->blsh', qc, kc) * w
        num = jnp.einsum('blsh,bshd->blhd', s, vc) + a_in[..., None] * jnp.einsum('blhd,bhde->blhe', qc, C)
        den = jnp.sum(s, axis=2) + a_in * jnp.einsum('blhd,bhd->blh', qc, n)
        h = num / jnp.maximum(jnp.abs(den), jnp.exp(-mt))[..., None]
        m_new = mt[:, -1]
        w_end = jnp.exp(b[:, -1:] - b + ic - m_new[:, None])
        dec = jnp.exp(b[:, -1] + m - m_new)
        kw = kc * w_end[..., None]
        C_new = dec[..., None, None] * C + jnp.einsum('bshd,bshe->bhde', kw, vc)
        n_new = dec[..., None] * n + jnp.sum(kw, axis=1)
        return (C_new, n_new, m_new), h

    (C, n, m), hs = lax.scan(step, (C0, n0, m0),
                             (_chunks(q), _chunks(k), _chunks(v), _chunks(ig), _chunks(lf)))
    return _unchunk(hs), C, n, m


def _mlstm(q, k, v, o, gates, gate_b, norm_g, st):
    f32 = jnp.float32
    B, L, _ = q.shape
    q = q.astype(f32).reshape(B, L, M_HEADS, M_DH)
    k = k.astype(f32).reshape(B, L, M_HEADS, M_DH) * (M_DH ** -0.5)
    v = v.astype(f32).reshape(B, L, M_HEADS, M_DH)
    g = (gates.astype(f32) + gate_b.astype(f32)).reshape(B, L, 4, M_HEADS)
    i_f, lf_f = g[:, :, 0], jax.nn.log_sigmoid(g[:, :, 1])
    i_b, lf_b = g[:, :, 2], jax.nn.log_sigmoid(g[:, :, 3])
    C0, n0, m0 = (s.astype(f32) for s in st)
    hf, Cf, nf, mf = _mlstm_scan(q, k, v, i_f, lf_f, C0[:, 0], n0[:, 0], m0[:, 0])
    fl = lambda t: jnp.flip(t, axis=1)
    hb, Cb, nb, mb = _mlstm_scan(fl(q), fl(k), fl(v), fl(i_b), fl(lf_b), C0[:, 1], n0[:, 1], m0[:, 1])
    h = hf + fl(hb)
    h = h * lax.rsqrt(jnp.mean(h * h, axis=-1, keepdims=True) + EPS) * norm_g.astype(f32).reshape(M_HEADS, M_DH)
    h = jax.nn.sigmoid(o.astype(f32)) * h.reshape(B, L, M_W)
    new_st = (jnp.stack([Cf, Cb], 1), jnp.stack([nf, nb], 1), jnp.stack([mf, mb], 1))
    return h, new_st


def _ssd_scan(x, a, Bm, Cm, S0):
    mask = jnp.tril(jnp.ones((CHUNK, CHUNK), bool))[None, :, :, None, None]

    def step(S, inp):
        xc, ac, bc, cc = inp
        acs = jnp.cumsum(ac, axis=1)
        Lm = jnp.exp(jnp.where(mask, acs[:, :, None] - acs[:, None, :], -jnp.inf))
        cb = jnp.einsum('blgn,bsgn->blsg', cc, bc)
        y = jnp.einsum('blsgr,bsgrp->blgrp', cb[..., None] * Lm, xc) + \
            jnp.einsum('blgn,bgrpn->blgrp', cc, S) * jnp.exp(acs)[..., None]
        dec_end = jnp.exp(acs[:, -1:] - acs)
        S_new = jnp.exp(acs[:, -1])[..., None, None] * S + \
            jnp.einsum('blgn,blgrp->bgrpn', bc, xc * dec_end[..., None])
        return S_new, y

    S, ys = lax.scan(step, S0, (_chunks(x), _chunks(a), _chunks(Bm), _chunks(Cm)))
    return _unchunk(ys), S


def _ssd(z, xbc, dt_pre, dt_bias, A_log, Dskip, norm_g, st):
    f32 = jnp.float32
    B, L, _ = z.shape
    xbc = xbc.astype(f32)
    xs = xbc[..., :SSD_INNER].reshape(B, L, SSD_GROUPS, SSD_HPG, SSD_HEADDIM)
    GN = SSD_GROUPS * SSD_STATE
    Bm = xbc[..., SSD_INNER:SSD_INNER + GN].reshape(B, L, SSD_GROUPS, SSD_STATE)
    Cm = xbc[..., SSD_INNER + GN:].reshape(B, L, SSD_GROUPS, SSD_STATE)
    dt = jax.nn.softplus(dt_pre.astype(f32).reshape(B, L, 2, SSD_HEADS) + dt_bias.astype(f32))
    A = -jnp.exp(A_log.astype(f32))
    S0 = st.astype(f32)
    fl = lambda t: jnp.flip(t, axis=1)
    outs = []
    for d in range(2):
        dtd = dt[:, :, d].reshape(B, L, SSD_GROUPS, SSD_HPG)
        xd = xs * dtd[..., None]
        ad = dtd * A[d].reshape(SSD_GROUPS, SSD_HPG)
        s0 = S0[:, d].reshape(B, SSD_GROUPS, SSD_HPG, SSD_HEADDIM, SSD_STATE)
        if d == 0:
            y, S = _ssd_scan(xd, ad, Bm, Cm, s0)
        else:
            y, S = _ssd_scan(fl(xd), fl(ad), fl(Bm), fl(Cm), s0)
            y = fl(y)
        outs.append((y, S.reshape(B, SSD_HEADS, SSD_HEADDIM, SSD_STATE)))
    y = outs[0][0] + outs[1][0] + Dskip.astype(f32).reshape(SSD_GROUPS, SSD_HPG)[..., None] * xs
    y = y.reshape(B, L, SSD_INNER) * jax.nn.silu(z.astype(f32))
    y = y * lax.rsqrt(jnp.mean(y * y, axis=-1, keepdims=True) + EPS) * norm_g.astype(f32)
    return y, jnp.stack([outs[0][1], outs[1][1]], 1)


def _even_mixer(h, e, st, P):
    proj = h @ P['ev_in_w'][e]
    cv = _short_conv(proj[..., :EV_CONV], P['ev_conv_w'][e], P['ev_conv_b'][e])
    hv, hx1, hx2 = cv[..., :HY_W], cv[..., HY_W:2 * HY_W], cv[..., 2 * HY_W:3 * HY_W]
    o0 = 3 * HY_W
    mq = jax.nn.silu(cv[..., o0:o0 + M_W])
    mk = jax.nn.silu(cv[..., o0 + M_W:o0 + 2 * M_W])
    mv = proj[..., EV_CONV:EV_CONV + M_W]
    mo = proj[..., EV_CONV + M_W:EV_CONV + 2 * M_W]
    mg = proj[..., EV_CONV + 2 * M_W:]
    y_hy = _hyena(hv, hx1, hx2, P['hy_w1'][e], P['hy_b1'][e], P['hy_w2'][e], P['hy_b2'][e],
                  P['hy_w3'][e], P['hy_freq'][e], P['hy_bias'][e])
    y_m, new_st = _mlstm(mq, mk, mv, mo, mg, P['m_gate_b'][e], P['m_norm_g'][e], st)
    y = jnp.concatenate([y_hy.astype(h.dtype), y_m.astype(h.dtype)], axis=-1)
    return y @ P['ev_out_w'][e], new_st


def _odd_mixer(h, o, st, P):
    proj = h @ P['od_in_w'][o]
    z = proj[..., :SSD_INNER]
    xbc = jax.nn.silu(_short_conv(proj[..., SSD_INNER:SSD_INNER + SSD_CONV_DIM], P['od_conv_w'][o], P['od_conv_b'][o]))
    dt_pre = proj[..., SSD_INNER + SSD_CONV_DIM:]
    y, new_st = _ssd(z, xbc, dt_pre, P['ssd_dt_bias'][o], P['ssd_A_log'][o], P['ssd_D'][o], P['ssd_norm_g'][o], st)
    return y.astype(h.dtype) @ P['od_out_w'][o], new_st


def _trunk(x, cond, init_states, grid, P):
    states_out = []
    for l in range(DEPTH):
        mod = (jax.nn.silu(cond) @ P['mod_w'][l] + P['mod_b'][l])[:, None, :]
        sh1, sc1, g1, sh2, sc2, g2 = jnp.split(mod, 6, axis=-1)
        h = _rmsnorm(x, P['norm1_g'][l]) * (1 + sc1) + sh1
        colmajor = grid and (l // 2) % 2 == 1
        if colmajor:
            h = _to_colmajor(h)
        if l % 2 == 0:
            out, st = _even_mixer(h, l // 2, init_states[l], P)
        else:
            out, st = _odd_mixer(h, l // 2, init_states[l], P)
        if colmajor:
            out = _from_colmajor(out)
        x = x + g1 * out
        h = _rmsnorm(x, P['norm2_g'][l]) * (1 + sc2) + sh2
        ff = (jax.nn.silu(h @ P['ffn_w1'][l]) * (h @ P['ffn_w3'][l])) @ P['ffn_w2'][l]
        x = x + g2 * ff
        states_out.append(st)
    return _rmsnorm(x, P['final_g']), states_out


def setup_inputs(seed: int = 0) -> dict:
    key = jax.random.key(seed)
    ks = iter(jax.random.split(key, 64))
    nrm = lambda shape, s: s * jax.random.normal(next(ks), shape, jnp.float32)
    D = D_MODEL
    inp = {}
    inp['x_prompt'] = nrm((BATCH, SEQ, D), 1.0)
    inp['x_sample'] = nrm((DEC_BATCH, DEC_SEQ, D), 1.0)
    inp['state_mlstm_C'] = nrm((DEC_BATCH, N_EVEN, 2, M_HEADS, M_DH, M_DH), 0.1)
    inp['state_mlstm_n'] = nrm((DEC_BATCH, N_EVEN, 2, M_HEADS, M_DH), 0.1)
    inp['state_mlstm_m'] = nrm((DEC_BATCH, N_EVEN, 2, M_HEADS), 1.0)
    inp['state_ssd'] = nrm((DEC_BATCH, N_ODD, 2, SSD_HEADS, SSD_HEADDIM, SSD_STATE), 0.1)
    inp['c'] = nrm((DEC_BATCH, D), 1.0)
    inp['c_ctx'] = nrm((D,), 1.0)
    inp['mod_w'] = nrm((DEPTH, D, 6 * D), 0.5 * D ** -0.5)
    inp['mod_b'] = nrm((DEPTH, 6 * D), 0.02)
    inp['norm1_g'] = 1.0 + nrm((DEPTH, D), 0.02)
    inp['norm2_g'] = 1.0 + nrm((DEPTH, D), 0.02)
    inp['ffn_w1'] = nrm((DEPTH, D, FFN_HIDDEN), D ** -0.5)
    inp['ffn_w3'] = nrm((DEPTH, D, FFN_HIDDEN), D ** -0.5)
    inp['ffn_w2'] = nrm((DEPTH, FFN_HIDDEN, D), FFN_HIDDEN ** -0.5)
    inp['final_g'] = 1.0 + nrm((D,), 0.02)
    inp['ev_in_w'] = nrm((N_EVEN, D, EV_PROJ), D ** -0.5)
    inp['ev_conv_w'] = nrm((N_EVEN, SHORT_CONV, EV_CONV), SHORT_CONV ** -0.5)
    inp['ev_conv_b'] = nrm((N_EVEN, EV_CONV), 0.02)
    inp['hy_w1'] = nrm((N_EVEN, HY_EMB, HY_FILTER_HIDDEN), HY_EMB ** -0.5)
    inp['hy_b1'] = nrm((N_EVEN, HY_FILTER_HIDDEN), 0.1)
    inp['hy_w2'] = nrm((N_EVEN, HY_FILTER_HIDDEN, HY_FILTER_HIDDEN), HY_FILTER_HIDDEN ** -0.5)
    inp['hy_b2'] = nrm((N_EVEN, HY_FILTER_HIDDEN), 0.1)
    inp['hy_w3'] = nrm((N_EVEN, HY_FILTER_HIDDEN, 2 * HY_ORDER * HY_W), HY_FILTER_HIDDEN ** -0.5)
    inp['hy_freq'] = 1.0 + nrm((N_EVEN, HY_FILTER_HIDDEN), 0.02)
    inp['hy_bias'] = nrm((N_EVEN, HY_ORDER, HY_W), 0.5)
    ib = nrm((N_EVEN, 2, M_HEADS), 0.1)
    fb = jnp.linspace(3.0, 6.0, M_HEADS, dtype=jnp.float32) + nrm((N_EVEN, 2, M_HEADS), 0.1)
    inp['m_gate_b'] = jnp.stack([ib[:, 0], fb[:, 0], ib[:, 1], fb[:, 1]], axis=1).reshape(N_EVEN, 4 * M_HEADS)
    inp['m_norm_g'] = 1.0 + nrm((N_EVEN, M_W), 0.02)
    inp['ev_out_w'] = nrm((N_EVEN, EV_OUT_IN, D), EV_OUT_IN ** -0.5)
    inp['od_in_w'] = nrm((N_ODD, D, OD_PROJ), D ** -0.5)
    inp['od_conv_w'] = nrm((N_ODD, SHORT_CONV, SSD_CONV_DIM), SHORT_CONV ** -0.5)
    inp['od_conv_b'] = nrm((N_ODD, SSD_CONV_DIM), 0.02)
    u = jax.random.uniform(next(ks), (N_ODD, 2, SSD_HEADS), jnp.float32)
    dt0 = jnp.exp(u * (math.log(0.1) - math.log(0.001)) + math.log(0.001))
    inp['ssd_dt_bias'] = dt0 + jnp.log(-jnp.expm1(-dt0))
    inp['ssd_A_log'] = jnp.log(jax.random.uniform(next(ks), (N_ODD, 2, SSD_HEADS), jnp.float32, 1.0, 16.0))
    inp['ssd_D'] = 1.0 + nrm((N_ODD, SSD_HEADS), 0.1)
    inp['ssd_norm_g'] = 1.0 + nrm((N_ODD, SSD_INNER), 0.02)
    inp['od_out_w'] = nrm((N_ODD, SSD_INNER, D), SSD_INNER ** -0.5)
    return inp


def reference(x_prompt, x_sample, state_mlstm_C, state_mlstm_n, state_mlstm_m, state_ssd, c, c_ctx,
              mod_w, mod_b, norm1_g, norm2_g, ffn_w1, ffn_w3, ffn_w2, final_g,
              ev_in_w, ev_conv_w, ev_conv_b, hy_w1, hy_b1, hy_w2, hy_b2, hy_w3, hy_freq, hy_bias,
              m_gate_b, m_norm_g, ev_out_w,
              od_in_w, od_conv_w, od_conv_b, ssd_dt_bias, ssd_A_log, ssd_D, ssd_norm_g, od_out_w):
    P = dict(mod_w=mod_w, mod_b=mod_b, norm1_g=norm1_g, norm2_g=norm2_g, ffn_w1=ffn_w1, ffn_w3=ffn_w3,
             ffn_w2=ffn_w2, final_g=final_g, ev_in_w=ev_in_w, ev_conv_w=ev_conv_w, ev_conv_b=ev_conv_b,
             hy_w1=hy_w1, hy_b1=hy_b1, hy_w2=hy_w2, hy_b2=hy_b2, hy_w3=hy_w3, hy_freq=hy_freq, hy_bias=hy_bias,
             m_gate_b=m_gate_b, m_norm_g=m_norm_g, ev_out_w=ev_out_w, od_in_w=od_in_w, od_conv_w=od_conv_w,
             od_conv_b=od_conv_b, ssd_dt_bias=ssd_dt_bias, ssd_A_log=ssd_A_log, ssd_D=ssd_D,
             ssd_norm_g=ssd_norm_g, od_out_w=od_out_w)
    f32 = jnp.float32
    Bp = x_prompt.shape[0]
    ctx_init = []
    for l in range(DEPTH):
        if l % 2 == 0:
            ctx_init.append((jnp.zeros((Bp, 2, M_HEADS, M_DH, M_DH), f32),
                             jnp.zeros((Bp, 2, M_HEADS, M_DH), f32),
                             jnp.zeros((Bp, 2, M_HEADS), f32)))
        else:
            ctx_init.append(jnp.zeros((Bp, 2, SSD_HEADS, SSD_HEADDIM, SSD_STATE), f32))
    y_prompt, ctx_states = _trunk(x_prompt, c_ctx[None, :], ctx_init, False, P)
    lat_init = []
    for l in range(DEPTH):
        if l % 2 == 0:
            e = l // 2
            lat_init.append((state_mlstm_C[:, e], state_mlstm_n[:, e], state_mlstm_m[:, e]))
        else:
            lat_init.append(state_ssd[:, l // 2])
    y_sample, _ = _trunk(x_sample, c, lat_init, True, P)
    dt = x_prompt.dtype
    new_mlstm_C = jnp.stack([ctx_states[l][0] for l in range(0, DEPTH, 2)], axis=1).astype(dt)
    new_mlstm_n = jnp.stack([ctx_states[l][1] for l in range(0, DEPTH, 2)], axis=1).astype(dt)
    new_mlstm_m = jnp.stack([ctx_states[l][2] for l in range(0, DEPTH, 2)], axis=1).astype(dt)
    new_ssd = jnp.stack([ctx_states[l] for l in range(1, DEPTH, 2)], axis=1).astype(dt)
    return (y_prompt, y_sample, new_mlstm_C, new_mlstm_n, new_mlstm_m, new_ssd)
```

```python
import contextlib
import math
import numpy as np
import concourse.bass as bass
import concourse.mybir as mybir
from concourse.bass_utils import run_bass_kernel_spmd

F32 = mybir.dt.float32
BF16 = mybir.dt.bfloat16
F32R = mybir.dt.float32r
I32 = mybir.dt.int32
ALU = mybir.AluOpType
AF = mybir.ActivationFunctionType
AX = mybir.AxisListType


class Buf:
    __slots__ = ("name", "w", "r", "dsem", "dcnt")

    def __init__(self, name):
        self.name = name
        self.w = None
        self.r = {}
        self.dsem = None
        self.dcnt = 0


class Sched:
    ENGS = ("pe", "act", "dve", "pool", "sp")

    def __init__(self, nc, stack):
        self.nc = nc
        self.stack = stack
        self.h = {"pe": nc.tensor, "act": nc.scalar, "dve": nc.vector, "pool": nc.gpsimd, "sp": nc.sync}
        self.sems = {}
        self.issued = {}
        self.prog = {e: [] for e in self.ENGS}
        self.waited = {e: {} for e in self.ENGS}
        self.n = {e: 0 for e in self.ENGS}
        for e in self.ENGS:
            self._sem("E_" + e)
        self.nd = 0
        self.dma_sem_pool = []
        self.pool_i = 0

    def _sem(self, key):
        if key not in self.sems:
            self.sems[key] = self.stack.enter_context(self.nc.semaphore(key))
            self.issued[key] = 0
        return key

    def buf(self, name, dma=False, persistent=False):
        b = Buf(name)
        if dma:
            if persistent:
                b.dsem = self._sem("D_" + name)
            else:
                if self.pool_i == len(self.dma_sem_pool):
                    self.dma_sem_pool.append(self._sem("D_pool%d" % self.pool_i))
                b.dsem = self.dma_sem_pool[self.pool_i]
                self.pool_i += 1
        return b

    def recycle(self):
        self.pool_i = 0

    def share_dsem(self, b, other):
        b.dsem = other.dsem
        return b

    def _deps(self, eng, reads, writes):
        deps = {}

        def add(k, v):
            if k is None:
                return
            if deps.get(k, 0) < v:
                deps[k] = v
        for b in reads:
            if b.w is not None:
                add(*b.w)
        for b in writes:
            if b.w is not None:
                add(*b.w)
            for k, v in b.r.items():
                add(k, v)
        out = []
        wd = self.waited[eng]
        for k, v in deps.items():
            if eng == "pe" and k == "E_pe":
                continue
            if k.startswith("D_"):
                v = self.issued[k]
            if wd.get(k, 0) >= v:
                continue
            wd[k] = v
            out.append((k, v))
        return out

    def op(self, eng, fn, reads=(), writes=()):
        waits = self._deps(eng, reads, writes)
        self.n[eng] += 1
        key = "E_" + eng
        val = self.n[eng]
        self.issued[key] = val
        self.prog[eng].append((waits, fn, key, 1))
        for b in reads:
            if b.r.get(key, 0) < val:
                b.r[key] = val
        for b in writes:
            b.w = (key, val)
            b.r = {}
        return val

    def dma(self, eng, out_ap, in_ap, dst, src=(), **kw):
        assert dst.dsem is not None, dst.name
        waits = self._deps(eng, src, [dst])
        key = dst.dsem
        self.issued[key] += 16
        val = self.issued[key]

        def fn(e, out_ap=out_ap, in_ap=in_ap, kw=kw):
            return e.dma_start(out=out_ap, in_=in_ap, **kw)
        self.prog[eng].append((waits, fn, key, 16))
        for b in src:
            if b.r.get(key, 0) < val:
                b.r[key] = val
        dst.w = (key, val)
        dst.r = {}
        self.nd += 1

    def barrier(self):
        snap = dict(self.issued)
        for e in self.ENGS:
            waits = []
            wd = self.waited[e]
            for k, v in snap.items():
                if v == 0 or wd.get(k, 0) >= v:
                    continue
                if e == "pe" and k == "E_pe":
                    continue
                wd[k] = v
                waits.append((k, v))
            if waits:
                self.prog[e].append((waits, None, None, 0))
        self.recycle()

    def finish(self, eng="sp", bufs=()):
        waits = self._deps(eng, bufs, [])
        self.prog[eng].append((waits, None, None, 0))

    def emit(self):
        nc = self.nc
        with nc.Block() as block:
            def mk(ename):
                def body(e):
                    for waits, fn, key, inc in self.prog[ename]:
                        if fn is None:
                            for k, v in waits:
                                e.wait_ge(self.sems[k], v)
                            continue
                        if ename == "pe":
                            for k, v in waits:
                                e.wait_ge(self.sems[k], v)
                            ins = fn(e)
                        else:
                            for k, v in waits[1:]:
                                e.wait_ge(self.sems[k], v)
                            ins = fn(e)
                            if waits:
                                ins._wait_ge(self.sems[waits[0][0]], waits[0][1])
                        ins.then_inc(self.sems[key], inc)
                return body
            block.tensor(mk("pe"))
            block.scalar(mk("act"))
            block.vector(mk("dve"))
            block.gpsimd(mk("pool"))
            block.sync(mk("sp"))


D = 1024
FH = 2816
EV_NP, OD_NP = 3600, 6208
EPS = 1e-6


def cdiv(a, b):
    return -(-a // b)


class Arena:
    def __init__(self, t, words):
        self.t, self.words, self.off = t, words, 0

    def reset(self):
        self.off = 0

    def alloc(self, free_shape, dtype=F32, parts=128):
        n = int(np.prod(free_shape))
        esz = 2 if dtype == BF16 else 4
        w = cdiv(n * esz, 4)
        assert self.off + w <= self.words, ("arena overflow", self.off, w, self.words)
        ap = self.t[0:parts, self.off:self.off + w]
        self.off += w
        if dtype != F32 and dtype != F32R:
            ap = ap.bitcast(dtype)
        if len(free_shape) == 2:
            ap = ap.rearrange("p (a b) -> p a b", a=free_shape[0])
        elif len(free_shape) == 3:
            ap = ap.rearrange("p (a b c) -> p a b c", a=free_shape[0], b=free_shape[1])
        return ap


class Dual:
    def __init__(self, a32, a32r):
        self.a, self.r = a32, a32r

    def alloc(self, free_shape, dtype=F32, parts=128):
        return (self.r if dtype == F32R else self.a).alloc(free_shape, dtype, parts)

    @property
    def off(self):
        return (self.a.off, self.r.off)

    @off.setter
    def off(self, v):
        self.a.off, self.r.off = v

    def reset(self):
        self.a.off = self.r.off = 0


class MK:
    def __init__(self, cfg):
        self.cfg = cfg
        self.seqs = cfg["seqs"]
        self.T = sum(s[1] for s in self.seqs)
        self.depth = cfg.get("depth", 4)
        self.dbg = cfg.get("debug", ())
        self.nc = bass.Bass("TRN2", target_bir_lowering=False)
        self.st = contextlib.ExitStack()
        self.S = Sched(self.nc, self.st)
        self.ins = {}
        self.nbuf = 0

    def din(self, name, shape, dtype=F32):
        t = self.nc.dram_tensor(name, list(shape), dtype, kind="ExternalInput")
        self.ins[name] = t
        return t.ap()

    def dscr(self, name, shape, out=False):
        kind = "ExternalOutput" if (out or name in self.dbg) else "Internal"
        return self.nc.dram_tensor(name, list(shape), F32, kind=kind).ap()

    def B(self, name, dma=False, persistent=False):
        self.nbuf += 1
        return self.S.buf("%s_%d" % (name, self.nbuf), dma, persistent)

    def tiles(self):
        out = []
        g0 = [s for s in self.seqs if s[2] == 0]
        g1 = [s for s in self.seqs if s[2] == 1]
        for grp in (g0, g1):
            if not grp:
                continue
            a = grp[0][0]
            b = grp[-1][0] + grp[-1][1]
            assert (b - a) % 512 == 0
            for t0 in range(a, b, 512):
                out.append((t0, grp[0][2]))
        return out

    def setup(self):
        nc, st, T = self.nc, self.st, self.T
        self.xin = self.din("xin", [T, D])
        self.condT = self.din("condT", [128, 16])
        self.ident_d = self.din("ident", [128, 128])
        W = {}
        for name, shape in WSHAPES.items():
            W[name] = self.din(name, shape)
        self.W = W
        self.X = self.dscr("X", [T, D])
        self.MODS = self.dscr("MODS", [2, 128, 6 * D])
        self.bMODS = self.B("MODS", True, True)
        self.Xalt = self.dscr("Xalt", [T, D])
        self.bXalt = self.B("Xalt", True, True)
        self.PF = self.dscr("PF", [49 * 128, T])
        self.PT = self.dscr("PT", [T, 49 * 128])
        self.YT = self.dscr("YT", [T, 2048])
        self.yout = self.dscr("yout", [T, D], out=True)
        self.bX, self.bPF, self.bPT, self.bYT, self.bYO = (self.B("X", True, True), self.B("PF", True, True), self.B("PT", True, True),
                                                        self.B("YT", True, True), self.B("YO", True, True))
        AW, AWR = 21 * 1024 + 768, 29 * 1024
        self.arena_t = st.enter_context(nc.sbuf_tensor("arena", [128, AW], F32))
        self.arena_r = st.enter_context(nc.sbuf_tensor("arenar", [128, AWR], F32R))
        self.ar = Dual(Arena(self.arena_t, AW), Arena(self.arena_r, AWR))
        self.pers_t = st.enter_context(nc.sbuf_tensor("pers", [128, 1024], F32))
        self.pers = Arena(self.pers_t, 1024)
        self.banks = [st.enter_context(nc.psum_tensor("bank%d" % i, [128, 512], F32)) for i in range(8)]
        self.bbank = [self.B("bank%d" % i) for i in range(8)]
        S = self.S
        self.ident = self.pers.alloc([128])
        self.bconst = self.B("const", True, True)
        S.dma("sp", self.ident, self.ident_d, self.bconst)
        self.csil = self.pers.alloc([16])
        self.bcs = self.B("csil", True, True)
        S.dma("sp", self.csil, self.condT, self.bcs)
        S.op("act", lambda e: e.activation(out=self.csil, in_=self.csil, func=AF.Silu), [self.bcs], [self.bcs])
        self.negmask = self.pers.alloc([2, 128])
        self.negmask_d = self.din("negmask", [128, 256])
        S.dma("sp", self.negmask, self.negmask_d.rearrange("p (a b) -> p a b", a=2), self.bconst)
        self.onecol = self.pers.alloc([1])
        S.op("pool", lambda e: e.memset(self.onecol, 1.0), [], [self.bconst])
        mk_ = self

        class SelView:
            def __getitem__(self, key):
                rows, r, _ = key
                n = rows.stop - rows.start
                return mk_.ident[rows, r:r + 1].to_broadcast([n, 128])

        class OnesView:
            def __getitem__(self, key):
                rows, cols_ = key
                return mk_.onecol[rows, 0:1].to_broadcast([rows.stop - rows.start, cols_.stop - cols_.start])
        self.sel = SelView()
        self.ones_row = OnesView()
        self.st_ssd = self.din("st_ssd", [2 * 2 * 32 * 64, 128])
        self.st_C = self.din("st_C", [2 * 2 * 4 * 128, 128])
        self.st_n = self.din("st_n", [2 * 2 * 4, 128])
        self.st_m = self.din("st_m", [1, 16])
        self.o_ssd = self.dscr("newssd", [4 * 2 * 2 * 32 * 64, 128], out=True)
        self.o_C = self.dscr("newC", [4 * 2 * 2 * 4 * 128, 128], out=True)
        self.o_n = self.dscr("newn", [4 * 2 * 2 * 4, 128], out=True)
        self.o_m = self.dscr("newm", [1, 64], out=True)
        self.bso = self.B("so", True, True)
        self.YS = self.dscr("YS", [T, 2048])
        self.bYS = self.B("YS", True, True)
        self.nrot = 8
        self.rr = 0

    def evac_eng(self):
        self.rr += 1
        return "act" if self.rr % 2 else "dve"

    def copy(self, eng, out, in_, reads, writes):
        if eng == "act":
            self.S.op("act", lambda e: e.activation(out=out, in_=in_, func=AF.Copy), reads, writes)
        else:
            self.S.op(eng, lambda e: e.tensor_copy(out=out, in_=in_), reads, writes)

    def phase_mod(self, l, reload=False):
        S, ar, W = self.S, self.ar, self.W
        S.barrier()
        ar.reset()
        modv = [ar.alloc([6, D]) for _ in range(2)]
        self.modv = modv
        self.bmod = [self.B("modv", True) for _ in range(2)]
        self.arena_base = ar.off
        if reload:
            for g in range(2):
                S.dma("sp", modv[g].rearrange("p a b -> p (a b)"), self.MODS[g], self.bmod[g], [self.bMODS])
            return
        cb = ar.alloc([16, 128], F32R)
        bcb = self.B("cb")
        for i in range(16):
            S.op("dve", lambda e, i=i: e.tensor_copy(out=cb[:, i, :], in_=self.csil[:, i:i + 1].to_broadcast([128, 128])),
                 [self.bcs], [bcb])
        mb = ar.alloc([6 * D])
        bmb = self.B("mb", True)
        S.dma("sp", mb, W["mod_b"][l, :].partition_broadcast(128), bmb)
        wp = [ar.alloc([8, 512], F32R) for _ in range(2)]
        bwp = [self.B("wp", True) for _ in range(2)]
        wv = W["mod_w"][l].rearrange("(kc p) n -> p kc n", p=128)
        for nt in range(12):
            s = nt % 2
            S.dma("pool", wp[s], wv[:, :, nt * 512:(nt + 1) * 512], bwp[s])
            for g in range(2):
                bk = (2 * nt + g) % 8
                for kc in range(8):
                    S.op("pe", lambda e, g=g, kc=kc, bk=bk, s=s: e.matmul(self.banks[bk][:], lhsT=cb[:, g * 8 + kc, :],
                                                                        rhs=wp[s][:, kc, :], start=(kc == 0), stop=(kc == 7)),
                         [bcb, bwp[s]], [self.bbank[bk]])
                dst = modv[g].rearrange("p a b -> p (a b)")[:, nt * 512:(nt + 1) * 512]
                S.op("dve", lambda e, dst=dst, bk=bk, nt=nt: e.tensor_tensor(out=dst, in0=self.banks[bk][:],
                                                                          in1=mb[:, nt * 512:(nt + 1) * 512], op=ALU.add),
                     [self.bbank[bk], bmb], [self.bmod[g]])
        gb = ar.alloc([2, D])
        bgb = self.B("gb", True)
        S.dma("sp", gb[:, 0, :], W["norm1_g"][l, :].partition_broadcast(128), bgb)
        S.dma("sp", gb[:, 1, :], W["norm2_g"][l, :].partition_broadcast(128), bgb)
        for g in range(2):
            for k, slot in ((0, 1), (1, 4)):
                S.op("dve", lambda e, g=g, k=k, slot=slot: e.scalar_tensor_tensor(
                    out=modv[g][:, slot, :], in0=modv[g][:, slot, :], scalar=1.0, in1=gb[:, k, :],
                    op0=ALU.add, op1=ALU.mult), [self.bmod[g], bgb], [self.bmod[g]])
        for g in range(2):
            S.dma("sp", self.MODS[g], modv[g].rearrange("p a b -> p (a b)"), self.bMODS, [self.bmod[g]])
        S.barrier()
        ar.off = self.arena_base

    def norm_tile(self, x, bx, h, bh, A, Bv, bAB, junk, bj, st2, bst):
        S = self.S
        S.op("act", lambda e: e.activation(out=junk, in_=x, func=AF.Square, accum_out=st2[:, 0:1]), [bx], [bj, bst])
        S.op("dve", lambda e: e.tensor_scalar(out=st2[:, 1:2], in0=st2[:, 0:1], scalar1=1.0 / D, scalar2=EPS,
                                              op0=ALU.mult, op1=ALU.add), [bst], [bst])
        S.op("act", lambda e: e.activation(out=st2[:, 3:4], in_=st2[:, 1:2], func=AF.Sqrt), [bst], [bst])
        S.op("dve", lambda e: e.reciprocal(out=st2[:, 2:3], in_=st2[:, 3:4]), [bst], [bst])
        if A is None:
            S.op("dve", lambda e: e.tensor_scalar(out=h, in0=x, scalar1=st2[:, 2:3], scalar2=None, op0=ALU.mult),
                 [bx, bst], [bh])
            return
        S.op("dve", lambda e: e.scalar_tensor_tensor(out=h, in0=x, scalar=st2[:, 2:3], in1=A, op0=ALU.mult, op1=ALU.mult),
             [bx, bst] + bAB, [bh])
        if Bv is not None:
            S.op("pool", lambda e: e.tensor_tensor(out=h, in0=h, in1=Bv, op=ALU.add), [bh] + bAB, [bh])

    def to_kmajor(self, src, bsrc, nk, dstT, bdst, col0):
        S = self.S
        for k0 in range(0, nk, 4):
            n = min(4, nk - k0)
            bk = self.next_bank()
            for j in range(n):
                S.op("pe", lambda e, j=j, k0=k0, bk=bk: e.transpose(out=self.banks[bk][:, j * 128:(j + 1) * 128],
                                                                  in_=src[:, (k0 + j) * 128:(k0 + j + 1) * 128], identity=self.ident),
                     [bsrc, self.bconst], [self.bbank[bk]])
            self.copy(self.evac_eng(), dstT[:, k0:k0 + n, col0:col0 + 128],
                      self.banks[bk][:, 0:n * 128].rearrange("p (a b) -> p a b", a=n), [self.bbank[bk]], [bdst])

    def next_bank(self):
        self.bkrr = (getattr(self, "bkrr", -1) + 1) % self.nrot
        return self.bkrr

    def g_bank(self):
        self.gkrr = 13 - getattr(self, "gkrr", 7)
        return self.gkrr

    def phase_in(self, l):
        S, ar, W = self.S, self.ar, self.W
        S.barrier()
        ar.off = self.arena_base
        even = (l % 2 == 0)
        NP = EV_NP if even else OD_NP
        Win = (W["ev_in_w"] if even else W["od_in_w"])[l // 2]
        Wv = Win.rearrange("(kc p) n -> p kc n", p=128)
        xt = [ar.alloc([D]) for _ in range(2)]
        bxt = [self.B("xt", True) for _ in range(2)]
        ht = [ar.alloc([D]) for _ in range(2)]
        bht = [self.B("ht") for _ in range(2)]
        junk = ar.alloc([D])
        bj = self.B("junk")
        st2 = [ar.alloc([4]) for _ in range(2)]
        bst = [self.B("st") for _ in range(2)]
        hT = ar.alloc([8, 1024], F32R)
        bhT = self.B("hT")
        wp = [ar.alloc([8, 512], F32R) for _ in range(2)]
        bwp = [self.B("wp", True) for _ in range(2)]
        og = [ar.alloc([512]) for _ in range(4)]
        bog = [self.B("og") for _ in range(4)]
        Xsrc = self.xin if l == 0 else self.X
        bXs = None if l == 0 else self.bX
        npan = cdiv(NP, 512)
        cnt = 0
        tl = self.tiles()
        groups = []
        while tl:
            if len(tl) > 1 and tl[0][1] == tl[1][1]:
                groups.append(tl[:2])
                tl = tl[2:]
            else:
                groups.append(tl[:1])
                tl = tl[1:]
        for grp in groups:
            for ti, (t0, g) in enumerate(grp):
                mv = self.modv[g]
                for s4 in range(4):
                    i = cnt % 2
                    cnt += 1
                    S.dma("sp", xt[i], Xsrc[t0 + s4 * 128:t0 + (s4 + 1) * 128, :], bxt[i], [bXs] if bXs else [])
                    self.norm_tile(xt[i], bxt[i], ht[i], bht[i], mv[:, 1, :], mv[:, 0, :], [self.bmod[g]], junk, bj, st2[i], bst[i])
                    self.to_kmajor(ht[i], bht[i], 8, hT, bhT, ti * 512 + s4 * 128)
            for pj in range(npan):
                c0 = pj * 512
                cw = min(512, NP - c0)
                s = pj % 2
                S.dma("pool", wp[s][:, :, 0:cw], Wv[:, :, c0:c0 + cw], bwp[s])
                for ti, (t0, g) in enumerate(grp):
                    for b0 in range(0, cw, 128):
                        bw = min(128, cw - b0)
                        bk = self.next_bank()
                        for kc in range(8):
                            S.op("pe", lambda e, kc=kc, bk=bk, s=s, b0=b0, bw=bw, ti=ti: e.matmul(
                                self.banks[bk][0:bw, :], lhsT=wp[s][:, kc, b0:b0 + bw], rhs=hT[:, kc, ti * 512:(ti + 1) * 512],
                                start=(kc == 0), stop=(kc == 7)), [bwp[s], bhT], [self.bbank[bk]])
                        o = cnt % 4
                        cnt += 1
                        self.copy(self.evac_eng(), og[o][0:bw, :], self.banks[bk][0:bw, :], [self.bbank[bk]], [bog[o]])
                        S.dma("sp", self.PF[c0 + b0:c0 + b0 + bw, t0:t0 + 512], og[o][0:bw, :], self.bPF, [bog[o]])

    def phase_out(self, l):
        S, ar, W = self.S, self.ar, self.W
        S.barrier()
        ar.off = self.arena_base
        even = (l % 2 == 0)
        KI = 1024 if even else 2048
        nk = KI // 128
        Wout = (W["ev_out_w"] if even else W["od_out_w"])[l // 2]
        wo = ar.alloc([nk, D], F32R)
        bwo = self.B("wo", True)
        Wv = Wout.rearrange("(kc p) n -> p kc n", p=128)
        for kc in range(nk):
            for h in range(2):
                S.dma("pool", wo[:, kc, h * 512:(h + 1) * 512], Wv[:, kc, h * 512:(h + 1) * 512], bwo)
        yt = [ar.alloc([KI]) for _ in range(2)]
        byt = [self.B("yt", True) for _ in range(2)]
        yT = [ar.alloc([nk, 128], F32R) for _ in range(2)]
        byT = [self.B("yT") for _ in range(2)]
        xt = [ar.alloc([D]) for _ in range(2)]
        bxt = [self.B("xt", True) for _ in range(2)]
        Xsrc = self.xin if l == 0 else self.X
        bXs = None if l == 0 else self.bX
        cnt = 0
        for (t0, g) in self.tiles():
            G1 = self.modv[g][:, 2, :]
            for s4 in range(4):
                i = cnt % 2
                cnt += 1
                r0 = t0 + s4 * 128
                S.dma("sp", yt[i], self.YT[r0:r0 + 128, 0:KI], byt[i], [self.bYT])
                S.dma("sp", xt[i], Xsrc[r0:r0 + 128, :], bxt[i], [bXs] if bXs else [])
                self.to_kmajor(yt[i], byt[i], nk, yT[i], byT[i], 0)
                for h in range(2):
                    bk = self.next_bank()
                    for kc in range(nk):
                        S.op("pe", lambda e, kc=kc, bk=bk, i=i, h=h: e.matmul(
                            self.banks[bk][:], lhsT=yT[i][:, kc, :], rhs=wo[:, kc, h * 512:(h + 1) * 512],
                            start=(kc == 0), stop=(kc == nk - 1)), [byT[i], bwo], [self.bbank[bk]])
                    S.op("dve", lambda e, bk=bk, i=i, h=h, G1=G1: e.tensor_tensor(
                        out=yt[i][:, h * 512:(h + 1) * 512], in0=self.banks[bk][:], in1=G1[:, h * 512:(h + 1) * 512], op=ALU.mult),
                        [self.bbank[bk], self.bmod[g]], [byt[i]])
                    S.op("pool", lambda e, i=i, h=h: e.tensor_tensor(
                        out=xt[i][:, h * 512:(h + 1) * 512], in0=xt[i][:, h * 512:(h + 1) * 512],
                        in1=yt[i][:, h * 512:(h + 1) * 512], op=ALU.add), [byt[i], bxt[i]], [bxt[i]])
                S.dma("sp", self.X[r0:r0 + 128, :], xt[i], self.bX, [bxt[i]])

    def store_rows(self, dst, bdst, r0, src, bsrc, g, perm):
        S = self.S
        if perm and g == 0:
            v = dst[0:4096, :].rearrange("(c r) d -> r c d", r=64)
            ra = r0 // 64
            for j in range(2):
                S.dma("sp", v[ra + j], src[j * 64:(j + 1) * 64, :], bdst, [bsrc])
        else:
            S.dma("sp", dst[r0:r0 + 128, :], src, bdst, [bsrc])

    def phase_ffn(self, l, final=False, perm=False):
        S, ar, W = self.S, self.ar, self.W
        S.barrier()
        ar.off = self.arena_base
        W1 = W["ffn_w1"][l].rearrange("(kc p) n -> p kc n", p=128)
        W3 = W["ffn_w3"][l].rearrange("(kc p) n -> p kc n", p=128)
        W2 = W["ffn_w2"][l].rearrange("(kc p) n -> p kc n", p=128)
        NJ = FH // 128
        xt = ar.alloc([4, D])
        bxt = [self.B("xt", True) for _ in range(4)]
        ht = [ar.alloc([D]) for _ in range(2)]
        bht = [self.B("ht") for _ in range(2)]
        junk = ht[1]
        bj = bht[1]
        st2 = [ar.alloc([4]) for _ in range(2)]
        bst = [self.B("st") for _ in range(2)]
        hT = ar.alloc([8, 512], F32R)
        bhT = self.B("hT")
        w13 = [[ar.alloc([8, 256], F32R) for _ in range(2)] for _ in range(2)]
        bw13 = [[self.B("w13", True) for _ in range(2)] for _ in range(2)]
        uT = ar.alloc([NJ, 512], F32R)
        buT = self.B("uT")
        sg = [ar.alloc([512]) for _ in range(2)]
        bsg = [self.B("sg") for _ in range(2)]
        w2 = ar.alloc([NJ, 256], F32R)
        bw2 = self.B("w2", True)
        if final:
            fg = ar.alloc([D])
            bfg = self.B("fg", True)
            S.dma("sp", fg, W["final_g"].partition_broadcast(128), bfg)
        cnt = 0
        Xdst, bXdst = (self.Xalt, self.bXalt) if perm else (self.X, self.bX)
        for (t0, g) in self.tiles():
            mv = self.modv[g]
            for s4 in range(4):
                r0 = t0 + s4 * 128
                S.dma("sp", xt[:, s4, :], self.X[r0:r0 + 128, :], bxt[s4], [self.bX])
                self.norm_tile(xt[:, s4, :], bxt[s4], ht[0], bht[0], mv[:, 4, :], mv[:, 3, :], [self.bmod[g]], junk, bj, st2[0], bst[0])
                self.to_kmajor(ht[0], bht[0], 8, hT, bhT, s4 * 128)
            for jp in range(NJ // 2):
                s = jp % 2
                S.dma("pool", w13[0][s], W1[:, :, jp * 256:(jp + 1) * 256], bw13[0][s])
                S.dma("pool", w13[1][s], W3[:, :, jp * 256:(jp + 1) * 256], bw13[1][s])
                for jj in range(2):
                    j = jp * 2 + jj
                    bka, bkb = self.next_bank(), self.next_bank()
                    for m, bk in ((0, bka), (1, bkb)):
                        for kc in range(8):
                            S.op("pe", lambda e, kc=kc, bk=bk, m=m, s=s, jj=jj: e.matmul(
                                self.banks[bk][:], lhsT=w13[m][s][:, kc, jj * 128:(jj + 1) * 128], rhs=hT[:, kc, :],
                                start=(kc == 0), stop=(kc == 7)), [bw13[m][s], bhT], [self.bbank[bk]])
                    q = cnt % 2
                    cnt += 1
                    S.op("act", lambda e, q=q, bka=bka: e.activation(out=sg[q], in_=self.banks[bka][:], func=AF.Silu),
                         [self.bbank[bka]], [bsg[q]])
                    S.op("dve", lambda e, q=q, bkb=bkb, j=j: e.tensor_tensor(out=uT[:, j, :], in0=sg[q], in1=self.banks[bkb][:],
                                                                           op=ALU.mult), [bsg[q], self.bbank[bkb]], [buT])
            for nq in range(4):
                S.dma("pool", w2, W2[:, :, nq * 256:(nq + 1) * 256], bw2)
                for s4 in range(4):
                    bk = self.next_bank()
                    for j in range(NJ):
                        S.op("pe", lambda e, j=j, bk=bk, s4=s4: e.matmul(
                            self.banks[bk][:, 0:256], lhsT=uT[:, j, s4 * 128:(s4 + 1) * 128], rhs=w2[:, j, :],
                            start=(j == 0), stop=(j == NJ - 1)), [buT, bw2], [self.bbank[bk]])
                    q = cnt % 2
                    cnt += 1
                    S.op("dve", lambda e, q=q, bk=bk, nq=nq, mv=mv: e.tensor_tensor(
                        out=sg[q][:, 0:256], in0=self.banks[bk][:, 0:256], in1=mv[:, 5, nq * 256:(nq + 1) * 256], op=ALU.mult),
                        [self.bbank[bk], self.bmod[g]], [bsg[q]])
                    S.op("pool", lambda e, q=q, s4=s4, nq=nq: e.tensor_tensor(
                        out=xt[:, s4, nq * 256:(nq + 1) * 256], in0=xt[:, s4, nq * 256:(nq + 1) * 256], in1=sg[q][:, 0:256],
                        op=ALU.add), [bsg[q], bxt[s4]], [bxt[s4]])
            for s4 in range(4):
                r0 = t0 + s4 * 128
                if not final:
                    self.store_rows(Xdst, bXdst, r0, xt[:, s4, :], bxt[s4], g, perm)
                else:
                    self.norm_tile(xt[:, s4, :], bxt[s4], ht[0], bht[0], fg, None, [bfg], junk, bj, st2[0], bst[0])
                    self.store_rows(self.yout, self.bYO, r0, ht[0], bht[0], g, perm)
        if perm and not final:
            self.X, self.Xalt, self.bX, self.bXalt = self.Xalt, self.X, self.bXalt, self.bX

    def finish(self):
        self.S.barrier()
        self.S.finish("sp", [self.bYO])
        self.S.emit()
        self.st.close()


WSHAPES = dict(
    mod_w=[4, 1024, 6144], mod_b=[4, 6144], norm1_g=[4, 1024], norm2_g=[4, 1024],
    ffn_w1=[4, 1024, 2816], ffn_w3=[4, 1024, 2816], ffn_w2=[4, 2816, 1024], final_g=[1024],
    ev_in_w=[2, 1024, 3600], ev_conv_w=[2, 3, 2560], ev_conv_b=[2, 2560],
    hy_w1=[2, 33, 64], hy_b1=[2, 64], hy_w2=[2, 64, 64], hy_b2=[2, 64], hy_w3=[2, 64, 2048], hy_freq=[2, 64],
    hy_bias=[2, 2, 512], m_gate_b=[2, 16], m_norm_g=[2, 512], ev_out_w=[2, 1024, 1024],
    ev_cw=[2, 128, 80], od_cw=[2, 128, 128],
    od_in_w=[2, 1024, 6208], od_conv_w=[2, 3, 4096], od_conv_b=[2, 4096], ssd_dt_bias=[2, 2, 32],
    ssd_A_log=[2, 2, 32], ssd_D=[2, 32], ssd_norm_g=[2, 2048], od_out_w=[2, 2048, 1024],
)


def phase_prep(self, l):
    S, ar, W = self.S, self.ar, self.W
    S.barrier()
    ar.reset()
    even = (l % 2 == 0)
    if even:
        cw, cb_ = W["ev_conv_w"][l // 2], W["ev_conv_b"][l // 2]
        plan = [(b, 128, b, b >= 12, (128 ** -0.5 if b >= 16 else 1.0), (b < 12 or b >= 16)) for b in range(20)]
        plan += [(b, 128, None, False, 1.0, True) for b in range(20, 28)]
    else:
        cw, cb_ = W["od_conv_w"][l // 2], W["od_conv_b"][l // 2]
        plan = [(b, 128, None, False, 1.0, True) for b in range(16)]
        plan += [(b, 128, b - 16, True, 1.0, b < 40) for b in range(16, 48)]
    Lmax = max(s[1] for s in self.seqs)
    xp = [ar.alloc([Lmax + 2]) for _ in range(2)]
    bxp = [self.B("xp", True) for _ in range(2)]
    acc = [ar.alloc([Lmax]) for _ in range(2)]
    bacc = [self.B("acc") for _ in range(2)]
    tm = [ar.alloc([Lmax // 128, 128], F32R) for _ in range(2)]
    btm = [self.B("tm") for _ in range(2)]
    ncb = 20 if even else 32
    cwt = ar.alloc([ncb, 4])
    bcw = self.B("cwt", True)
    S.dma("sp", cwt, (W["ev_cw"] if even else W["od_cw"])[l // 2].rearrange("p (b f) -> p b f", f=4), bcw)
    cnt = 0
    for (blk, rows, cbi, silu, scale, want_tm) in plan:
        for (s0, L, g) in self.seqs:
            i = cnt % 2
            cnt += 1
            x = xp[i]
            S.op("pool", lambda e, x=x, L=L: e.memset(x[:, 0:1], 0.0), [], [bxp[i]])
            S.op("pool", lambda e, x=x, L=L: e.memset(x[:, L + 1:L + 2], 0.0), [], [bxp[i]])
            S.dma("sp", x[:, 1:L + 1], self.PF[blk * 128:(blk + 1) * 128, s0:s0 + L], bxp[i], [self.bPF])
            src, bsrc = x[:, 1:L + 1], bxp[i]
            if cbi is not None:
                a = acc[i][:, 0:L]
                S.op("act", lambda e, x=x, a=a, L=L, cbi=cbi: e.activation(out=a, in_=x[:, 1:L + 1], func=AF.Identity,
                                                                       scale=cwt[:, cbi, 1:2], bias=cwt[:, cbi, 3:4]),
                     [bxp[i], bcw], [bacc[i]])
                S.op("dve", lambda e, x=x, a=a, L=L, cbi=cbi: e.scalar_tensor_tensor(out=a, in0=x[:, 0:L], scalar=cwt[:, cbi, 0:1],
                                                                                  in1=a, op0=ALU.mult, op1=ALU.add),
                     [bxp[i], bcw, bacc[i]], [bacc[i]])
                S.op("dve", lambda e, x=x, a=a, L=L, cbi=cbi: e.scalar_tensor_tensor(out=a, in0=x[:, 2:L + 2], scalar=cwt[:, cbi, 2:3],
                                                                                  in1=a, op0=ALU.mult, op1=ALU.add),
                     [bxp[i], bcw, bacc[i]], [bacc[i]])
                if silu:
                    S.op("act", lambda e, a=a: e.activation(out=a, in_=a, func=AF.Silu), [bacc[i]], [bacc[i]])
                if scale != 1.0:
                    S.op("pool", lambda e, a=a, scale=scale: e.tensor_scalar(out=a, in0=a, scalar1=scale, scalar2=None, op0=ALU.mult),
                         [bacc[i]], [bacc[i]])
                S.dma("sp", self.PF[blk * 128:(blk + 1) * 128, s0:s0 + L], a, self.bPF, [bacc[i]])
                src, bsrc = a, bacc[i]
            if want_tm:
                nt = L // 128
                for c0 in range(0, nt, 4):
                    bk = self.next_bank()
                    n4 = min(4, nt - c0)
                    for j in range(n4):
                        c = c0 + j
                        S.op("pe", lambda e, j=j, c=c, bk=bk, src=src: e.transpose(out=self.banks[bk][:, j * 128:(j + 1) * 128],
                                                                                 in_=src[:, c * 128:(c + 1) * 128], identity=self.ident),
                             [bsrc, self.bconst], [self.bbank[bk]])
                    self.copy(self.evac_eng(), tm[i][:, c0:c0 + n4, :],
                              self.banks[bk][:, 0:n4 * 128].rearrange("p (a b) -> p a b", a=n4), [self.bbank[bk]], [btm[i]])
                S.dma("sp", self.PT[s0:s0 + L, blk * 128:(blk + 1) * 128].rearrange("(a p) c -> p a c", p=128),
                      tm[i][:, 0:nt, :].bitcast(F32), self.bPT, [btm[i]])


MK.phase_prep = phase_prep


NEG = -30000.0


def scan_rows_prep(self, rho, kap, nrow, L, cols, bcols, r1, r2, brow, sdecT, extra=None, rowsets=None, rp_init=None, brp=None):
    S, ar = self.S, self.ar
    nc_ = L // 128
    h2 = nrow // 2
    RP = ar.alloc([nc_])
    RE = ar.alloc([nc_])
    bR = self.B("RPRE")
    v3 = lambda t: t[0:nrow, 0:L].rearrange("p (c t) -> p c t", t=128)
    S.op("pool", lambda e: e.memset(RP[0:nrow, :], 0.0), [], [bR])
    if rowsets is None:
        rowsets = [(0, nrow, 0)]
    if rp_init is not None:
        S.op("dve", lambda e: e.tensor_copy(out=RP[0:h2, 0:1], in_=rp_init[0:h2, 0:1]), [bR, brp], [bR])
        S.op("dve", lambda e: e.tensor_copy(out=RP[h2:nrow, nc_ - 1:nc_], in_=rp_init[h2:nrow, 0:1]), [bR, brp], [bR])
    if nc_ > 1:
        S.op("dve", lambda e: e.tensor_copy(out=RP[0:h2, 1:nc_], in_=v3(rho)[0:h2, 0:nc_ - 1, 127]), [brow, bR], [bR])
        S.op("dve", lambda e: e.tensor_copy(out=RP[h2:nrow, 0:nc_ - 1], in_=v3(rho)[h2:nrow, 1:nc_, 0]), [brow, bR], [bR])
    S.op("dve", lambda e: e.tensor_copy(out=RE[0:h2, :], in_=v3(rho)[0:h2, :, 127]), [brow, bR], [bR])
    S.op("dve", lambda e: e.tensor_copy(out=RE[h2:nrow, :], in_=v3(rho)[h2:nrow, :, 0]), [brow, bR], [bR])

    def to_cols(src, q):
        for (lo, hi, cb) in rowsets:
            nr = hi - lo
            for c0 in range(0, nc_, 8):
                n8 = min(8, nc_ - c0)
                bk = self.next_bank()
                for j in range(n8):
                    c = c0 + j
                    S.op("pe", lambda e, j=j, c=c, bk=bk, lo=lo, hi=hi, nr=nr: e.transpose(
                        out=self.banks[bk][:, j * 64:j * 64 + nr], in_=src[lo:hi, c * 128:(c + 1) * 128], identity=self.ident[lo:hi, lo:hi]),
                         [brow, self.bconst], [self.bbank[bk]])
                self.copy(self.evac_eng(), cols[:, c0:c0 + n8, q, cb:cb + nr],
                          self.banks[bk][:, 0:n8 * 64].rearrange("p (a b) -> p a b", a=n8)[:, :, 0:nr], [self.bbank[bk]], [bcols])
    to_cols(kap, 0)
    S.op("dve", lambda e: e.tensor_tensor(out=v3(r1), in0=v3(rho), in1=RP[0:nrow, :].unsqueeze(2).to_broadcast([nrow, nc_, 128]),
                                          op=ALU.subtract), [brow, bR], [brow])
    S.op("act", lambda e: e.activation(out=r1[0:nrow, 0:L], in_=r1[0:nrow, 0:L], func=AF.Exp), [brow], [brow])
    to_cols(r1, 1)
    S.op("dve", lambda e: e.tensor_tensor(out=v3(r1), in0=v3(kap), in1=RE[0:nrow, :].unsqueeze(2).to_broadcast([nrow, nc_, 128]),
                                          op=ALU.add), [brow, bR], [brow])
    S.op("act", lambda e: e.activation(out=r1[0:nrow, 0:L], in_=r1[0:nrow, 0:L], func=AF.Exp), [brow], [brow])
    to_cols(r1, 2)
    if extra is not None:
        to_cols(extra, 3)
    S.op("dve", lambda e: e.tensor_tensor(out=sdecT[0:nrow, 0:nc_], in0=RE[0:nrow, :], in1=RP[0:nrow, :], op=ALU.subtract), [bR], [bR])
    S.op("act", lambda e: e.activation(out=sdecT[0:nrow, 0:nc_], in_=sdecT[0:nrow, 0:nc_], func=AF.Exp), [bR], [bR])
    return bR


def bcast_rows(self, srcT, bsrc, nrow, ncol, dst, bdst, rows=None):
    S = self.S
    per = max(1, 512 // ncol)
    if rows is None:
        rows = list(range(nrow))
    for r0 in range(0, len(rows), per):
        n = min(per, len(rows) - r0)
        bk = self.next_bank()
        for j in range(n):
            r = rows[r0 + j]
            S.op("pe", lambda e, j=j, r=r, bk=bk: e.matmul(self.banks[bk][:, j * ncol:(j + 1) * ncol],
                                                         lhsT=self.sel[0:nrow, r, :], rhs=srcT[0:nrow, 0:ncol], start=True, stop=True),
                 [bsrc, self.bconst], [self.bbank[bk]])
        self.copy(self.evac_eng(), dst[:, r0:r0 + n, 0:ncol], self.banks[bk][:, 0:n * ncol].rearrange("p (a b) -> p a b", a=n),
                  [self.bbank[bk]], [bdst])


def scan_unit(self, d, c, GT_bank, rho, brow, row, cols, bcols, colidx, Xr, bX, CTc, bCT, Btm_c, bB, S32, Sr, bS, SDcol, bSD,
              dv, yout_ap, byout, accumulate, w):
    S = self.S
    bkD = self.next_bank()
    S.op("pe", lambda e: e.matmul(self.banks[bkD][:, 0:128], lhsT=self.sel[0:self.selrows, row, :], rhs=rho[0:self.selrows, c * 128:(c + 1) * 128],
                                  start=True, stop=False), [brow, self.bconst], [self.bbank[bkD]])
    S.op("pe", lambda e: e.matmul(self.banks[bkD][:, 0:128], lhsT=self.ident, rhs=self.negmask[:, d, :], start=False, stop=True),
         [self.bconst], [self.bbank[bkD]])
    i = w["i"] = (w.get("i", 0) + 1) % 2
    E, bE = w["E"][i], w["bE"][i]
    ST, bST = w["ST"][i], w["bST"][i]
    Aw, bAw = w["Aw"][i], w["bAw"][i]
    tmp, btmp = w["tmp"][i], w["btmp"][i]
    S.op("act", lambda e: e.activation(out=E, in_=self.banks[bkD][:, 0:128], func=AF.Exp, bias=cols[:, c, 0, colidx:colidx + 1]),
         [self.bbank[bkD], bcols], [bE])
    S.op("dve", lambda e: e.tensor_tensor(out=ST, in0=GT_bank[0], in1=E, op=ALU.mult), [GT_bank[1], bE], [bST])
    bkO = self.next_bank()
    S.op("pe", lambda e: e.matmul(self.banks[bkO][:, 0:dv], lhsT=ST, rhs=Xr, start=True, stop=True), [bST, bX], [self.bbank[bkO]])
    S.op("pe", lambda e: e.matmul(self.banks[bkO][:, 256:256 + dv], lhsT=CTc, rhs=Sr, start=True, stop=True), [bCT, bS], [self.bbank[bkO]])
    S.op("act", lambda e: e.activation(out=tmp[:, 0:dv], in_=self.banks[bkO][:, 0:dv], func=AF.Copy), [self.bbank[bkO]], [btmp])
    if not accumulate:
        S.op("dve", lambda e: e.scalar_tensor_tensor(out=yout_ap, in0=self.banks[bkO][:, 256:256 + dv], scalar=cols[:, c, 1, colidx:colidx + 1],
                                                     in1=tmp[:, 0:dv], op0=ALU.mult, op1=ALU.add), [self.bbank[bkO], bcols, btmp], [byout])
    else:
        S.op("dve", lambda e: e.scalar_tensor_tensor(out=tmp[:, 0:dv], in0=self.banks[bkO][:, 256:256 + dv], scalar=cols[:, c, 1, colidx:colidx + 1],
                                                     in1=tmp[:, 0:dv], op0=ALU.mult, op1=ALU.add), [self.bbank[bkO], bcols, btmp], [btmp])
        S.op("pool", lambda e: e.tensor_tensor(out=yout_ap, in0=yout_ap, in1=tmp[:, 0:dv], op=ALU.add), [btmp, byout], [byout])
    S.op("pool", lambda e: e.tensor_scalar(out=Aw, in0=Btm_c, scalar1=cols[:, c, 2, colidx:colidx + 1], scalar2=None, op0=ALU.mult),
         [bB, bcols], [bAw])
    bkS = self.next_bank()
    S.op("pe", lambda e: e.matmul(self.banks[bkS][:, 0:dv], lhsT=Aw, rhs=Xr, start=True, stop=True), [bAw, bX], [self.bbank[bkS]])
    S.op("dve", lambda e: e.scalar_tensor_tensor(out=S32, in0=S32, scalar=SDcol, in1=self.banks[bkS][:, 0:dv], op0=ALU.mult, op1=ALU.add),
         [bS, bSD, self.bbank[bkS]], [bS])
    S.op("act", lambda e: e.activation(out=Sr, in_=S32, func=AF.Copy), [bS], [bS])


def unit_work(self):
    ar = self.ar
    return dict(E=[ar.alloc([128]) for _ in range(2)], bE=[self.B("E") for _ in range(2)],
                ST=[ar.alloc([128], F32R) for _ in range(2)], bST=[self.B("ST") for _ in range(2)],
                Aw=[ar.alloc([128], F32R) for _ in range(2)], bAw=[self.B("Aw") for _ in range(2)],
                tmp=[ar.alloc([192]) for _ in range(2)], btmp=[self.B("tmp") for _ in range(2)])


def phase_ssd(self, l):
    S, ar, W = self.S, self.ar, self.W
    o = l // 2
    S.barrier()
    ar.reset()
    self.selrows = 64
    self.nrot = 6
    Lmax = max(s[1] for s in self.seqs)
    ncm = Lmax // 128
    cst = ar.alloc([4])
    bcst = self.B("cst", True)
    S.dma("sp", cst[0:64, 0:1], W["ssd_dt_bias"][o].rearrange("d (h one) -> (d h) one", one=1), bcst)
    S.dma("sp", cst[0:64, 1:2], W["ssd_A_log"][o].rearrange("d (h one) -> (d h) one", one=1), bcst)
    S.op("act", lambda e: e.activation(out=cst[0:64, 2:3], in_=cst[0:64, 1:2], func=AF.Exp), [bcst], [bcst])
    S.op("dve", lambda e: e.tensor_scalar(out=cst[0:64, 2:3], in0=cst[0:64, 2:3], scalar1=-1.0, scalar2=None, op0=ALU.mult), [bcst], [bcst])
    Dt = ar.alloc([32])
    S.dma("sp", Dt, W["ssd_D"][o].partition_broadcast(128), bcst)
    rho = ar.alloc([Lmax])
    brow = self.B("rows", True)
    cols = ar.alloc([ncm, 3, 64])
    bcols = self.B("cols")
    sdecT = ar.alloc([ncm])
    SD = ar.alloc([8, ncm])
    bSD = self.B("SD")
    w = batch_work(self, 8, 64)
    S32 = [ar.alloc([256]) for _ in range(2)]
    Sr = [ar.alloc([256], F32R) for _ in range(2)]
    bS = [[self.B("S") for _ in range(4)] for _ in range(2)]
    stg = [ar.alloc([128]) for _ in range(2)]
    bstg = [self.B("stg", True) for _ in range(2)]
    BT = ar.alloc([Lmax], F32R)
    CT = ar.alloc([Lmax], F32R)
    Btm = ar.alloc([ncm, 128], F32R)
    Xtm = ar.alloc([ncm, 256], F32R)
    bBT, bCT, bBtm, bXtm = self.B("BT", True), self.B("CT", True), self.B("Btm", True), self.B("Xtm", True)
    mark = ar.off
    for si, (s0, L, g) in enumerate(self.seqs):
        nc_ = L // 128
        S.barrier()
        ar.off = mark
        r1 = ar.alloc([Lmax])
        r2 = ar.alloc([Lmax])
        S.dma("sp", r1[0:64, 0:L], self.PF[48 * 128:48 * 128 + 64, s0:s0 + L], brow, [self.bPF])
        S.op("act", lambda e, L=L: e.activation(out=r1[0:64, 0:L], in_=r1[0:64, 0:L], func=AF.Exp, bias=cst[0:64, 0:1]), [brow, bcst], [brow])
        S.op("act", lambda e, L=L: e.activation(out=r1[0:64, 0:L], in_=r1[0:64, 0:L], func=AF.Ln, bias=1.0), [brow], [brow])
        S.op("dve", lambda e, L=L: e.tensor_scalar(out=r2[0:64, 0:L], in0=r1[0:64, 0:L], scalar1=cst[0:64, 2:3], scalar2=None, op0=ALU.mult),
             [brow, bcst], [brow])
        S.op("dve", lambda e, L=L: e.tensor_tensor_scan(out=rho[0:64, 0:L], data0=self.ones_row[0:64, 0:L], data1=r2[0:64, 0:L], initial=0.0,
                                                        op0=ALU.mult, op1=ALU.add), [brow, self.bconst], [brow])
        S.op("dve", lambda e, L=L: e.tensor_copy(out=cst[32:64, 3:4], in_=rho[32:64, L - 1:L]), [brow, bcst], [bcst])
        S.op("dve", lambda e, L=L: e.tensor_scalar(out=rho[32:64, 0:L], in0=rho[32:64, 0:L], scalar1=-1.0, scalar2=cst[32:64, 3:4],
                                                   op0=ALU.mult, op1=ALU.add), [brow, bcst], [brow])
        S.op("dve", lambda e, L=L: e.tensor_tensor(out=rho[32:64, 0:L], in0=rho[32:64, 0:L], in1=r2[32:64, 0:L], op=ALU.add), [brow], [brow])
        S.op("act", lambda e, L=L: e.activation(out=r2[0:64, 0:L], in_=r1[0:64, 0:L], func=AF.Ln), [brow], [brow])
        S.op("dve", lambda e, L=L: e.tensor_tensor(out=r2[0:64, 0:L], in0=r2[0:64, 0:L], in1=rho[0:64, 0:L], op=ALU.subtract), [brow], [brow])
        bR = scan_rows_prep(self, rho, r2, 64, L, cols, bcols, r1, r2, brow, sdecT)
        S.barrier()
        ar.off = mark
        yacc = ar.alloc([ncm, 256])
        byacc = self.B("yacc")
        for gq in range(8):
            bcast_rows(self, sdecT, bR, 64, nc_, SD, bSD, rows=[d * 32 + gq * 4 + r for d in range(2) for r in range(4)])
            S.dma("pool", BT[:, 0:L], self.PF[(32 + gq) * 128:(33 + gq) * 128, s0:s0 + L], bBT, [self.bPF])
            S.dma("pool", CT[:, 0:L], self.PF[(40 + gq) * 128:(41 + gq) * 128, s0:s0 + L], bCT, [self.bPF])
            S.dma("pool", Btm[:, 0:nc_, :], self.PT[s0:s0 + L, (32 + gq) * 128:(33 + gq) * 128].rearrange("(c p) k -> p c k", p=128), bBtm, [self.bPT])
            S.dma("pool", Xtm[:, 0:nc_, :], self.PT[s0:s0 + L, (16 + 2 * gq) * 128:(18 + 2 * gq) * 128].rearrange("(c p) k -> p c k", p=128), bXtm, [self.bPT])
            for d in range(2):
                if g == 0:
                    for pr in range(2):
                        row0 = ((o * 2 + d) * 32 + gq * 4 + pr * 2) * 64
                        S.dma("sp", stg[pr], self.st_ssd[row0:row0 + 128, :], bstg[pr])
                        bk = self.next_bank()
                        S.op("pe", lambda e, pr=pr, bk=bk: e.transpose(out=self.banks[bk][:, 0:128], in_=stg[pr], identity=self.ident),
                             [bstg[pr], self.bconst], [self.bbank[bk]])
                        for hh in range(2):
                            r = pr * 2 + hh
                            S.op("dve", lambda e, bk=bk, hh=hh, r=r, d=d: e.tensor_copy(out=S32[d][:, r * 64:(r + 1) * 64], in_=self.banks[bk][:, hh * 64:(hh + 1) * 64]),
                                 [self.bbank[bk]], [bS[d][r]])
                            S.op("act", lambda e, bk=bk, hh=hh, r=r, d=d: e.activation(out=Sr[d][:, r * 64:(r + 1) * 64], in_=self.banks[bk][:, hh * 64:(hh + 1) * 64], func=AF.Copy),
                                 [self.bbank[bk]], [bS[d][r]])
                else:
                    for r in range(4):
                        S.op("pool", lambda e, r=r, d=d: e.memset(S32[d][:, r * 64:(r + 1) * 64], 0.0), [], [bS[d][r]])
                        S.op("dve", lambda e, r=r, d=d: e.tensor_copy(out=Sr[d][:, r * 64:(r + 1) * 64], in_=S32[d][:, r * 64:(r + 1) * 64]), [bS[d][r]], [bS[d][r]])
            visited = set()
            for k in range(nc_):
                U = []
                for d, c in ((0, k), (1, nc_ - 1 - k)):
                    S.op("pe", lambda e, c=c, d=d: e.matmul(self.banks[6][:, d * 128:(d + 1) * 128], lhsT=BT[:, c * 128:(c + 1) * 128], rhs=CT[:, c * 128:(c + 1) * 128],
                                                          start=True, stop=True), [bBT, bCT], [self.bbank[6]])
                    acc = c in visited
                    visited.add(c)
                    for r in range(4):
                        dh = d * 32 + gq * 4 + r
                        U.append(dict(j=d * 4 + r, d=d, c=c, row=dh, ci=dh, rho=rho, brow=brow, cols=cols, bcols=bcols,
                                      D=(d, r * 128), G=(6, d * 128), O=(2, (d * 4 + r) * 64, 3, (d * 4 + r) * 64), Sb=(4, (d * 4 + r) * 64), dv=64,
                                      Xr=Xtm[:, c, r * 64:(r + 1) * 64], bX=bXtm, CTc=CT[:, c * 128:(c + 1) * 128], bCT=bCT, Btm=Btm[:, c, :], bB=bBtm,
                                      S32=S32[d][:, r * 64:(r + 1) * 64], Sr=Sr[d][:, r * 64:(r + 1) * 64], bS=bS[d][r], SDcol=SD[:, d * 4 + r, c:c + 1], bSD=bSD,
                                      yout=yacc[:, c, r * 64:(r + 1) * 64], byout=byacc, acc=acc))
                scan_step(self, U, w)
            for d in range(2):
                if g == 1:
                    pi = si - 1
                    for pr in range(2):
                        bk = self.next_bank()
                        S.op("pe", lambda e, pr=pr, bk=bk, d=d: e.transpose(out=self.banks[bk][:, 0:128], in_=S32[d][:, pr * 128:(pr + 1) * 128], identity=self.ident),
                             [bS[d][pr * 2], bS[d][pr * 2 + 1], self.bconst], [self.bbank[bk]])
                        self.copy("dve", stg[pr], self.banks[bk][:, 0:128], [self.bbank[bk]], [bstg[pr]])
                        row0 = (((pi * 2 + o) * 2 + d) * 32 + gq * 4 + pr * 2) * 64
                        S.dma("sp", self.o_ssd[row0:row0 + 128, :], stg[pr], self.bso, [bstg[pr]])
            for r in range(4):
                h = gq * 4 + r
                S.op("dve", lambda e, r=r, h=h, nc_=nc_: e.scalar_tensor_tensor(out=yacc[:, 0:nc_, r * 64:(r + 1) * 64], in0=Xtm[:, 0:nc_, r * 64:(r + 1) * 64].bitcast(F32),
                                                                           scalar=Dt[:, h:h + 1], in1=yacc[:, 0:nc_, r * 64:(r + 1) * 64], op0=ALU.mult, op1=ALU.add),
                     [bXtm, bcst, byacc], [byacc])
            S.dma("sp", self.YS[s0:s0 + L, gq * 256:(gq + 1) * 256].rearrange("(c p) k -> p c k", p=128), yacc[:, 0:nc_, :], self.bYS, [byacc])
    S.barrier()
    ar.reset()
    self.nrot = 8
    ng = ar.alloc([2048])
    bng = self.B("ng", True)
    S.dma("sp", ng, W["ssd_norm_g"][o].partition_broadcast(128), bng)
    ys = [ar.alloc([2048]) for _ in range(2)]
    zs = [ar.alloc([2048]) for _ in range(2)]
    bys = [self.B("ys", True) for _ in range(2)]
    bzs = [self.B("zs", True) for _ in range(2)]
    junk = ar.alloc([2048])
    bj = self.B("junk")
    st2 = [ar.alloc([4]) for _ in range(2)]
    bst = [self.B("st") for _ in range(2)]
    for it in range(self.T // 128):
        i = it % 2
        r0 = it * 128
        S.dma("sp", ys[i], self.YS[r0:r0 + 128, :], bys[i], [self.bYS])
        S.dma("sp", zs[i], self.PT[r0:r0 + 128, 0:2048], bzs[i], [self.bPT])
        S.op("act", lambda e, i=i: e.activation(out=zs[i], in_=zs[i], func=AF.Silu), [bzs[i]], [bzs[i]])
        S.op("dve", lambda e, i=i: e.tensor_tensor(out=ys[i], in0=ys[i], in1=zs[i], op=ALU.mult), [bys[i], bzs[i]], [bys[i]])
        S.op("act", lambda e, i=i: e.activation(out=junk, in_=ys[i], func=AF.Square, accum_out=st2[i][:, 0:1]), [bys[i]], [bj, bst[i]])
        S.op("dve", lambda e, i=i: e.tensor_scalar(out=st2[i][:, 1:2], in0=st2[i][:, 0:1], scalar1=1.0 / 2048, scalar2=EPS, op0=ALU.mult, op1=ALU.add), [bst[i]], [bst[i]])
        S.op("act", lambda e, i=i: e.activation(out=st2[i][:, 3:4], in_=st2[i][:, 1:2], func=AF.Sqrt), [bst[i]], [bst[i]])
        S.op("dve", lambda e, i=i: e.reciprocal(out=st2[i][:, 2:3], in_=st2[i][:, 3:4]), [bst[i]], [bst[i]])
        S.op("dve", lambda e, i=i: e.scalar_tensor_tensor(out=ys[i], in0=ys[i], scalar=st2[i][:, 2:3], in1=ng, op0=ALU.mult, op1=ALU.mult),
             [bys[i], bst[i], bng], [bys[i]])
        S.dma("sp", self.YT[r0:r0 + 128, :], ys[i], self.bYT, [bys[i]])


MK.phase_ssd = phase_ssd


def phase_mlstm(self, l):
    S, ar, W = self.S, self.ar, self.W
    e_ = l // 2
    S.barrier()
    ar.reset()
    self.selrows = 64
    self.nrot = 6
    Lmax = max(s[1] for s in self.seqs)
    ncm = Lmax // 128
    G0 = 28 * 128
    cst = ar.alloc([8])
    bcst = self.B("cst", True)
    S.op("pool", lambda e: e.memset(cst, 0.0), [], [bcst])
    gbv = W["m_gate_b"][e_].rearrange("(q one) -> q one", one=1)
    for d in range(2):
        S.dma("sp", cst[d * 32:d * 32 + 4, 0:1], gbv[d * 8:d * 8 + 4, :], bcst)
        S.dma("sp", cst[d * 32:d * 32 + 4, 1:2], gbv[d * 8 + 4:d * 8 + 8, :], bcst)
    S.op("dve", lambda e: e.tensor_scalar(out=cst[0:64, 1:2], in0=cst[0:64, 1:2], scalar1=-1.0, scalar2=None, op0=ALU.mult), [bcst], [bcst])
    ngm = ar.alloc([512])
    S.dma("sp", ngm, W["m_norm_g"][e_].partition_broadcast(128), bcst)
    rho = ar.alloc([Lmax])
    brow = self.B("rows", True)
    cols = ar.alloc([ncm, 4, 8])
    bcols = self.B("cols")
    sdecT = ar.alloc([ncm])
    SD = ar.alloc([64, ncm])
    bSD = self.B("SD")
    w = unit_work(self)
    S32 = [ar.alloc([130]) for _ in range(2)]
    Sr = [ar.alloc([130], F32R) for _ in range(2)]
    bS = [self.B("S", True) for _ in range(2)]
    KT = ar.alloc([Lmax], F32R)
    QT = ar.alloc([Lmax], F32R)
    Ktm = ar.alloc([ncm, 128], F32R)
    Vtm = ar.alloc([ncm, 130], F32R)
    bKT, bQT, bKtm, bVtm = self.B("KT", True), self.B("QT", True), self.B("Ktm", True), self.B("Vtm", True)
    tot = [ar.alloc([136]) for _ in range(2)]
    btot = [self.B("tot") for _ in range(2)]
    mark = ar.off
    rowsets = [(0, 4, 0), (32, 36, 4)]
    cnt = 0
    for si, (s0, L, g) in enumerate(self.seqs):
        nc_ = L // 128
        S.barrier()
        ar.off = mark
        r1 = ar.alloc([Lmax])
        r2 = ar.alloc([Lmax])
        for t in (rho, r1, r2):
            S.op("pool", lambda e, t=t, L=L: e.memset(t[0:64, 0:L], 0.0), [], [brow])
        S.op("pool", lambda e: e.memset(cst[0:64, 2:3], 0.0), [], [bcst])
        if g == 0:
            for d in range(2):
                S.dma("sp", cst[d * 32:d * 32 + 4, 2:3], self.st_m[0:1, (e_ * 2 + d) * 4:(e_ * 2 + d) * 4 + 4].rearrange("one q -> q one"), bcst)
        S.op("dve", lambda e: e.tensor_scalar(out=cst[0:64, 3:4], in0=cst[0:64, 2:3], scalar1=-1.0, scalar2=None, op0=ALU.mult), [bcst], [bcst])
        for d in range(2):
            S.dma("sp", r1[d * 32:d * 32 + 4, 0:L], self.PF[G0 + d * 8:G0 + d * 8 + 4, s0:s0 + L], brow, [self.bPF])
            S.dma("sp", r2[d * 32:d * 32 + 4, 0:L], self.PF[G0 + d * 8 + 4:G0 + d * 8 + 8, s0:s0 + L], brow, [self.bPF])
        A = lambda t, L=L: t[0:64, 0:L]
        S.op("dve", lambda e, L=L, A=A: e.tensor_scalar(out=A(r1), in0=A(r1), scalar1=cst[0:64, 0:1], scalar2=None, op0=ALU.add), [brow, bcst], [brow])
        S.op("act", lambda e, L=L, A=A: e.activation(out=A(r2), in_=A(r2), func=AF.Exp, scale=-1.0, bias=cst[0:64, 1:2]), [brow, bcst], [brow])
        S.op("act", lambda e, L=L, A=A: e.activation(out=A(r2), in_=A(r2), func=AF.Ln, bias=1.0), [brow], [brow])
        S.op("dve", lambda e, L=L, A=A: e.tensor_tensor_scan(out=rho[0:32, 0:L], data0=self.ones_row[0:32, 0:L], data1=r2[0:32, 0:L], initial=0.0,
                                                        op0=ALU.mult, op1=ALU.add), [brow, self.bconst], [brow])
        S.op("dve", lambda e, L=L, A=A: e.tensor_tensor_scan(out=rho[32:64, L - 1::-1], data0=self.ones_row[32:64, 0:L], data1=r2[32:64, L - 1::-1], initial=0.0,
                                                        op0=ALU.mult, op1=ALU.add), [brow, self.bconst], [brow])
        S.op("dve", lambda e, L=L, A=A: e.tensor_scalar(out=A(r2), in0=A(rho), scalar1=-1.0, scalar2=None, op0=ALU.mult), [brow], [brow])
        S.op("dve", lambda e, L=L, A=A: e.tensor_tensor(out=A(r1), in0=A(r1), in1=A(rho), op=ALU.add), [brow], [brow])
        S.op("dve", lambda e, L=L, A=A: e.tensor_tensor_scan(out=rho[0:32, 0:L], data0=self.ones_row[0:32, 0:L], data1=r1[0:32, 0:L], initial=cst[0:32, 2:3],
                                                        op0=ALU.mult, op1=ALU.max), [brow, bcst, self.bconst], [brow])
        S.op("dve", lambda e, L=L, A=A: e.tensor_tensor_scan(out=rho[32:64, L - 1::-1], data0=self.ones_row[32:64, 0:L], data1=r1[32:64, L - 1::-1], initial=cst[32:64, 2:3],
                                                        op0=ALU.mult, op1=ALU.max), [brow, bcst, self.bconst], [brow])
        S.op("dve", lambda e, L=L, A=A: e.tensor_scalar(out=A(rho), in0=A(rho), scalar1=-1.0, scalar2=None, op0=ALU.mult), [brow], [brow])
        S.op("dve", lambda e, L=L, A=A: e.tensor_tensor(out=cst[0:32, 4:5], in0=r2[0:32, L - 1:L], in1=rho[0:32, L - 1:L], op=ALU.subtract), [brow, bcst], [bcst])
        S.op("dve", lambda e, L=L, A=A: e.tensor_tensor(out=cst[32:64, 4:5], in0=r2[32:64, 0:1], in1=rho[32:64, 0:1], op=ALU.subtract), [brow, bcst], [bcst])
        if g == 1:
            pi = si - 1
            for d in range(2):
                q0 = ((pi * 2 + e_) * 2 + d) * 4
                S.dma("sp", self.o_m[0:1, q0:q0 + 4].rearrange("one q -> q one"), cst[d * 32:d * 32 + 4, 4:5], self.bso, [bcst])
        S.op("dve", lambda e, L=L, A=A: e.tensor_tensor(out=A(r2), in0=A(rho), in1=A(r2), op=ALU.subtract), [brow], [brow])
        S.op("act", lambda e, L=L, A=A: e.activation(out=A(r2), in_=A(r2), func=AF.Exp), [brow], [brow])
        r3 = ar.alloc([Lmax])
        bR = scan_rows_prep(self, rho, r1, 64, L, cols, bcols, r3, None, brow, sdecT, extra=r2, rowsets=rowsets, rp_init=cst[:, 3:4], brp=bcst)
        bcast_rows(self, sdecT, bR, 64, nc_, SD, bSD)
        S.barrier()
        ar.off = mark
        hacc = ar.alloc([ncm, 128])
        bh = self.B("hacc")
        og = ar.alloc([ncm, 128])
        bog = self.B("og", True)
        sq = ar.alloc([ncm, 128])
        ms = ar.alloc([ncm, 2])
        bsq = self.B("sq")
        for h in range(4):
            S.dma("pool", QT[:, 0:L], self.PF[(12 + h) * 128:(13 + h) * 128, s0:s0 + L], bQT, [self.bPF])
            S.dma("pool", KT[:, 0:L], self.PF[(16 + h) * 128:(17 + h) * 128, s0:s0 + L], bKT, [self.bPF])
            S.dma("pool", Ktm[:, 0:nc_, :], self.PT[s0:s0 + L, (16 + h) * 128:(17 + h) * 128].rearrange("(c p) k -> p c k", p=128), bKtm, [self.bPT])
            S.op("act", lambda e, nc_=nc_: e.activation(out=Vtm[:, 0:nc_, 128:129], in_=self.onecol.unsqueeze(1).to_broadcast([128, nc_, 1]), func=AF.Copy),
                 [self.bconst], [bVtm])
            S.op("act", lambda e, nc_=nc_: e.activation(out=Vtm[:, 0:nc_, 129:130], in_=self.onecol.unsqueeze(1).to_broadcast([128, nc_, 1]), func=AF.Copy, scale=0.0),
                 [self.bconst], [bVtm])
            S.dma("pool", Vtm[:, 0:nc_, 0:128], self.PT[s0:s0 + L, (20 + h) * 128:(21 + h) * 128].rearrange("(c p) k -> p c k", p=128), bVtm, [self.bPT])
            S.dma("sp", og[:, 0:nc_, :], self.PT[s0:s0 + L, (24 + h) * 128:(25 + h) * 128].rearrange("(c p) k -> p c k", p=128), bog, [self.bPT])
            for d in range(2):
                S.op("pool", lambda e, d=d: e.memset(S32[d], 0.0), [], [bS[d]])
                if g == 0:
                    r0 = ((e_ * 2 + d) * 4 + h)
                    S.dma("sp", S32[d][:, 0:128], self.st_C[r0 * 128:(r0 + 1) * 128, :], bS[d])
                    S.dma("sp", S32[d][:, 128:129], self.st_n[r0:r0 + 1, :].rearrange("one q -> q one"), bS[d])
                S.op("act", lambda e, d=d: e.activation(out=Sr[d], in_=S32[d], func=AF.Copy), [bS[d]], [bS[d]])
                order = range(nc_) if d == 0 else range(nc_ - 1, -1, -1)
                for c in order:
                    bkG = self.g_bank()
                    S.op("pe", lambda e, c=c, bkG=bkG: e.matmul(self.banks[bkG][:, 0:128], lhsT=KT[:, c * 128:(c + 1) * 128], rhs=QT[:, c * 128:(c + 1) * 128],
                                                              start=True, stop=True), [bKT, bQT], [self.bbank[bkG]])
                    i = cnt % 2
                    cnt += 1
                    scan_unit(self, d, c, (self.banks[bkG][:, 0:128], self.bbank[bkG]), rho, brow, d * 32 + h, cols, bcols, d * 4 + h,
                              Vtm[:, c, :], bVtm, QT[:, c * 128:(c + 1) * 128], bQT, Ktm[:, c, :], bKtm,
                              S32[d], Sr[d], bS[d], SD[:, d * 32 + h, c:c + 1], bSD, 130, tot[i][:, 0:130], btot[i], False, w)
                    ci = d * 4 + h
                    S.op("act", lambda e, i=i: e.activation(out=tot[i][:, 132:133], in_=tot[i][:, 128:129], func=AF.Abs), [btot[i]], [btot[i]])
                    S.op("dve", lambda e, i=i, c=c, ci=ci: e.tensor_tensor(out=tot[i][:, 132:133], in0=tot[i][:, 132:133], in1=cols[:, c, 3, ci:ci + 1], op=ALU.max),
                         [btot[i], bcols], [btot[i]])
                    S.op("dve", lambda e, i=i: e.reciprocal(out=tot[i][:, 131:132], in_=tot[i][:, 132:133]), [btot[i]], [btot[i]])
                    if d == 0:
                        S.op("dve", lambda e, i=i, c=c: e.tensor_scalar(out=hacc[:, c, :], in0=tot[i][:, 0:128], scalar1=tot[i][:, 131:132], scalar2=None, op0=ALU.mult),
                             [btot[i]], [bh])
                    else:
                        S.op("dve", lambda e, i=i, c=c: e.scalar_tensor_tensor(out=hacc[:, c, :], in0=tot[i][:, 0:128], scalar=tot[i][:, 131:132], in1=hacc[:, c, :],
                                                                             op0=ALU.mult, op1=ALU.add), [btot[i], bh], [bh])
                if g == 1:
                    pi = si - 1
                    r0 = (((pi * 2 + e_) * 2 + d) * 4 + h)
                    S.dma("sp", self.o_C[r0 * 128:(r0 + 1) * 128, :], S32[d][:, 0:128], self.bso, [bS[d]])
                    S.dma("sp", self.o_n[r0:r0 + 1, :].rearrange("one q -> q one"), S32[d][:, 128:129], self.bso, [bS[d]])
            S.op("pool", lambda e, nc_=nc_: e.tensor_tensor(out=sq[:, 0:nc_, :], in0=hacc[:, 0:nc_, :], in1=hacc[:, 0:nc_, :], op=ALU.mult), [bh], [bsq])
            S.op("dve", lambda e, nc_=nc_: e.tensor_reduce(out=ms[:, 0:nc_, 0], in_=sq[:, 0:nc_, :], axis=AX.X, op=ALU.add), [bsq], [bsq])
            S.op("dve", lambda e, nc_=nc_: e.tensor_scalar(out=ms[:, 0:nc_, 0], in0=ms[:, 0:nc_, 0], scalar1=1.0 / 128, scalar2=EPS, op0=ALU.mult, op1=ALU.add), [bsq], [bsq])
            S.op("act", lambda e, nc_=nc_: e.activation(out=ms[:, 0:nc_, 1], in_=ms[:, 0:nc_, 0], func=AF.Sqrt), [bsq], [bsq])
            S.op("dve", lambda e, nc_=nc_: e.reciprocal(out=ms[:, 0:nc_, 0], in_=ms[:, 0:nc_, 1]), [bsq], [bsq])
            S.op("dve", lambda e, nc_=nc_: e.tensor_tensor(out=hacc[:, 0:nc_, :], in0=hacc[:, 0:nc_, :], in1=ms[:, 0:nc_, 0:1].to_broadcast([128, nc_, 128]), op=ALU.mult),
                 [bh, bsq], [bh])
            S.op("pool", lambda e, nc_=nc_, h=h: e.tensor_tensor(out=hacc[:, 0:nc_, :], in0=hacc[:, 0:nc_, :],
                                                               in1=ngm[:, h * 128:(h + 1) * 128].unsqueeze(1).to_broadcast([128, nc_, 128]), op=ALU.mult), [bh, bcst], [bh])
            S.op("act", lambda e, nc_=nc_: e.activation(out=og[:, 0:nc_, :], in_=og[:, 0:nc_, :], func=AF.Sigmoid), [bog], [bog])
            S.op("dve", lambda e, nc_=nc_: e.tensor_tensor(out=hacc[:, 0:nc_, :], in0=hacc[:, 0:nc_, :], in1=og[:, 0:nc_, :], op=ALU.mult), [bh, bog], [bh])
            S.dma("sp", self.YT[s0:s0 + L, 512 + h * 128:512 + (h + 1) * 128].rearrange("(c p) k -> p c k", p=128), hacc[:, 0:nc_, :], self.bYT, [bh])
    self.nrot = 8


MK.phase_mlstm = phase_mlstm


def hy_consts(self):
    self.hyc = {}
    for L in sorted(set(s[1] for s in self.seqs)):
        NB = L // 128 + 1
        self.hyc[L] = dict(
            NB=NB, CP=self.din("CP%d" % L, [NB * 128, NB * 128]), SP=self.din("SP%d" % L, [NB * 128, NB * 128]),
            featsT=self.din("featsT%d" % L, [33, L]), tcol=self.din("tcol%d" % L, [128, L // 128]),
            wcol=self.din("wcol%d" % L, [128, NB]), KS=self.dscr("KS%d" % L, [NB * 128, 2, 1024]), bKS=self.B("KS%d" % L, True, True))
    self.deltas = self.din("deltas", [512])


def sin_act(self, t, bt, tmp, n, L):
    S = self.S
    PI = math.pi
    v = t[0:n, 0:L]
    u = tmp[0:n, 0:L]
    S.op("dve", lambda e: e.tensor_single_scalar(out=u, in_=v, scalar=PI, op=ALU.is_gt), [bt], [bt])
    S.op("dve", lambda e: e.scalar_tensor_tensor(out=v, in0=u, scalar=-2 * PI, in1=v, op0=ALU.mult, op1=ALU.add), [bt], [bt])
    S.op("dve", lambda e: e.tensor_single_scalar(out=u, in_=v, scalar=-PI, op=ALU.is_lt), [bt], [bt])
    S.op("dve", lambda e: e.scalar_tensor_tensor(out=v, in0=u, scalar=2 * PI, in1=v, op0=ALU.mult, op1=ALU.add), [bt], [bt])
    S.op("act", lambda e: e.activation(out=v, in_=v, func=AF.Sin), [bt], [bt])


def dft_apply(self, hc, mats, rhs_list, nkc, nout, ncols, evac):
    S, ar = self.S, self.ar
    nm = len(mats)
    KG = 4
    NOB = 4 // nm
    mt = [[ar.alloc([KG, NOB * 128], F32R) for _ in range(2)] for _ in range(nm)]
    bmt = [[self.B("mt", True) for _ in range(2)] for _ in range(nm)]
    cnt = 0
    for ob0 in range(0, nout, NOB):
        nob = min(NOB, nout - ob0)
        acc = {(m, j): (self.next_bank(), 0) for m in range(nm) for j in range(nob)}
        for k0 in range(0, nkc, KG):
            nk = min(KG, nkc - k0)
            s = cnt % 2
            cnt += 1
            for m in range(nm):
                Mv = mats[m].rearrange("(kc p) n -> p kc n", p=128)
                S.dma("pool", mt[m][s][:, 0:nk, 0:nob * 128], Mv[:, k0:k0 + nk, ob0 * 128:(ob0 + nob) * 128], bmt[m][s])
            for m in range(nm):
                rt, brt = rhs_list[m]
                for j in range(nob):
                    bk, off = acc[(m, j)]
                    for kk in range(nk):
                        kc = k0 + kk
                        S.op("pe", lambda e, m=m, j=j, kk=kk, kc=kc, bk=bk, off=off, s=s, rt=rt: e.matmul(
                            self.banks[bk][:, off:off + ncols], lhsT=mt[m][s][:, kk, j * 128:(j + 1) * 128], rhs=rt[:, kc, 0:ncols],
                            start=(kc == 0), stop=(kc == nkc - 1), skip_group_check=True), [bmt[m][s], brt], [self.bbank[bk]])
        for j in range(nob):
            evac(ob0 + j, [self.banks[acc[(m, j)][0]][:, acc[(m, j)][1]:acc[(m, j)][1] + ncols] for m in range(nm)],
                 [self.bbank[acc[(m, j)][0]] for m in range(nm)])


def phase_hyena_filters(self, l, L):
    S, ar, W = self.S, self.ar, self.W
    e_ = l // 2
    hc = self.hyc[L]
    NB, nkc = hc["NB"], L // 128
    S.barrier()
    ar.reset()
    self.nrot = 8
    cst = ar.alloc([8])
    bcst = self.B("cst", True)
    for j, nm_ in enumerate(("hy_b1", "hy_b2", "hy_freq")):
        S.dma("sp", cst[0:64, j:j + 1], W[nm_][e_].rearrange("(q one) -> q one", one=1), bcst)
    dl = ar.alloc([512])
    S.dma("sp", dl, self.deltas.partition_broadcast(128), bcst)
    tcol = ar.alloc([nkc])
    S.dma("sp", tcol, hc["tcol"], bcst)
    S.op("dve", lambda e: e.tensor_scalar(out=tcol, in0=tcol, scalar1=-1.0, scalar2=None, op0=ALU.mult), [bcst], [bcst])
    wcol = ar.alloc([NB])
    S.dma("sp", wcol, hc["wcol"], bcst)
    ones = ar.alloc([128], F32R)
    bw = self.B("hw", True)
    S.op("act", lambda e: e.activation(out=ones, in_=self.onecol.to_broadcast([128, 128]), func=AF.Copy), [self.bconst], [bw])
    w3 = ar.alloc([2048], F32R)
    S.dma("pool", w3[0:64, :], W["hy_w3"][e_], bw)
    h2T = ar.alloc([L], F32R)
    bh = self.B("hT")
    rinv = ar.alloc([512])
    brinv = self.B("rinv")
    mark = ar.off
    fT = ar.alloc([L], F32R)
    w1 = ar.alloc([128], F32R)
    w2 = ar.alloc([128], F32R)
    S.op("act", lambda e: e.activation(out=w1, in_=self.onecol.to_broadcast([128, 128]), func=AF.Copy, scale=0.0), [self.bconst], [bw])
    S.op("act", lambda e: e.activation(out=w2, in_=self.onecol.to_broadcast([128, 128]), func=AF.Copy, scale=0.0), [self.bconst], [bw])
    h1T = ar.alloc([L], F32R)
    pre = ar.alloc([L])
    tmp = ar.alloc([L])
    bpre = self.B("pre")
    S.dma("pool", fT[0:33, :], hc["featsT"], bw)
    S.dma("pool", w1[0:33, 0:64], W["hy_w1"][e_], bw)
    S.dma("pool", w2[0:64, 0:64], W["hy_w2"][e_], bw)
    for (lhs, kk, src, bcol, dst) in ((w1, 33, fT, 0, h1T), (w2, 64, h1T, 1, h2T)):
        for t0 in range(0, L, 512):
            tw = min(512, L - t0)
            bk = self.next_bank()
            S.op("pe", lambda e, lhs=lhs, kk=kk, src=src, t0=t0, tw=tw, bk=bk: e.matmul(self.banks[bk][:, 0:tw], lhsT=lhs[0:kk, :], rhs=src[0:kk, t0:t0 + tw],
                                                                                   start=True, stop=True), [bw, bh], [self.bbank[bk]])
            S.op("dve", lambda e, t0=t0, tw=tw, bk=bk, bcol=bcol: e.tensor_scalar(out=pre[0:64, t0:t0 + tw], in0=self.banks[bk][0:64, 0:tw], scalar1=cst[0:64, bcol:bcol + 1],
                                                                               scalar2=cst[0:64, 2:3], op0=ALU.add, op1=ALU.mult), [self.bbank[bk], bcst], [bpre])
        sin_act(self, pre, bpre, tmp, 64, L)
        S.op("act", lambda e, dst=dst: e.activation(out=dst[0:64, 0:L], in_=pre[0:64, 0:L], func=AF.Copy), [bpre], [bh])
    S.barrier()
    ar.off = mark
    kbuf = ar.alloc([nkc, 512], F32R)
    bkb = self.B("kbuf")
    dec = ar.alloc([512])
    hf = ar.alloc([512])
    hb = ar.alloc([512])
    ab = ar.alloc([512], F32R)
    bwk = self.B("wk")
    og = [ar.alloc([2, 512]) for _ in range(2)]
    bog = [self.B("og") for _ in range(2)]
    for o in range(2):
        for sign, mat, slot in ((1.0, hc["CP"], 0), (-1.0, hc["SP"], 1)):
            bkn = self.next_bank()
            for c in range(nkc):
                bka, bkb_ = self.next_bank(), self.next_bank()
                if bka == bkn or bkb_ == bkn:
                    bka, bkb_ = self.next_bank(), self.next_bank()
                for d, bk in ((0, bka), (1, bkb_)):
                    c0 = (d * 2 + o) * 512
                    S.op("pe", lambda e, c=c, bk=bk, c0=c0: e.matmul(self.banks[bk][:, :], lhsT=h2T[0:64, c * 128:(c + 1) * 128], rhs=w3[0:64, c0:c0 + 512],
                                                                  start=True, stop=True), [bh, bw], [self.bbank[bk]])
                S.op("act", lambda e, c=c: e.activation(out=dec, in_=dl, func=AF.Exp, scale=tcol[:, c:c + 1]), [bcst], [bwk])
                S.op("dve", lambda e, bka=bka: e.tensor_tensor(out=hf, in0=self.banks[bka][:, :], in1=dec, op=ALU.mult), [self.bbank[bka], bwk], [bwk])
                S.op("dve", lambda e, bkb_=bkb_: e.tensor_tensor(out=hb, in0=self.banks[bkb_][:, :], in1=dec, op=ALU.mult), [self.bbank[bkb_], bwk], [bwk])
                if c == 0:
                    S.op("pool", lambda e: e.memset(hb[0:1, :], 0.0), [bwk], [bwk])
                if sign > 0:
                    S.op("pool", lambda e, c=c: e.tensor_tensor(out=kbuf[:, c, :], in0=hb, in1=hf, op=ALU.add), [bwk], [bkb])
                    S.op("act", lambda e: e.activation(out=hf, in_=hf, func=AF.Abs), [bwk], [bwk])
                    S.op("act", lambda e: e.activation(out=hb, in_=hb, func=AF.Abs), [bwk], [bwk])
                    S.op("pool", lambda e: e.tensor_tensor(out=ab, in0=hb, in1=hf, op=ALU.add), [bwk], [bwk])
                    S.op("pe", lambda e, c=c, bkn=bkn: e.matmul(self.banks[bkn][:, :], lhsT=ones, rhs=ab, start=(c == 0), stop=(c == nkc - 1), skip_group_check=True),
                         [bwk, bw], [self.bbank[bkn]])
                else:
                    S.op("pool", lambda e, c=c: e.tensor_tensor(out=kbuf[:, c, :], in0=hb, in1=hf, op=ALU.subtract), [bwk], [bkb])
            if sign > 0:
                S.op("dve", lambda e, bkn=bkn: e.reciprocal(out=rinv, in_=self.banks[bkn][:, :]), [self.bbank[bkn]], [brinv])

            def evac(ob, aps, bbs, o=o, slot=slot):
                i = ob % 2
                S.op("dve", lambda e: e.scalar_tensor_tensor(out=og[i][:, slot, :], in0=aps[0], scalar=wcol[:, ob:ob + 1], in1=rinv, op0=ALU.mult, op1=ALU.mult),
                     [bbs[0], bcst, brinv], [bog[i]])
                S.dma("sp", hc["KS"][ob * 128:(ob + 1) * 128, slot, o * 512:(o + 1) * 512], og[i][:, slot, :], hc["bKS"], [bog[i]])
            m2 = ar.off
            dft_apply(self, hc, [mat], [(kbuf, bkb)], nkc, NB, 512, evac)
            S.barrier()
            ar.off = m2


def phase_hyena(self, l):
    S, ar, W = self.S, self.ar, self.W
    e_ = l // 2
    for L in sorted(self.hyc):
        phase_hyena_filters(self, l, L)
    S.barrier()
    ar.reset()
    self.nrot = 8
    for si, (s0, L, g) in enumerate(self.seqs):
        for grp in range(2):
            hyena_group(self, e_, s0, L, grp)


def hyena_group(self, e_, s0, L, grp):
    S, ar, W = self.S, self.ar, self.W
    hc = self.hyc[L]
    NB, nkc = hc["NB"], L // 128
    if True:
        if True:
            S.barrier()
            ar.reset()
            z = ar.alloc([nkc, 256], F32R)
            bz = self.B("z", True)
            Y = ar.alloc([NB, 2, 256], F32R)
            bY = self.B("Y")
            gate = ar.alloc([nkc, 256])
            bgate = self.B("gate", True)
            bias = ar.alloc([2, 256])
            bbias = self.B("bias", True)
            for o in range(2):
                S.dma("sp", bias[:, o, :], W["hy_bias"][e_, o, grp * 256:(grp + 1) * 256].partition_broadcast(128), bbias)
            kt = [ar.alloc([2, 256]) for _ in range(2)]
            bkt = [self.B("kt", True) for _ in range(2)]
            t1 = [ar.alloc([256]) for _ in range(4)]
            bt1 = self.B("t1")
            S.dma("pool", z, self.PT[s0:s0 + L, grp * 256:(grp + 1) * 256].rearrange("(c p) k -> p c k", p=128), bz, [self.bPT])
            m2 = ar.off
            for o in range(2):
                S.dma("sp", gate, self.PT[s0:s0 + L, 512 * (o + 1) + grp * 256:512 * (o + 1) + (grp + 1) * 256].rearrange("(c p) k -> p c k", p=128),
                      bgate, [self.bPT])

                def evac_f(ob, aps, bbs, o=o, grp=grp):
                    i = ob % 2
                    S.dma("sp", kt[i], hc["KS"][ob * 128:(ob + 1) * 128, :, o * 512 + grp * 256:o * 512 + (grp + 1) * 256], bkt[i], [hc["bKS"]])
                    zre, zs = aps
                    S.op("dve", lambda e: e.tensor_tensor(out=t1[0], in0=zre, in1=kt[i][:, 0, :], op=ALU.mult), [bbs[0], bkt[i], bt1], [bt1])
                    S.op("dve", lambda e: e.tensor_tensor(out=t1[1], in0=zs, in1=kt[i][:, 1, :], op=ALU.mult), [bbs[1], bkt[i], bt1], [bt1])
                    S.op("pool", lambda e: e.tensor_tensor(out=Y[:, ob, 0, :], in0=t1[0], in1=t1[1], op=ALU.add), [bt1], [bY, bt1])
                    S.op("dve", lambda e: e.tensor_tensor(out=t1[2], in0=zs, in1=kt[i][:, 0, :], op=ALU.mult), [bbs[1], bkt[i], bt1], [bt1])
                    S.op("dve", lambda e: e.tensor_tensor(out=t1[3], in0=zre, in1=kt[i][:, 1, :], op=ALU.mult), [bbs[0], bkt[i], bt1], [bt1])
                    S.op("pool", lambda e: e.tensor_tensor(out=Y[:, ob, 1, :], in0=t1[2], in1=t1[3], op=ALU.subtract), [bt1], [bY, bt1])
                dft_apply(self, hc, [hc["CP"], hc["SP"]], [(z, bz), (z, bz)], nkc, NB, 256, evac_f)
                S.barrier()
                ar.off = m2
                Yre = Y[:, :, 0, :]
                Yim = Y[:, :, 1, :]

                def evac_i(tb, aps, bbs, o=o):
                    S.op("pool", lambda e: e.tensor_tensor(out=t1[0], in0=z[:, tb, :].bitcast(F32), in1=bias[:, o, :], op=ALU.mult), [bz, bbias, bt1], [bt1])
                    S.op("dve", lambda e: e.tensor_tensor(out=t1[0], in0=t1[0], in1=aps[0], op=ALU.add), [bt1, bbs[0]], [bt1])
                    S.op("dve", lambda e: e.tensor_tensor(out=t1[0], in0=t1[0], in1=aps[1], op=ALU.add), [bt1, bbs[1]], [bt1])
                    S.op("pool", lambda e: e.tensor_tensor(out=z[:, tb, :], in0=t1[0], in1=gate[:, tb, :], op=ALU.mult), [bt1, bgate], [bz, bt1])
                dft_apply(self, hc, [hc["CP"], hc["SP"]], [(Yre, bY), (Yim, bY)], NB, nkc, 256, evac_i)
                S.barrier()
                ar.off = m2
            S.dma("sp", self.YT[s0:s0 + L, grp * 256:(grp + 1) * 256].rearrange("(c p) k -> p c k", p=128), z.bitcast(F32), self.bYT, [bz])


MK.phase_hyena = phase_hyena


def hy_host_consts(Ls):
    out = {}
    for L in Ls:
        N = 2 * L
        NB = L // 128 + 1
        a = np.arange(NB * 128, dtype=np.int64)
        ph = (a[:, None] * a[None, :]) % N
        valid = (a[:, None] <= L) & (a[None, :] <= L)
        ang = ph.astype(np.float64) * (2.0 * np.pi / N)
        out["CP%d" % L] = np.where(valid, np.cos(ang), 0.0).astype(np.float32)
        out["SP%d" % L] = np.where(valid, np.sin(ang), 0.0).astype(np.float32)
        t = np.linspace(0.0, 1.0, L, dtype=np.float32)[:, None]
        angf = (np.float32(2.0 * math.pi / L) * np.arange(L, dtype=np.float32)[:, None] * np.linspace(1e-4, 15, 16, dtype=np.float32)[None])
        feats = np.concatenate([t, np.cos(angf), -np.sin(angf)], -1).astype(np.float32)
        out["featsT%d" % L] = np.ascontiguousarray(feats.T)
        out["tcol%d" % L] = np.ascontiguousarray(t[:, 0].reshape(L // 128, 128).T)
        f = np.arange(NB * 128)
        w = np.where(f > L, 0.0, np.where((f == 0) | (f == L), 1.0, 2.0)) / N
        out["wcol%d" % L] = np.ascontiguousarray(w.reshape(NB, 128).T.astype(np.float32))
    out["deltas"] = np.abs(np.linspace(math.log(1e-2) / 1.5, math.log(1e-2) / 0.3, 512, dtype=np.float32)).astype(np.float32)
    return out


def batch_work(self, n, dvmax=192):
    ar = self.ar
    return dict(E=[ar.alloc([128]) for _ in range(n)], bE=[self.B("E") for _ in range(n)],
                ST=[ar.alloc([128], F32R) for _ in range(n)], bST=[self.B("ST") for _ in range(n)],
                Aw=[ar.alloc([128], F32R) for _ in range(n)], bAw=[self.B("Aw") for _ in range(n)],
                tmp=[ar.alloc([dvmax]) for _ in range(n)], btmp=[self.B("tmp") for _ in range(n)])


def scan_step(self, U, w):
    S = self.S
    bank, bb = self.banks, self.bbank
    for u in U:
        bk, off = u["D"]
        S.op("pe", lambda e, u=u, bk=bk, off=off: e.matmul(bank[bk][:, off:off + 128], lhsT=self.sel[0:self.selrows, u["row"], :],
                                                        rhs=u["rho"][0:self.selrows, u["c"] * 128:(u["c"] + 1) * 128], start=True, stop=False),
             [u["brow"], self.bconst], [bb[bk]])
        S.op("pe", lambda e, u=u, bk=bk, off=off: e.matmul(bank[bk][:, off:off + 128], lhsT=self.ident, rhs=self.negmask[:, u["d"], :], start=False, stop=True),
             [self.bconst], [bb[bk]])
    for u in U:
        j = u["j"]
        bk, off = u["D"]
        S.op("act", lambda e, u=u, j=j, bk=bk, off=off: e.activation(out=w["E"][j], in_=bank[bk][:, off:off + 128], func=AF.Exp,
                                                                   bias=u["cols"][:, u["c"], 0, u["ci"]:u["ci"] + 1]), [bb[bk], u["bcols"]], [w["bE"][j]])
    for u in U:
        j = u["j"]
        gk, goff = u["G"]
        S.op("dve", lambda e, u=u, j=j, gk=gk, goff=goff: e.tensor_tensor(out=w["ST"][j], in0=bank[gk][:, goff:goff + 128], in1=w["E"][j], op=ALU.mult),
             [bb[gk], w["bE"][j]], [w["bST"][j]])
    for u in U:
        j, dv = u["j"], u["dv"]
        bk, o1, bk2, o2 = u["O"]
        S.op("pe", lambda e, u=u, j=j, dv=dv, bk=bk, o1=o1: e.matmul(bank[bk][:, o1:o1 + dv], lhsT=w["ST"][j], rhs=u["Xr"], start=True, stop=True),
             [w["bST"][j], u["bX"]], [bb[bk]])
        S.op("pe", lambda e, u=u, dv=dv, bk2=bk2, o2=o2: e.matmul(bank[bk2][:, o2:o2 + dv], lhsT=u["CTc"], rhs=u["Sr"], start=True, stop=True),
             [u["bCT"], u["bS"]], [bb[bk2]])
    for u in U:
        j, dv = u["j"], u["dv"]
        bk, o1, bk2, o2 = u["O"]
        S.op("act", lambda e, j=j, dv=dv, bk=bk, o1=o1: e.activation(out=w["tmp"][j][:, 0:dv], in_=bank[bk][:, o1:o1 + dv], func=AF.Copy), [bb[bk]], [w["btmp"][j]])
    for u in U:
        j, dv = u["j"], u["dv"]
        bk_, o1, bk, o2 = u["O"]
        fac = u["cols"][:, u["c"], 1, u["ci"]:u["ci"] + 1]
        if not u["acc"]:
            S.op("dve", lambda e, u=u, j=j, dv=dv, bk=bk, o2=o2, fac=fac: e.scalar_tensor_tensor(out=u["yout"], in0=bank[bk][:, o2:o2 + dv], scalar=fac, in1=w["tmp"][j][:, 0:dv],
                                                                                          op0=ALU.mult, op1=ALU.add), [bb[bk], u["bcols"], w["btmp"][j]], [u["byout"]])
        else:
            S.op("dve", lambda e, u=u, j=j, dv=dv, bk=bk, o2=o2, fac=fac: e.scalar_tensor_tensor(out=w["tmp"][j][:, 0:dv], in0=bank[bk][:, o2:o2 + dv], scalar=fac, in1=w["tmp"][j][:, 0:dv],
                                                                                          op0=ALU.mult, op1=ALU.add), [bb[bk], u["bcols"], w["btmp"][j]], [w["btmp"][j]])
            S.op("pool", lambda e, u=u, j=j, dv=dv: e.tensor_tensor(out=u["yout"], in0=u["yout"], in1=w["tmp"][j][:, 0:dv], op=ALU.add), [w["btmp"][j], u["byout"]], [u["byout"]])
    for u in U:
        if u.get("post"):
            u["post"](u)
    for u in U:
        j = u["j"]
        S.op("pool", lambda e, u=u, j=j: e.tensor_scalar(out=w["Aw"][j], in0=u["Btm"], scalar1=u["cols"][:, u["c"], 2, u["ci"]:u["ci"] + 1], scalar2=None, op0=ALU.mult),
             [u["bB"], u["bcols"]], [w["bAw"][j]])
    for u in U:
        j, dv = u["j"], u["dv"]
        bk, off = u["Sb"]
        S.op("pe", lambda e, u=u, j=j, dv=dv, bk=bk, off=off: e.matmul(bank[bk][:, off:off + dv], lhsT=w["Aw"][j], rhs=u["Xr"], start=True, stop=True),
             [w["bAw"][j], u["bX"]], [bb[bk]])
    for u in U:
        dv = u["dv"]
        bk, off = u["Sb"]
        S.op("dve", lambda e, u=u, dv=dv, bk=bk, off=off: e.scalar_tensor_tensor(out=u["S32"], in0=u["S32"], scalar=u["SDcol"], in1=bank[bk][:, off:off + dv], op0=ALU.mult, op1=ALU.add),
             [u["bS"], u["bSD"], bb[bk]], [u["bS"]])
    for u in U:
        S.op("act", lambda e, u=u: e.activation(out=u["Sr"], in_=u["S32"], func=AF.Copy), [u["bS"]], [u["bS"]])


FULL_SEQS = [(0, 4096, 0), (4096, 256, 1), (4352, 256, 1), (4608, 256, 1), (4864, 256, 1)]
_CACHE = {}


def _pack_cw(w, b):
    n = w.shape[1] // 128
    a = np.concatenate([w, b[None]], 0)
    return np.ascontiguousarray(a.reshape(4, n, 128).transpose(2, 1, 0).reshape(128, n * 4))


def build_full(seqs=FULL_SEQS, depth=4):
    m = MK(dict(seqs=seqs))
    m.setup()
    hy_consts(m)
    S = m.S
    for l in range(depth):
        m.phase_mod(l)
        m.phase_in(l)
        m.phase_prep(l)
        if l % 2 == 0:
            m.phase_hyena(l)
            m.phase_mlstm(l)
        else:
            m.phase_ssd(l)
        m.phase_mod(l, reload=True)
        m.phase_out(l)
        m.phase_ffn(l, final=(l == depth - 1), perm=((l == 1 or l == 3) and seqs[0][1] == 4096))
    S.barrier()
    S.finish("sp", [m.bYO, m.bso])
    S.emit()
    m.st.close()
    return m


def host_consts(seqs):
    f32 = np.float32
    c = hy_host_consts(sorted(set(s[1] for s in seqs)))
    c["ident"] = np.eye(128, dtype=f32)
    s_i, l_i = np.arange(128)[:, None], np.arange(128)[None, :]
    c["negmask"] = np.concatenate([np.where(s_i > l_i, NEG, 0.0), np.where(s_i < l_i, NEG, 0.0)], 1).astype(f32)
    return c


def kernel(**inp):
    f32 = np.float32
    if "m" not in _CACHE:
        _CACHE["m"] = build_full()
        _CACHE["c"] = host_consts(FULL_SEQS)
    m = _CACHE["m"]
    xp = np.asarray(inp["x_prompt"], f32)
    xs = np.asarray(inp["x_sample"], f32)
    c = np.asarray(inp["c"], f32)
    cctx = np.asarray(inp["c_ctx"], f32)
    shared = {k: np.ascontiguousarray(np.asarray(inp[k], f32)) for k in WSHAPES if k in inp}
    shared["ev_cw"] = np.stack([_pack_cw(shared["ev_conv_w"][e], shared["ev_conv_b"][e]) for e in range(2)])
    shared["od_cw"] = np.stack([_pack_cw(shared["od_conv_w"][e], shared["od_conv_b"][e]) for e in range(2)])
    shared.update(_CACHE["c"])
    sC = np.asarray(inp["state_mlstm_C"], f32)
    sn = np.asarray(inp["state_mlstm_n"], f32)
    sm = np.asarray(inp["state_mlstm_m"], f32)
    ss = np.asarray(inp["state_ssd"], f32)
    in_maps = []
    for b in range(8):
        d = dict(shared)
        d["xin"] = np.ascontiguousarray(np.concatenate([xs[b], xp[4 * b:4 * b + 4].reshape(1024, 1024)], 0))
        cond = np.stack([c[b], cctx], 0)
        d["condT"] = np.ascontiguousarray(cond.reshape(2, 8, 128).transpose(2, 0, 1).reshape(128, 16))
        d["st_ssd"] = np.ascontiguousarray(ss[b].reshape(-1, 128))
        d["st_C"] = np.ascontiguousarray(sC[b].reshape(-1, 128))
        d["st_n"] = np.ascontiguousarray(sn[b].reshape(-1, 128))
        d["st_m"] = np.ascontiguousarray(sm[b].reshape(1, 16))
        in_maps.append({k: v for k, v in d.items() if k in m.ins})
    res = run_bass_kernel_spmd(m.nc, in_maps, core_ids=list(range(8)))
    outs = res.results
    y_sample = np.stack([outs[b]["yout"][:4096] for b in range(8)], 0)
    y_prompt = np.concatenate([outs[b]["yout"][4096:].reshape(4, 256, 1024) for b in range(8)], 0)
    newC = np.concatenate([outs[b]["newC"].reshape(4, 2, 2, 4, 128, 128) for b in range(8)], 0)
    newn = np.concatenate([outs[b]["newn"].reshape(4, 2, 2, 4, 128) for b in range(8)], 0)
    newm = np.concatenate([outs[b]["newm"].reshape(4, 2, 2, 4) for b in range(8)], 0)
    newssd = np.concatenate([outs[b]["newssd"].reshape(4, 2, 2, 32, 64, 128) for b in range(8)], 0)
    return (y_prompt.astype(f32), y_sample.astype(f32), newC.astype(f32), newn.astype(f32), newm.astype(f32), newssd.astype(f32))
```

```python
import contextlib
import math
import numpy as np
import concourse.bass as bass
import concourse.mybir as mybir
from concourse.bass_utils import run_bass_kernel_spmd

F32 = mybir.dt.float32
BF16 = mybir.dt.bfloat16
F32R = mybir.dt.float32r
I32 = mybir.dt.int32
ALU = mybir.AluOpType
AF = mybir.ActivationFunctionType
AX = mybir.AxisListType


class Buf:
    __slots__ = ("name", "w", "r", "dsem", "dcnt")

    def __init__(self, name):
        self.name = name
        self.w = None
        self.r = {}
        self.dsem = None
        self.dcnt = 0


class Sched:
    ENGS = ("pe", "act", "dve", "pool", "sp")

    def __init__(self, nc, stack):
        self.nc = nc
        self.stack = stack
        self.h = {"pe": nc.tensor, "act": nc.scalar, "dve": nc.vector, "pool": nc.gpsimd, "sp": nc.sync}
        self.sems = {}
        self.issued = {}
        self.prog = {e: [] for e in self.ENGS}
        self.waited = {e: {} for e in self.ENGS}
        self.n = {e: 0 for e in self.ENGS}
        for e in self.ENGS:
            self._sem("E_" + e)
        self.nd = 0
        self.dma_sem_pool = []
        self.pool_i = 0

    def _sem(self, key):
        if key not in self.sems:
            self.sems[key] = self.stack.enter_context(self.nc.semaphore(key))
            self.issued[key] = 0
        return key

    def buf(self, name, dma=False, persistent=False):
        b = Buf(name)
        if dma:
            if persistent:
                b.dsem = self._sem("D_" + name)
            else:
                if self.pool_i == len(self.dma_sem_pool):
                    self.dma_sem_pool.append(self._sem("D_pool%d" % self.pool_i))
                b.dsem = self.dma_sem_pool[self.pool_i]
                self.pool_i += 1
        return b

    def recycle(self):
        self.pool_i = 0

    def share_dsem(self, b, other):
        b.dsem = other.dsem
        return b

    def _deps(self, eng, reads, writes):
        deps = {}

        def add(k, v):
            if k is None:
                return
            if deps.get(k, 0) < v:
                deps[k] = v
        for b in reads:
            if b.w is not None:
                add(*b.w)
        for b in writes:
            if b.w is not None:
                add(*b.w)
            for k, v in b.r.items():
                add(k, v)
        out = []
        wd = self.waited[eng]
        for k, v in deps.items():
            if eng == "pe" and k == "E_pe":
                continue
            if k.startswith("D_"):
                v = self.issued[k]
            if wd.get(k, 0) >= v:
                continue
            wd[k] = v
            out.append((k, v))
        return out

    def op(self, eng, fn, reads=(), writes=()):
        waits = self._deps(eng, reads, writes)
        self.n[eng] += 1
        key = "E_" + eng
        val = self.n[eng]
        self.issued[key] = val
        self.prog[eng].append((waits, fn, key, 1))
        for b in reads:
            if b.r.get(key, 0) < val:
                b.r[key] = val
        for b in writes:
            b.w = (key, val)
            b.r = {}
        return val

    def dma(self, eng, out_ap, in_ap, dst, src=(), **kw):
        assert dst.dsem is not None, dst.name
        waits = self._deps(eng, src, [dst])
        key = dst.dsem
        self.issued[key] += 16
        val = self.issued[key]

        def fn(e, out_ap=out_ap, in_ap=in_ap, kw=kw):
            return e.dma_start(out=out_ap, in_=in_ap, **kw)
        self.prog[eng].append((waits, fn, key, 16))
        for b in src:
            if b.r.get(key, 0) < val:
                b.r[key] = val
        dst.w = (key, val)
        dst.r = {}
        self.nd += 1

    def barrier(self):
        snap = dict(self.issued)
        for e in self.ENGS:
            waits = []
            wd = self.waited[e]
            for k, v in snap.items():
                if v == 0 or wd.get(k, 0) >= v:
                    continue
                if e == "pe" and k == "E_pe":
                    continue
                wd[k] = v
                waits.append((k, v))
            if waits:
                self.prog[e].append((waits, None, None, 0))
        self.recycle()

    def finish(self, eng="sp", bufs=()):
        waits = self._deps(eng, bufs, [])
        self.prog[eng].append((waits, None, None, 0))

    def emit(self):
        nc = self.nc
        with nc.Block() as block:
            def mk(ename):
                def body(e):
                    for waits, fn, key, inc in self.prog[ename]:
                        if fn is None:
                            for k, v in waits:
                                e.wait_ge(self.sems[k], v)
                            continue
                        if ename == "pe":
                            for k, v in waits:
                                e.wait_ge(self.sems[k], v)
                            ins = fn(e)
                        else:
                            for k, v in waits[1:]:
                                e.wait_ge(self.sems[k], v)
                            ins = fn(e)
                            if waits:
                                ins._wait_ge(self.sems[waits[0][0]], waits[0][1])
                        ins.then_inc(self.sems[key], inc)
                return body
            block.tensor(mk("pe"))
            block.scalar(mk("act"))
            block.vector(mk("dve"))
            block.gpsimd(mk("pool"))
            block.sync(mk("sp"))


D = 1024
FH = 2816
EV_NP, OD_NP = 3600, 6208
EPS = 1e-6


def cdiv(a, b):
    return -(-a // b)


class Arena:
    def __init__(self, t, words):
        self.t, self.words, self.off = t, words, 0

    def reset(self):
        self.off = 0

    def alloc(self, free_shape, dtype=F32, parts=128):
        n = int(np.prod(free_shape))
        esz = 2 if dtype == BF16 else 4
        w = cdiv(n * esz, 4)
        assert self.off + w <= self.words, ("arena overflow", self.off, w, self.words)
        ap = self.t[0:parts, self.off:self.off + w]
        self.off += w
        if dtype != F32 and dtype != F32R:
            ap = ap.bitcast(dtype)
        if len(free_shape) == 2:
            ap = ap.rearrange("p (a b) -> p a b", a=free_shape[0])
        elif len(free_shape) == 3:
            ap = ap.rearrange("p (a b c) -> p a b c", a=free_shape[0], b=free_shape[1])
        return ap


class Dual:
    def __init__(self, a32, a32r):
        self.a, self.r = a32, a32r

    def alloc(self, free_shape, dtype=F32, parts=128):
        return (self.r if dtype == F32R else self.a).alloc(free_shape, dtype, parts)

    @property
    def off(self):
        return (self.a.off, self.r.off)

    @off.setter
    def off(self, v):
        self.a.off, self.r.off = v

    def reset(self):
        self.a.off = self.r.off = 0


class MK:
    def __init__(self, cfg):
        self.cfg = cfg
        self.seqs = cfg["seqs"]
        self.T = sum(s[1] for s in self.seqs)
        self.depth = cfg.get("depth", 4)
        self.dbg = cfg.get("debug", ())
        self.nc = bass.Bass("TRN2", target_bir_lowering=False)
        self.st = contextlib.ExitStack()
        self.S = Sched(self.nc, self.st)
        self.ins = {}
        self.nbuf = 0

    def din(self, name, shape, dtype=F32):
        t = self.nc.dram_tensor(name, list(shape), dtype, kind="ExternalInput")
        self.ins[name] = t
        return t.ap()

    def dscr(self, name, shape, out=False):
        kind = "ExternalOutput" if (out or name in self.dbg) else "Internal"
        return self.nc.dram_tensor(name, list(shape), F32, kind=kind).ap()

    def B(self, name, dma=False, persistent=False):
        self.nbuf += 1
        return self.S.buf("%s_%d" % (name, self.nbuf), dma, persistent)

    def tiles(self):
        out = []
        g0 = [s for s in self.seqs if s[2] == 0]
        g1 = [s for s in self.seqs if s[2] == 1]
        for grp in (g0, g1):
            if not grp:
                continue
            a = grp[0][0]
            b = grp[-1][0] + grp[-1][1]
            assert (b - a) % 512 == 0
            for t0 in range(a, b, 512):
                out.append((t0, grp[0][2]))
        return out

    def setup(self):
        nc, st, T = self.nc, self.st, self.T
        self.xin = self.din("xin", [T, D])
        self.condT = self.din("condT", [128, 16])
        self.ident_d = self.din("ident", [128, 128])
        W = {}
        for name, shape in WSHAPES.items():
            W[name] = self.din(name, shape)
        self.W = W
        self.X = self.dscr("X", [T, D])
        self.MODS = self.dscr("MODS", [2, 128, 6 * D])
        self.bMODS = self.B("MODS", True, True)
        self.Xalt = self.dscr("Xalt", [T, D])
        self.bXalt = self.B("Xalt", True, True)
        self.PF = self.dscr("PF", [49 * 128, T])
        self.PT = self.dscr("PT", [T, 49 * 128])
        self.YT = self.dscr("YT", [T, 2048])
        self.yout = self.dscr("yout", [T, D], out=True)
        self.bX, self.bPF, self.bPT, self.bYT, self.bYO = (self.B("X", True, True), self.B("PF", True, True), self.B("PT", True, True),
                                                        self.B("YT", True, True), self.B("YO", True, True))
        AW, AWR = 21 * 1024 + 768, 29 * 1024
        self.arena_t = st.enter_context(nc.sbuf_tensor("arena", [128, AW], F32))
        self.arena_r = st.enter_context(nc.sbuf_tensor("arenar", [128, AWR], F32R))
        self.ar = Dual(Arena(self.arena_t, AW), Arena(self.arena_r, AWR))
        self.pers_t = st.enter_context(nc.sbuf_tensor("pers", [128, 1024], F32))
        self.pers = Arena(self.pers_t, 1024)
        self.banks = [st.enter_context(nc.psum_tensor("bank%d" % i, [128, 512], F32)) for i in range(8)]
        self.bbank = [self.B("bank%d" % i) for i in range(8)]
        S = self.S
        self.ident = self.pers.alloc([128])
        self.bconst = self.B("const", True, True)
        S.dma("sp", self.ident, self.ident_d, self.bconst)
        self.csil = self.pers.alloc([16])
        self.bcs = self.B("csil", True, True)
        S.dma("sp", self.csil, self.condT, self.bcs)
        S.op("act", lambda e: e.activation(out=self.csil, in_=self.csil, func=AF.Silu), [self.bcs], [self.bcs])
        self.negmask = self.pers.alloc([2, 128])
        self.negmask_d = self.din("negmask", [128, 256])
        S.dma("sp", self.negmask, self.negmask_d.rearrange("p (a b) -> p a b", a=2), self.bconst)
        self.onecol = self.pers.alloc([1])
        S.op("pool", lambda e: e.memset(self.onecol, 1.0), [], [self.bconst])
        self.ident_bf = self.pers.alloc([128], BF16)
        self.negmask_bf = self.pers.alloc([2, 128], BF16)
        S.op("act", lambda e: e.activation(out=self.ident_bf, in_=self.ident, func=AF.Copy), [self.bconst], [self.bconst])
        S.op("act", lambda e: e.activation(out=self.negmask_bf, in_=self.negmask, func=AF.Copy), [self.bconst], [self.bconst])
        mk_ = self

        class SelView:
            def __getitem__(self, key):
                rows, r, _ = key
                n = rows.stop - rows.start
                return mk_.ident[rows, r:r + 1].to_broadcast([n, 128])

        class OnesView:
            def __getitem__(self, key):
                rows, cols_ = key
                return mk_.onecol[rows, 0:1].to_broadcast([rows.stop - rows.start, cols_.stop - cols_.start])
        self.sel = SelView()
        self.ones_row = OnesView()
        self.st_ssd = self.din("st_ssd", [2 * 2 * 32 * 64, 128])
        self.st_C = self.din("st_C", [2 * 2 * 4 * 128, 128])
        self.st_n = self.din("st_n", [2 * 2 * 4, 128])
        self.st_m = self.din("st_m", [1, 16])
        self.o_ssd = self.dscr("newssd", [4 * 2 * 2 * 32 * 64, 128], out=True)
        self.o_C = self.dscr("newC", [4 * 2 * 2 * 4 * 128, 128], out=True)
        self.o_n = self.dscr("newn", [4 * 2 * 2 * 4, 128], out=True)
        self.o_m = self.dscr("newm", [1, 64], out=True)
        self.bso = self.B("so", True, True)
        self.YS = self.dscr("YS", [T, 2048])
        self.bYS = self.B("YS", True, True)
        self.nrot = 8
        self.rr = 0

    def evac_eng(self):
        self.rr += 1
        return "act" if self.rr % 2 else "dve"

    def copy(self, eng, out, in_, reads, writes):
        if eng == "act":
            self.S.op("act", lambda e: e.activation(out=out, in_=in_, func=AF.Copy), reads, writes)
        else:
            self.S.op(eng, lambda e: e.tensor_copy(out=out, in_=in_), reads, writes)

    def phase_mod(self, l, reload=False):
        S, ar, W = self.S, self.ar, self.W
        S.barrier()
        ar.reset()
        modv = [ar.alloc([6, D]) for _ in range(2)]
        self.modv = modv
        self.bmod = [self.B("modv", True) for _ in range(2)]
        self.arena_base = ar.off
        if reload:
            for g in range(2):
                S.dma("sp", modv[g].rearrange("p a b -> p (a b)"), self.MODS[g], self.bmod[g], [self.bMODS])
            return
        cb = ar.alloc([16, 128], F32R)
        bcb = self.B("cb")
        for i in range(16):
            S.op("dve", lambda e, i=i: e.tensor_copy(out=cb[:, i, :], in_=self.csil[:, i:i + 1].to_broadcast([128, 128])),
                 [self.bcs], [bcb])
        mb = ar.alloc([6 * D])
        bmb = self.B("mb", True)
        S.dma("sp", mb, W["mod_b"][l, :].partition_broadcast(128), bmb)
        wp = [ar.alloc([8, 512], F32R) for _ in range(2)]
        bwp = [self.B("wp", True) for _ in range(2)]
        wv = W["mod_w"][l].rearrange("(kc p) n -> p kc n", p=128)
        for nt in range(12):
            s = nt % 2
            S.dma("pool", wp[s], wv[:, :, nt * 512:(nt + 1) * 512], bwp[s])
            for g in range(2):
                bk = (2 * nt + g) % 8
                for kc in range(8):
                    S.op("pe", lambda e, g=g, kc=kc, bk=bk, s=s: e.matmul(self.banks[bk][:], lhsT=cb[:, g * 8 + kc, :],
                                                                        rhs=wp[s][:, kc, :], start=(kc == 0), stop=(kc == 7)),
                         [bcb, bwp[s]], [self.bbank[bk]])
                dst = modv[g].rearrange("p a b -> p (a b)")[:, nt * 512:(nt + 1) * 512]
                S.op("dve", lambda e, dst=dst, bk=bk, nt=nt: e.tensor_tensor(out=dst, in0=self.banks[bk][:],
                                                                          in1=mb[:, nt * 512:(nt + 1) * 512], op=ALU.add),
                     [self.bbank[bk], bmb], [self.bmod[g]])
        gb = ar.alloc([2, D])
        bgb = self.B("gb", True)
        S.dma("sp", gb[:, 0, :], W["norm1_g"][l, :].partition_broadcast(128), bgb)
        S.dma("sp", gb[:, 1, :], W["norm2_g"][l, :].partition_broadcast(128), bgb)
        for g in range(2):
            for k, slot in ((0, 1), (1, 4)):
                S.op("dve", lambda e, g=g, k=k, slot=slot: e.scalar_tensor_tensor(
                    out=modv[g][:, slot, :], in0=modv[g][:, slot, :], scalar=1.0, in1=gb[:, k, :],
                    op0=ALU.add, op1=ALU.mult), [self.bmod[g], bgb], [self.bmod[g]])
        for g in range(2):
            S.dma("sp", self.MODS[g], modv[g].rearrange("p a b -> p (a b)"), self.bMODS, [self.bmod[g]])
        S.barrier()
        ar.off = self.arena_base

    def norm_tile(self, x, bx, h, bh, A, Bv, bAB, junk, bj, st2, bst):
        S = self.S
        S.op("act", lambda e: e.activation(out=junk, in_=x, func=AF.Square, accum_out=st2[:, 0:1]), [bx], [bj, bst])
        S.op("dve", lambda e: e.tensor_scalar(out=st2[:, 1:2], in0=st2[:, 0:1], scalar1=1.0 / D, scalar2=EPS,
                                              op0=ALU.mult, op1=ALU.add), [bst], [bst])
        S.op("act", lambda e: e.activation(out=st2[:, 3:4], in_=st2[:, 1:2], func=AF.Sqrt), [bst], [bst])
        S.op("dve", lambda e: e.reciprocal(out=st2[:, 2:3], in_=st2[:, 3:4]), [bst], [bst])
        if A is None:
            S.op("dve", lambda e: e.tensor_scalar(out=h, in0=x, scalar1=st2[:, 2:3], scalar2=None, op0=ALU.mult),
                 [bx, bst], [bh])
            return
        S.op("dve", lambda e: e.scalar_tensor_tensor(out=h, in0=x, scalar=st2[:, 2:3], in1=A, op0=ALU.mult, op1=ALU.mult),
             [bx, bst] + bAB, [bh])
        if Bv is not None:
            S.op("pool", lambda e: e.tensor_tensor(out=h, in0=h, in1=Bv, op=ALU.add), [bh] + bAB, [bh])

    def to_kmajor(self, src, bsrc, nk, dstT, bdst, col0):
        S = self.S
        for k0 in range(0, nk, 4):
            n = min(4, nk - k0)
            bk = self.next_bank()
            for j in range(n):
                S.op("pe", lambda e, j=j, k0=k0, bk=bk: e.transpose(out=self.banks[bk][:, j * 128:(j + 1) * 128],
                                                                  in_=src[:, (k0 + j) * 128:(k0 + j + 1) * 128], identity=self.ident),
                     [bsrc, self.bconst], [self.bbank[bk]])
            self.copy(self.evac_eng(), dstT[:, k0:k0 + n, col0:col0 + 128],
                      self.banks[bk][:, 0:n * 128].rearrange("p (a b) -> p a b", a=n), [self.bbank[bk]], [bdst])

    def next_bank(self):
        self.bkrr = (getattr(self, "bkrr", -1) + 1) % self.nrot
        return self.bkrr

    def g_bank(self):
        self.gkrr = 13 - getattr(self, "gkrr", 7)
        return self.gkrr

    def phase_in(self, l):
        S, ar, W = self.S, self.ar, self.W
        S.barrier()
        ar.off = self.arena_base
        even = (l % 2 == 0)
        NP = EV_NP if even else OD_NP
        Win = (W["ev_in_w"] if even else W["od_in_w"])[l // 2]
        Wv = Win.rearrange("(kc p) n -> p kc n", p=128)
        xt = [ar.alloc([D]) for _ in range(2)]
        bxt = [self.B("xt", True) for _ in range(2)]
        ht = [ar.alloc([D]) for _ in range(2)]
        bht = [self.B("ht") for _ in range(2)]
        junk = ar.alloc([D])
        bj = self.B("junk")
        st2 = [ar.alloc([4]) for _ in range(2)]
        bst = [self.B("st") for _ in range(2)]
        hT = ar.alloc([8, 1024], F32R)
        bhT = self.B("hT")
        wp = [ar.alloc([8, 512], F32R) for _ in range(2)]
        bwp = [self.B("wp", True) for _ in range(2)]
        og = [ar.alloc([512]) for _ in range(4)]
        bog = [self.B("og") for _ in range(4)]
        Xsrc = self.xin if l == 0 else self.X
        bXs = None if l == 0 else self.bX
        npan = cdiv(NP, 512)
        cnt = 0
        tl = self.tiles()
        groups = []
        while tl:
            if len(tl) > 1 and tl[0][1] == tl[1][1]:
                groups.append(tl[:2])
                tl = tl[2:]
            else:
                groups.append(tl[:1])
                tl = tl[1:]
        for grp in groups:
            for ti, (t0, g) in enumerate(grp):
                mv = self.modv[g]
                for s4 in range(4):
                    i = cnt % 2
                    cnt += 1
                    S.dma("sp", xt[i], Xsrc[t0 + s4 * 128:t0 + (s4 + 1) * 128, :], bxt[i], [bXs] if bXs else [])
                    self.norm_tile(xt[i], bxt[i], ht[i], bht[i], mv[:, 1, :], mv[:, 0, :], [self.bmod[g]], junk, bj, st2[i], bst[i])
                    self.to_kmajor(ht[i], bht[i], 8, hT, bhT, ti * 512 + s4 * 128)
            for pj in range(npan):
                c0 = pj * 512
                cw = min(512, NP - c0)
                s = pj % 2
                S.dma("pool", wp[s][:, :, 0:cw], Wv[:, :, c0:c0 + cw], bwp[s])
                for ti, (t0, g) in enumerate(grp):
                    for b0 in range(0, cw, 128):
                        bw = min(128, cw - b0)
                        bk = self.next_bank()
                        for kc in range(8):
                            S.op("pe", lambda e, kc=kc, bk=bk, s=s, b0=b0, bw=bw, ti=ti: e.matmul(
                                self.banks[bk][0:bw, :], lhsT=wp[s][:, kc, b0:b0 + bw], rhs=hT[:, kc, ti * 512:(ti + 1) * 512],
                                start=(kc == 0), stop=(kc == 7)), [bwp[s], bhT], [self.bbank[bk]])
                        o = cnt % 4
                        cnt += 1
                        self.copy(self.evac_eng(), og[o][0:bw, :], self.banks[bk][0:bw, :], [self.bbank[bk]], [bog[o]])
                        S.dma("sp", self.PF[c0 + b0:c0 + b0 + bw, t0:t0 + 512], og[o][0:bw, :], self.bPF, [bog[o]])

    def phase_out(self, l):
        S, ar, W = self.S, self.ar, self.W
        S.barrier()
        ar.off = self.arena_base
        even = (l % 2 == 0)
        KI = 1024 if even else 2048
        nk = KI // 128
        Wout = (W["ev_out_w"] if even else W["od_out_w"])[l // 2]
        wo = ar.alloc([nk, D], F32R)
        bwo = self.B("wo", True)
        Wv = Wout.rearrange("(kc p) n -> p kc n", p=128)
        for kc in range(nk):
            for h in range(2):
                S.dma("pool", wo[:, kc, h * 512:(h + 1) * 512], Wv[:, kc, h * 512:(h + 1) * 512], bwo)
        yt = [ar.alloc([KI]) for _ in range(2)]
        byt = [self.B("yt", True) for _ in range(2)]
        yT = [ar.alloc([nk, 128], F32R) for _ in range(2)]
        byT = [self.B("yT") for _ in range(2)]
        xt = [ar.alloc([D]) for _ in range(2)]
        bxt = [self.B("xt", True) for _ in range(2)]
        Xsrc = self.xin if l == 0 else self.X
        bXs = None if l == 0 else self.bX
        cnt = 0
        for (t0, g) in self.tiles():
            G1 = self.modv[g][:, 2, :]
            for s4 in range(4):
                i = cnt % 2
                cnt += 1
                r0 = t0 + s4 * 128
                S.dma("sp", yt[i], self.YT[r0:r0 + 128, 0:KI], byt[i], [self.bYT])
                S.dma("sp", xt[i], Xsrc[r0:r0 + 128, :], bxt[i], [bXs] if bXs else [])
                self.to_kmajor(yt[i], byt[i], nk, yT[i], byT[i], 0)
                for h in range(2):
                    bk = self.next_bank()
                    for kc in range(nk):
                        S.op("pe", lambda e, kc=kc, bk=bk, i=i, h=h: e.matmul(
                            self.banks[bk][:], lhsT=yT[i][:, kc, :], rhs=wo[:, kc, h * 512:(h + 1) * 512],
                            start=(kc == 0), stop=(kc == nk - 1)), [byT[i], bwo], [self.bbank[bk]])
                    S.op("dve", lambda e, bk=bk, i=i, h=h, G1=G1: e.tensor_tensor(
                        out=yt[i][:, h * 512:(h + 1) * 512], in0=self.banks[bk][:], in1=G1[:, h * 512:(h + 1) * 512], op=ALU.mult),
                        [self.bbank[bk], self.bmod[g]], [byt[i]])
                    S.op("pool", lambda e, i=i, h=h: e.tensor_tensor(
                        out=xt[i][:, h * 512:(h + 1) * 512], in0=xt[i][:, h * 512:(h + 1) * 512],
                        in1=yt[i][:, h * 512:(h + 1) * 512], op=ALU.add), [byt[i], bxt[i]], [bxt[i]])
                S.dma("sp", self.X[r0:r0 + 128, :], xt[i], self.bX, [bxt[i]])

    def store_rows(self, dst, bdst, r0, src, bsrc, g, perm):
        S = self.S
        if perm and g == 0:
            v = dst[0:4096, :].rearrange("(c r) d -> r c d", r=64)
            ra = r0 // 64
            for j in range(2):
                S.dma("sp", v[ra + j], src[j * 64:(j + 1) * 64, :], bdst, [bsrc])
        else:
            S.dma("sp", dst[r0:r0 + 128, :], src, bdst, [bsrc])

    def phase_ffn(self, l, final=False, perm=False):
        S, ar, W = self.S, self.ar, self.W
        S.barrier()
        ar.off = self.arena_base
        W1 = W["ffn_w1"][l].rearrange("(kc p) n -> p kc n", p=128)
        W3 = W["ffn_w3"][l].rearrange("(kc p) n -> p kc n", p=128)
        W2 = W["ffn_w2"][l].rearrange("(kc p) n -> p kc n", p=128)
        NJ = FH // 128
        xt = ar.alloc([4, D])
        bxt = [self.B("xt", True) for _ in range(4)]
        ht = [ar.alloc([D]) for _ in range(2)]
        bht = [self.B("ht") for _ in range(2)]
        junk = ht[1]
        bj = bht[1]
        st2 = [ar.alloc([4]) for _ in range(2)]
        bst = [self.B("st") for _ in range(2)]
        hT = ar.alloc([8, 512], F32R)
        bhT = self.B("hT")
        w13 = [[ar.alloc([8, 256], F32R) for _ in range(2)] for _ in range(2)]
        bw13 = [[self.B("w13", True) for _ in range(2)] for _ in range(2)]
        uT = ar.alloc([NJ, 512], F32R)
        buT = self.B("uT")
        sg = [ar.alloc([512]) for _ in range(2)]
        bsg = [self.B("sg") for _ in range(2)]
        w2 = ar.alloc([NJ, 256], F32R)
        bw2 = self.B("w2", True)
        if final:
            fg = ar.alloc([D])
            bfg = self.B("fg", True)
            S.dma("sp", fg, W["final_g"].partition_broadcast(128), bfg)
        cnt = 0
        Xdst, bXdst = (self.Xalt, self.bXalt) if perm else (self.X, self.bX)
        for (t0, g) in self.tiles():
            mv = self.modv[g]
            for s4 in range(4):
                r0 = t0 + s4 * 128
                S.dma("sp", xt[:, s4, :], self.X[r0:r0 + 128, :], bxt[s4], [self.bX])
                self.norm_tile(xt[:, s4, :], bxt[s4], ht[0], bht[0], mv[:, 4, :], mv[:, 3, :], [self.bmod[g]], junk, bj, st2[0], bst[0])
                self.to_kmajor(ht[0], bht[0], 8, hT, bhT, s4 * 128)
            for jp in range(NJ // 2):
                s = jp % 2
                S.dma("pool", w13[0][s], W1[:, :, jp * 256:(jp + 1) * 256], bw13[0][s])
                S.dma("pool", w13[1][s], W3[:, :, jp * 256:(jp + 1) * 256], bw13[1][s])
                for jj in range(2):
                    j = jp * 2 + jj
                    bka, bkb = self.next_bank(), self.next_bank()
                    for m, bk in ((0, bka), (1, bkb)):
                        for kc in range(8):
                            S.op("pe", lambda e, kc=kc, bk=bk, m=m, s=s, jj=jj: e.matmul(
                                self.banks[bk][:], lhsT=w13[m][s][:, kc, jj * 128:(jj + 1) * 128], rhs=hT[:, kc, :],
                                start=(kc == 0), stop=(kc == 7)), [bw13[m][s], bhT], [self.bbank[bk]])
                    q = cnt % 2
                    cnt += 1
                    S.op("act", lambda e, q=q, bka=bka: e.activation(out=sg[q], in_=self.banks[bka][:], func=AF.Silu),
                         [self.bbank[bka]], [bsg[q]])
                    S.op("dve", lambda e, q=q, bkb=bkb, j=j: e.tensor_tensor(out=uT[:, j, :], in0=sg[q], in1=self.banks[bkb][:],
                                                                           op=ALU.mult), [bsg[q], self.bbank[bkb]], [buT])
            for nq in range(4):
                S.dma("pool", w2, W2[:, :, nq * 256:(nq + 1) * 256], bw2)
                for s4 in range(4):
                    bk = self.next_bank()
                    for j in range(NJ):
                        S.op("pe", lambda e, j=j, bk=bk, s4=s4: e.matmul(
                            self.banks[bk][:, 0:256], lhsT=uT[:, j, s4 * 128:(s4 + 1) * 128], rhs=w2[:, j, :],
                            start=(j == 0), stop=(j == NJ - 1)), [buT, bw2], [self.bbank[bk]])
                    q = cnt % 2
                    cnt += 1
                    S.op("dve", lambda e, q=q, bk=bk, nq=nq, mv=mv: e.tensor_tensor(
                        out=sg[q][:, 0:256], in0=self.banks[bk][:, 0:256], in1=mv[:, 5, nq * 256:(nq + 1) * 256], op=ALU.mult),
                        [self.bbank[bk], self.bmod[g]], [bsg[q]])
                    S.op("pool", lambda e, q=q, s4=s4, nq=nq: e.tensor_tensor(
                        out=xt[:, s4, nq * 256:(nq + 1) * 256], in0=xt[:, s4, nq * 256:(nq + 1) * 256], in1=sg[q][:, 0:256],
                        op=ALU.add), [bsg[q], bxt[s4]], [bxt[s4]])
            for s4 in range(4):
                r0 = t0 + s4 * 128
                if not final:
                    self.store_rows(Xdst, bXdst, r0, xt[:, s4, :], bxt[s4], g, perm)
                else:
                    self.norm_tile(xt[:, s4, :], bxt[s4], ht[0], bht[0], fg, None, [bfg], junk, bj, st2[0], bst[0])
                    self.store_rows(self.yout, self.bYO, r0, ht[0], bht[0], g, perm)
        if perm and not final:
            self.X, self.Xalt, self.bX, self.bXalt = self.Xalt, self.X, self.bXalt, self.bX

    def finish(self):
        self.S.barrier()
        self.S.finish("sp", [self.bYO])
        self.S.emit()
        self.st.close()


WSHAPES = dict(
    mod_w=[4, 1024, 6144], mod_b=[4, 6144], norm1_g=[4, 1024], norm2_g=[4, 1024],
    ffn_w1=[4, 1024, 2816], ffn_w3=[4, 1024, 2816], ffn_w2=[4, 2816, 1024], final_g=[1024],
    ev_in_w=[2, 1024, 3600], ev_conv_w=[2, 3, 2560], ev_conv_b=[2, 2560],
    hy_w1=[2, 33, 64], hy_b1=[2, 64], hy_w2=[2, 64, 64], hy_b2=[2, 64], hy_w3=[2, 64, 2048], hy_freq=[2, 64],
    hy_bias=[2, 2, 512], m_gate_b=[2, 16], m_norm_g=[2, 512], ev_out_w=[2, 1024, 1024],
    ev_cw=[2, 128, 80], od_cw=[2, 128, 128],
    od_in_w=[2, 1024, 6208], od_conv_w=[2, 3, 4096], od_conv_b=[2, 4096], ssd_dt_bias=[2, 2, 32],
    ssd_A_log=[2, 2, 32], ssd_D=[2, 32], ssd_norm_g=[2, 2048], od_out_w=[2, 2048, 1024],
)


def phase_prep(self, l):
    S, ar, W = self.S, self.ar, self.W
    S.barrier()
    ar.reset()
    even = (l % 2 == 0)
    if even:
        cw, cb_ = W["ev_conv_w"][l // 2], W["ev_conv_b"][l // 2]
        plan = [(b, 128, b, b >= 12, (128 ** -0.5 if b >= 16 else 1.0), (b < 12 or b >= 16)) for b in range(20)]
        plan += [(b, 128, None, False, 1.0, True) for b in range(20, 28)]
    else:
        cw, cb_ = W["od_conv_w"][l // 2], W["od_conv_b"][l // 2]
        plan = [(b, 128, None, False, 1.0, True) for b in range(16)]
        plan += [(b, 128, b - 16, True, 1.0, b < 40) for b in range(16, 48)]
    Lmax = max(s[1] for s in self.seqs)
    xp = [ar.alloc([Lmax + 2]) for _ in range(2)]
    bxp = [self.B("xp", True) for _ in range(2)]
    acc = [ar.alloc([Lmax]) for _ in range(2)]
    bacc = [self.B("acc") for _ in range(2)]
    tm = [ar.alloc([Lmax // 128, 128], F32R) for _ in range(2)]
    btm = [self.B("tm") for _ in range(2)]
    ncb = 20 if even else 32
    cwt = ar.alloc([ncb, 4])
    bcw = self.B("cwt", True)
    S.dma("sp", cwt, (W["ev_cw"] if even else W["od_cw"])[l // 2].rearrange("p (b f) -> p b f", f=4), bcw)
    cnt = 0
    for (blk, rows, cbi, silu, scale, want_tm) in plan:
        for (s0, L, g) in self.seqs:
            i = cnt % 2
            cnt += 1
            x = xp[i]
            S.op("pool", lambda e, x=x, L=L: e.memset(x[:, 0:1], 0.0), [], [bxp[i]])
            S.op("pool", lambda e, x=x, L=L: e.memset(x[:, L + 1:L + 2], 0.0), [], [bxp[i]])
            S.dma("sp", x[:, 1:L + 1], self.PF[blk * 128:(blk + 1) * 128, s0:s0 + L], bxp[i], [self.bPF])
            src, bsrc = x[:, 1:L + 1], bxp[i]
            if cbi is not None:
                a = acc[i][:, 0:L]
                S.op("act", lambda e, x=x, a=a, L=L, cbi=cbi: e.activation(out=a, in_=x[:, 1:L + 1], func=AF.Identity,
                                                                       scale=cwt[:, cbi, 1:2], bias=cwt[:, cbi, 3:4]),
                     [bxp[i], bcw], [bacc[i]])
                S.op("dve", lambda e, x=x, a=a, L=L, cbi=cbi: e.scalar_tensor_tensor(out=a, in0=x[:, 0:L], scalar=cwt[:, cbi, 0:1],
                                                                                  in1=a, op0=ALU.mult, op1=ALU.add),
                     [bxp[i], bcw, bacc[i]], [bacc[i]])
                S.op("dve", lambda e, x=x, a=a, L=L, cbi=cbi: e.scalar_tensor_tensor(out=a, in0=x[:, 2:L + 2], scalar=cwt[:, cbi, 2:3],
                                                                                  in1=a, op0=ALU.mult, op1=ALU.add),
                     [bxp[i], bcw, bacc[i]], [bacc[i]])
                if silu:
                    S.op("act", lambda e, a=a: e.activation(out=a, in_=a, func=AF.Silu), [bacc[i]], [bacc[i]])
                if scale != 1.0:
                    S.op("pool", lambda e, a=a, scale=scale: e.tensor_scalar(out=a, in0=a, scalar1=scale, scalar2=None, op0=ALU.mult),
                         [bacc[i]], [bacc[i]])
                S.dma("sp", self.PF[blk * 128:(blk + 1) * 128, s0:s0 + L], a, self.bPF, [bacc[i]])
                src, bsrc = a, bacc[i]
            if want_tm:
                nt = L // 128
                for c0 in range(0, nt, 4):
                    bk = self.next_bank()
                    n4 = min(4, nt - c0)
                    for j in range(n4):
                        c = c0 + j
                        S.op("pe", lambda e, j=j, c=c, bk=bk, src=src: e.transpose(out=self.banks[bk][:, j * 128:(j + 1) * 128],
                                                                                 in_=src[:, c * 128:(c + 1) * 128], identity=self.ident),
                             [bsrc, self.bconst], [self.bbank[bk]])
                    self.copy(self.evac_eng(), tm[i][:, c0:c0 + n4, :],
                              self.banks[bk][:, 0:n4 * 128].rearrange("p (a b) -> p a b", a=n4), [self.bbank[bk]], [btm[i]])
                S.dma("sp", self.PT[s0:s0 + L, blk * 128:(blk + 1) * 128].rearrange("(a p) c -> p a c", p=128),
                      tm[i][:, 0:nt, :].bitcast(F32), self.bPT, [btm[i]])


MK.phase_prep = phase_prep


NEG = -30000.0


def scan_rows_prep(self, rho, kap, nrow, L, cols, bcols, r1, r2, brow, sdecT, extra=None, rowsets=None, rp_init=None, brp=None):
    S, ar = self.S, self.ar
    nc_ = L // 128
    h2 = nrow // 2
    RP = ar.alloc([nc_])
    RE = ar.alloc([nc_])
    bR = self.B("RPRE")
    v3 = lambda t: t[0:nrow, 0:L].rearrange("p (c t) -> p c t", t=128)
    S.op("pool", lambda e: e.memset(RP[0:nrow, :], 0.0), [], [bR])
    if rowsets is None:
        rowsets = [(0, nrow, 0)]
    if rp_init is not None:
        S.op("dve", lambda e: e.tensor_copy(out=RP[0:h2, 0:1], in_=rp_init[0:h2, 0:1]), [bR, brp], [bR])
        S.op("dve", lambda e: e.tensor_copy(out=RP[h2:nrow, nc_ - 1:nc_], in_=rp_init[h2:nrow, 0:1]), [bR, brp], [bR])
    if nc_ > 1:
        S.op("dve", lambda e: e.tensor_copy(out=RP[0:h2, 1:nc_], in_=v3(rho)[0:h2, 0:nc_ - 1, 127]), [brow, bR], [bR])
        S.op("dve", lambda e: e.tensor_copy(out=RP[h2:nrow, 0:nc_ - 1], in_=v3(rho)[h2:nrow, 1:nc_, 0]), [brow, bR], [bR])
    S.op("dve", lambda e: e.tensor_copy(out=RE[0:h2, :], in_=v3(rho)[0:h2, :, 127]), [brow, bR], [bR])
    S.op("dve", lambda e: e.tensor_copy(out=RE[h2:nrow, :], in_=v3(rho)[h2:nrow, :, 0]), [brow, bR], [bR])

    def to_cols(src, q):
        for (lo, hi, cb) in rowsets:
            nr = hi - lo
            for c0 in range(0, nc_, 8):
                n8 = min(8, nc_ - c0)
                bk = self.next_bank()
                for j in range(n8):
                    c = c0 + j
                    S.op("pe", lambda e, j=j, c=c, bk=bk, lo=lo, hi=hi, nr=nr: e.transpose(
                        out=self.banks[bk][:, j * 64:j * 64 + nr], in_=src[lo:hi, c * 128:(c + 1) * 128], identity=self.ident[lo:hi, lo:hi]),
                         [brow, self.bconst], [self.bbank[bk]])
                self.copy(self.evac_eng(), cols[:, c0:c0 + n8, q, cb:cb + nr],
                          self.banks[bk][:, 0:n8 * 64].rearrange("p (a b) -> p a b", a=n8)[:, :, 0:nr], [self.bbank[bk]], [bcols])
    to_cols(kap, 0)
    S.op("dve", lambda e: e.tensor_tensor(out=v3(r1), in0=v3(rho), in1=RP[0:nrow, :].unsqueeze(2).to_broadcast([nrow, nc_, 128]),
                                          op=ALU.subtract), [brow, bR], [brow])
    S.op("act", lambda e: e.activation(out=r1[0:nrow, 0:L], in_=r1[0:nrow, 0:L], func=AF.Exp), [brow], [brow])
    to_cols(r1, 1)
    S.op("dve", lambda e: e.tensor_tensor(out=v3(r1), in0=v3(kap), in1=RE[0:nrow, :].unsqueeze(2).to_broadcast([nrow, nc_, 128]),
                                          op=ALU.add), [brow, bR], [brow])
    S.op("act", lambda e: e.activation(out=r1[0:nrow, 0:L], in_=r1[0:nrow, 0:L], func=AF.Exp), [brow], [brow])
    to_cols(r1, 2)
    if extra is not None:
        to_cols(extra, 3)
    S.op("dve", lambda e: e.tensor_tensor(out=sdecT[0:nrow, 0:nc_], in0=RE[0:nrow, :], in1=RP[0:nrow, :], op=ALU.subtract), [bR], [bR])
    S.op("act", lambda e: e.activation(out=sdecT[0:nrow, 0:nc_], in_=sdecT[0:nrow, 0:nc_], func=AF.Exp), [bR], [bR])
    return bR


def bcast_rows(self, srcT, bsrc, nrow, ncol, dst, bdst, rows=None):
    S = self.S
    per = max(1, 512 // ncol)
    if rows is None:
        rows = list(range(nrow))
    for r0 in range(0, len(rows), per):
        n = min(per, len(rows) - r0)
        bk = self.next_bank()
        for j in range(n):
            r = rows[r0 + j]
            S.op("pe", lambda e, j=j, r=r, bk=bk: e.matmul(self.banks[bk][:, j * ncol:(j + 1) * ncol],
                                                         lhsT=self.sel[0:nrow, r, :], rhs=srcT[0:nrow, 0:ncol], start=True, stop=True),
                 [bsrc, self.bconst], [self.bbank[bk]])
        self.copy(self.evac_eng(), dst[:, r0:r0 + n, 0:ncol], self.banks[bk][:, 0:n * ncol].rearrange("p (a b) -> p a b", a=n),
                  [self.bbank[bk]], [bdst])


def scan_unit(self, d, c, GT_bank, rho, brow, row, cols, bcols, colidx, Xr, bX, CTc, bCT, Btm_c, bB, S32, Sr, bS, SDcol, bSD,
              dv, yout_ap, byout, accumulate, w):
    S = self.S
    bkD = self.next_bank()
    S.op("pe", lambda e: e.matmul(self.banks[bkD][:, 0:128], lhsT=self.sel[0:self.selrows, row, :], rhs=rho[0:self.selrows, c * 128:(c + 1) * 128],
                                  start=True, stop=False), [brow, self.bconst], [self.bbank[bkD]])
    S.op("pe", lambda e: e.matmul(self.banks[bkD][:, 0:128], lhsT=self.ident_bf, rhs=self.negmask_bf[:, d, :], start=False, stop=True),
         [self.bconst], [self.bbank[bkD]])
    i = w["i"] = (w.get("i", 0) + 1) % 2
    E, bE = w["E"][i], w["bE"][i]
    ST, bST = w["ST"][i], w["bST"][i]
    Aw, bAw = w["Aw"][i], w["bAw"][i]
    tmp, btmp = w["tmp"][i], w["btmp"][i]
    S.op("act", lambda e: e.activation(out=E, in_=self.banks[bkD][:, 0:128], func=AF.Exp, bias=cols[:, c, 0, colidx:colidx + 1]),
         [self.bbank[bkD], bcols], [bE])
    S.op("dve", lambda e: e.tensor_tensor(out=ST, in0=GT_bank[0], in1=E, op=ALU.mult), [GT_bank[1], bE], [bST])
    bkO = self.next_bank()
    S.op("pe", lambda e: e.matmul(self.banks[bkO][:, 0:dv], lhsT=ST, rhs=Xr, start=True, stop=True), [bST, bX], [self.bbank[bkO]])
    S.op("pe", lambda e: e.matmul(self.banks[bkO][:, 256:256 + dv], lhsT=CTc, rhs=Sr, start=True, stop=True), [bCT, bS], [self.bbank[bkO]])
    S.op("act", lambda e: e.activation(out=tmp[:, 0:dv], in_=self.banks[bkO][:, 0:dv], func=AF.Copy), [self.bbank[bkO]], [btmp])
    if not accumulate:
        S.op("dve", lambda e: e.scalar_tensor_tensor(out=yout_ap, in0=self.banks[bkO][:, 256:256 + dv], scalar=cols[:, c, 1, colidx:colidx + 1],
                                                     in1=tmp[:, 0:dv], op0=ALU.mult, op1=ALU.add), [self.bbank[bkO], bcols, btmp], [byout])
    else:
        S.op("dve", lambda e: e.scalar_tensor_tensor(out=tmp[:, 0:dv], in0=self.banks[bkO][:, 256:256 + dv], scalar=cols[:, c, 1, colidx:colidx + 1],
                                                     in1=tmp[:, 0:dv], op0=ALU.mult, op1=ALU.add), [self.bbank[bkO], bcols, btmp], [btmp])
        S.op("pool", lambda e: e.tensor_tensor(out=yout_ap, in0=yout_ap, in1=tmp[:, 0:dv], op=ALU.add), [btmp, byout], [byout])
    S.op("pool", lambda e: e.tensor_scalar(out=Aw, in0=Btm_c, scalar1=cols[:, c, 2, colidx:colidx + 1], scalar2=None, op0=ALU.mult),
         [bB, bcols], [bAw])
    bkS = self.next_bank()
    S.op("pe", lambda e: e.matmul(self.banks[bkS][:, 0:dv], lhsT=Aw, rhs=Xr, start=True, stop=True), [bAw, bX], [self.bbank[bkS]])
    S.op("dve", lambda e: e.scalar_tensor_tensor(out=S32, in0=S32, scalar=SDcol, in1=self.banks[bkS][:, 0:dv], op0=ALU.mult, op1=ALU.add),
         [bS, bSD, self.bbank[bkS]], [bS])
    S.op("act", lambda e: e.activation(out=Sr, in_=S32, func=AF.Copy), [bS], [bS])


def unit_work(self):
    ar = self.ar
    return dict(E=[ar.alloc([128]) for _ in range(2)], bE=[self.B("E") for _ in range(2)],
                ST=[ar.alloc([128], F32R) for _ in range(2)], bST=[self.B("ST") for _ in range(2)],
                Aw=[ar.alloc([128], F32R) for _ in range(2)], bAw=[self.B("Aw") for _ in range(2)],
                tmp=[ar.alloc([192]) for _ in range(2)], btmp=[self.B("tmp") for _ in range(2)])


def phase_ssd(self, l):
    S, ar, W = self.S, self.ar, self.W
    o = l // 2
    S.barrier()
    ar.reset()
    self.selrows = 64
    self.nrot = 6
    Lmax = max(s[1] for s in self.seqs)
    ncm = Lmax // 128
    cst = ar.alloc([4])
    bcst = self.B("cst", True)
    S.dma("sp", cst[0:64, 0:1], W["ssd_dt_bias"][o].rearrange("d (h one) -> (d h) one", one=1), bcst)
    S.dma("sp", cst[0:64, 1:2], W["ssd_A_log"][o].rearrange("d (h one) -> (d h) one", one=1), bcst)
    S.op("act", lambda e: e.activation(out=cst[0:64, 2:3], in_=cst[0:64, 1:2], func=AF.Exp), [bcst], [bcst])
    S.op("dve", lambda e: e.tensor_scalar(out=cst[0:64, 2:3], in0=cst[0:64, 2:3], scalar1=-1.0, scalar2=None, op0=ALU.mult), [bcst], [bcst])
    Dt = ar.alloc([32])
    S.dma("sp", Dt, W["ssd_D"][o].partition_broadcast(128), bcst)
    rho = ar.alloc([Lmax])
    brow = self.B("rows", True)
    cols = ar.alloc([ncm, 3, 64])
    bcols = self.B("cols")
    sdecT = ar.alloc([ncm])
    SD = ar.alloc([8, ncm])
    bSD = self.B("SD")
    w = batch_work(self, 8, 64)
    S32 = [ar.alloc([256]) for _ in range(2)]
    Sr = [ar.alloc([256], F32R) for _ in range(2)]
    bS = [[self.B("S") for _ in range(4)] for _ in range(2)]
    stg = [ar.alloc([128]) for _ in range(2)]
    bstg = [self.B("stg", True) for _ in range(2)]
    BT = ar.alloc([Lmax], F32R)
    CT = ar.alloc([Lmax], F32R)
    Btm = ar.alloc([ncm, 128], F32R)
    Xtm = ar.alloc([ncm, 256], F32R)
    bBT, bCT, bBtm, bXtm = self.B("BT", True), self.B("CT", True), self.B("Btm", True), self.B("Xtm", True)
    mark = ar.off
    for si, (s0, L, g) in enumerate(self.seqs):
        nc_ = L // 128
        S.barrier()
        ar.off = mark
        r1 = ar.alloc([Lmax])
        r2 = ar.alloc([Lmax])
        S.dma("sp", r1[0:64, 0:L], self.PF[48 * 128:48 * 128 + 64, s0:s0 + L], brow, [self.bPF])
        S.op("act", lambda e, L=L: e.activation(out=r1[0:64, 0:L], in_=r1[0:64, 0:L], func=AF.Exp, bias=cst[0:64, 0:1]), [brow, bcst], [brow])
        S.op("act", lambda e, L=L: e.activation(out=r1[0:64, 0:L], in_=r1[0:64, 0:L], func=AF.Ln, bias=1.0), [brow], [brow])
        S.op("dve", lambda e, L=L: e.tensor_scalar(out=r2[0:64, 0:L], in0=r1[0:64, 0:L], scalar1=cst[0:64, 2:3], scalar2=None, op0=ALU.mult),
             [brow, bcst], [brow])
        S.op("dve", lambda e, L=L: e.tensor_tensor_scan(out=rho[0:64, 0:L], data0=self.ones_row[0:64, 0:L], data1=r2[0:64, 0:L], initial=0.0,
                                                        op0=ALU.mult, op1=ALU.add), [brow, self.bconst], [brow])
        S.op("dve", lambda e, L=L: e.tensor_copy(out=cst[32:64, 3:4], in_=rho[32:64, L - 1:L]), [brow, bcst], [bcst])
        S.op("dve", lambda e, L=L: e.tensor_scalar(out=rho[32:64, 0:L], in0=rho[32:64, 0:L], scalar1=-1.0, scalar2=cst[32:64, 3:4],
                                                   op0=ALU.mult, op1=ALU.add), [brow, bcst], [brow])
        S.op("dve", lambda e, L=L: e.tensor_tensor(out=rho[32:64, 0:L], in0=rho[32:64, 0:L], in1=r2[32:64, 0:L], op=ALU.add), [brow], [brow])
        S.op("act", lambda e, L=L: e.activation(out=r2[0:64, 0:L], in_=r1[0:64, 0:L], func=AF.Ln), [brow], [brow])
        S.op("dve", lambda e, L=L: e.tensor_tensor(out=r2[0:64, 0:L], in0=r2[0:64, 0:L], in1=rho[0:64, 0:L], op=ALU.subtract), [brow], [brow])
        bR = scan_rows_prep(self, rho, r2, 64, L, cols, bcols, r1, r2, brow, sdecT)
        S.barrier()
        ar.off = mark
        yacc = ar.alloc([ncm, 256])
        byacc = self.B("yacc")
        for gq in range(8):
            bcast_rows(self, sdecT, bR, 64, nc_, SD, bSD, rows=[d * 32 + gq * 4 + r for d in range(2) for r in range(4)])
            S.dma("pool", BT[:, 0:L], self.PF[(32 + gq) * 128:(33 + gq) * 128, s0:s0 + L], bBT, [self.bPF])
            S.dma("pool", CT[:, 0:L], self.PF[(40 + gq) * 128:(41 + gq) * 128, s0:s0 + L], bCT, [self.bPF])
            S.dma("pool", Btm[:, 0:nc_, :], self.PT[s0:s0 + L, (32 + gq) * 128:(33 + gq) * 128].rearrange("(c p) k -> p c k", p=128), bBtm, [self.bPT])
            S.dma("pool", Xtm[:, 0:nc_, :], self.PT[s0:s0 + L, (16 + 2 * gq) * 128:(18 + 2 * gq) * 128].rearrange("(c p) k -> p c k", p=128), bXtm, [self.bPT])
            for d in range(2):
                if g == 0:
                    for pr in range(2):
                        row0 = ((o * 2 + d) * 32 + gq * 4 + pr * 2) * 64
                        S.dma("sp", stg[pr], self.st_ssd[row0:row0 + 128, :], bstg[pr])
                        bk = self.next_bank()
                        S.op("pe", lambda e, pr=pr, bk=bk: e.transpose(out=self.banks[bk][:, 0:128], in_=stg[pr], identity=self.ident),
                             [bstg[pr], self.bconst], [self.bbank[bk]])
                        for hh in range(2):
                            r = pr * 2 + hh
                            S.op("dve", lambda e, bk=bk, hh=hh, r=r, d=d: e.tensor_copy(out=S32[d][:, r * 64:(r + 1) * 64], in_=self.banks[bk][:, hh * 64:(hh + 1) * 64]),
                                 [self.bbank[bk]], [bS[d][r]])
                            S.op("act", lambda e, bk=bk, hh=hh, r=r, d=d: e.activation(out=Sr[d][:, r * 64:(r + 1) * 64], in_=self.banks[bk][:, hh * 64:(hh + 1) * 64], func=AF.Copy),
                                 [self.bbank[bk]], [bS[d][r]])
                else:
                    for r in range(4):
                        S.op("pool", lambda e, r=r, d=d: e.memset(S32[d][:, r * 64:(r + 1) * 64], 0.0), [], [bS[d][r]])
                        S.op("dve", lambda e, r=r, d=d: e.tensor_copy(out=Sr[d][:, r * 64:(r + 1) * 64], in_=S32[d][:, r * 64:(r + 1) * 64]), [bS[d][r]], [bS[d][r]])
            visited = set()
            for k in range(nc_):
                U = []
                for d, c in ((0, k), (1, nc_ - 1 - k)):
                    S.op("pe", lambda e, c=c, d=d: e.matmul(self.banks[6][:, d * 128:(d + 1) * 128], lhsT=BT[:, c * 128:(c + 1) * 128], rhs=CT[:, c * 128:(c + 1) * 128],
                                                          start=True, stop=True), [bBT, bCT], [self.bbank[6]])
                    acc = c in visited
                    visited.add(c)
                    for r in range(4):
                        dh = d * 32 + gq * 4 + r
                        U.append(dict(j=d * 4 + r, d=d, c=c, row=dh, ci=dh, rho=rho, brow=brow, cols=cols, bcols=bcols,
                                      D=(d, r * 128), G=(6, d * 128), O=(2, (d * 4 + r) * 64, 3, (d * 4 + r) * 64), Sb=(4, (d * 4 + r) * 64), dv=64,
                                      Xr=Xtm[:, c, r * 64:(r + 1) * 64], bX=bXtm, CTc=CT[:, c * 128:(c + 1) * 128], bCT=bCT, Btm=Btm[:, c, :], bB=bBtm,
                                      S32=S32[d][:, r * 64:(r + 1) * 64], Sr=Sr[d][:, r * 64:(r + 1) * 64], bS=bS[d][r], SDcol=SD[:, d * 4 + r, c:c + 1], bSD=bSD,
                                      yout=yacc[:, c, r * 64:(r + 1) * 64], byout=byacc, acc=acc))
                scan_step(self, U, w)
            for d in range(2):
                if g == 1:
                    pi = si - 1
                    for pr in range(2):
                        bk = self.next_bank()
                        S.op("pe", lambda e, pr=pr, bk=bk, d=d: e.transpose(out=self.banks[bk][:, 0:128], in_=S32[d][:, pr * 128:(pr + 1) * 128], identity=self.ident),
                             [bS[d][pr * 2], bS[d][pr * 2 + 1], self.bconst], [self.bbank[bk]])
                        self.copy("dve", stg[pr], self.banks[bk][:, 0:128], [self.bbank[bk]], [bstg[pr]])
                        row0 = (((pi * 2 + o) * 2 + d) * 32 + gq * 4 + pr * 2) * 64
                        S.dma("sp", self.o_ssd[row0:row0 + 128, :], stg[pr], self.bso, [bstg[pr]])
            for r in range(4):
                h = gq * 4 + r
                S.op("dve", lambda e, r=r, h=h, nc_=nc_: e.scalar_tensor_tensor(out=yacc[:, 0:nc_, r * 64:(r + 1) * 64], in0=Xtm[:, 0:nc_, r * 64:(r + 1) * 64].bitcast(F32),
                                                                           scalar=Dt[:, h:h + 1], in1=yacc[:, 0:nc_, r * 64:(r + 1) * 64], op0=ALU.mult, op1=ALU.add),
                     [bXtm, bcst, byacc], [byacc])
            S.dma("sp", self.YS[s0:s0 + L, gq * 256:(gq + 1) * 256].rearrange("(c p) k -> p c k", p=128), yacc[:, 0:nc_, :], self.bYS, [byacc])
    S.barrier()
    ar.reset()
    self.nrot = 8
    ng = ar.alloc([2048])
    bng = self.B("ng", True)
    S.dma("sp", ng, W["ssd_norm_g"][o].partition_broadcast(128), bng)
    ys = [ar.alloc([2048]) for _ in range(2)]
    zs = [ar.alloc([2048]) for _ in range(2)]
    bys = [self.B("ys", True) for _ in range(2)]
    bzs = [self.B("zs", True) for _ in range(2)]
    junk = ar.alloc([2048])
    bj = self.B("junk")
    st2 = [ar.alloc([4]) for _ in range(2)]
    bst = [self.B("st") for _ in range(2)]
    for it in range(self.T // 128):
        i = it % 2
        r0 = it * 128
        S.dma("sp", ys[i], self.YS[r0:r0 + 128, :], bys[i], [self.bYS])
        S.dma("sp", zs[i], self.PT[r0:r0 + 128, 0:2048], bzs[i], [self.bPT])
        S.op("act", lambda e, i=i: e.activation(out=zs[i], in_=zs[i], func=AF.Silu), [bzs[i]], [bzs[i]])
        S.op("dve", lambda e, i=i: e.tensor_tensor(out=ys[i], in0=ys[i], in1=zs[i], op=ALU.mult), [bys[i], bzs[i]], [bys[i]])
        S.op("act", lambda e, i=i: e.activation(out=junk, in_=ys[i], func=AF.Square, accum_out=st2[i][:, 0:1]), [bys[i]], [bj, bst[i]])
        S.op("dve", lambda e, i=i: e.tensor_scalar(out=st2[i][:, 1:2], in0=st2[i][:, 0:1], scalar1=1.0 / 2048, scalar2=EPS, op0=ALU.mult, op1=ALU.add), [bst[i]], [bst[i]])
        S.op("act", lambda e, i=i: e.activation(out=st2[i][:, 3:4], in_=st2[i][:, 1:2], func=AF.Sqrt), [bst[i]], [bst[i]])
        S.op("dve", lambda e, i=i: e.reciprocal(out=st2[i][:, 2:3], in_=st2[i][:, 3:4]), [bst[i]], [bst[i]])
        S.op("dve", lambda e, i=i: e.scalar_tensor_tensor(out=ys[i], in0=ys[i], scalar=st2[i][:, 2:3], in1=ng, op0=ALU.mult, op1=ALU.mult),
             [bys[i], bst[i], bng], [bys[i]])
        S.dma("sp", self.YT[r0:r0 + 128, :], ys[i], self.bYT, [bys[i]])


MK.phase_ssd = phase_ssd


def phase_mlstm(self, l):
    S, ar, W = self.S, self.ar, self.W
    e_ = l // 2
    S.barrier()
    ar.reset()
    self.selrows = 64
    self.nrot = 6
    Lmax = max(s[1] for s in self.seqs)
    ncm = Lmax // 128
    G0 = 28 * 128
    cst = ar.alloc([8])
    bcst = self.B("cst", True)
    S.op("pool", lambda e: e.memset(cst, 0.0), [], [bcst])
    gbv = W["m_gate_b"][e_].rearrange("(q one) -> q one", one=1)
    for d in range(2):
        S.dma("sp", cst[d * 32:d * 32 + 4, 0:1], gbv[d * 8:d * 8 + 4, :], bcst)
        S.dma("sp", cst[d * 32:d * 32 + 4, 1:2], gbv[d * 8 + 4:d * 8 + 8, :], bcst)
    S.op("dve", lambda e: e.tensor_scalar(out=cst[0:64, 1:2], in0=cst[0:64, 1:2], scalar1=-1.0, scalar2=None, op0=ALU.mult), [bcst], [bcst])
    ngm = ar.alloc([512])
    S.dma("sp", ngm, W["m_norm_g"][e_].partition_broadcast(128), bcst)
    rho = ar.alloc([Lmax])
    brow = self.B("rows", True)
    cols = ar.alloc([ncm, 4, 8])
    bcols = self.B("cols")
    sdecT = ar.alloc([ncm])
    SD = ar.alloc([64, ncm])
    bSD = self.B("SD")
    w = unit_work(self)
    S32 = [ar.alloc([130]) for _ in range(2)]
    Sr = [ar.alloc([130], F32R) for _ in range(2)]
    bS = [self.B("S", True) for _ in range(2)]
    KT = ar.alloc([Lmax], F32R)
    QT = ar.alloc([Lmax], F32R)
    Ktm = ar.alloc([ncm, 128], F32R)
    Vtm = ar.alloc([ncm, 130], F32R)
    bKT, bQT, bKtm, bVtm = self.B("KT", True), self.B("QT", True), self.B("Ktm", True), self.B("Vtm", True)
    tot = [ar.alloc([136]) for _ in range(2)]
    btot = [self.B("tot") for _ in range(2)]
    mark = ar.off
    rowsets = [(0, 4, 0), (32, 36, 4)]
    cnt = 0
    for si, (s0, L, g) in enumerate(self.seqs):
        nc_ = L // 128
        S.barrier()
        ar.off = mark
        r1 = ar.alloc([Lmax])
        r2 = ar.alloc([Lmax])
        for t in (rho, r1, r2):
            S.op("pool", lambda e, t=t, L=L: e.memset(t[0:64, 0:L], 0.0), [], [brow])
        S.op("pool", lambda e: e.memset(cst[0:64, 2:3], 0.0), [], [bcst])
        if g == 0:
            for d in range(2):
                S.dma("sp", cst[d * 32:d * 32 + 4, 2:3], self.st_m[0:1, (e_ * 2 + d) * 4:(e_ * 2 + d) * 4 + 4].rearrange("one q -> q one"), bcst)
        S.op("dve", lambda e: e.tensor_scalar(out=cst[0:64, 3:4], in0=cst[0:64, 2:3], scalar1=-1.0, scalar2=None, op0=ALU.mult), [bcst], [bcst])
        for d in range(2):
            S.dma("sp", r1[d * 32:d * 32 + 4, 0:L], self.PF[G0 + d * 8:G0 + d * 8 + 4, s0:s0 + L], brow, [self.bPF])
            S.dma("sp", r2[d * 32:d * 32 + 4, 0:L], self.PF[G0 + d * 8 + 4:G0 + d * 8 + 8, s0:s0 + L], brow, [self.bPF])
        A = lambda t, L=L: t[0:64, 0:L]
        S.op("dve", lambda e, L=L, A=A: e.tensor_scalar(out=A(r1), in0=A(r1), scalar1=cst[0:64, 0:1], scalar2=None, op0=ALU.add), [brow, bcst], [brow])
        S.op("act", lambda e, L=L, A=A: e.activation(out=A(r2), in_=A(r2), func=AF.Exp, scale=-1.0, bias=cst[0:64, 1:2]), [brow, bcst], [brow])
        S.op("act", lambda e, L=L, A=A: e.activation(out=A(r2), in_=A(r2), func=AF.Ln, bias=1.0), [brow], [brow])
        S.op("dve", lambda e, L=L, A=A: e.tensor_tensor_scan(out=rho[0:32, 0:L], data0=self.ones_row[0:32, 0:L], data1=r2[0:32, 0:L], initial=0.0,
                                                        op0=ALU.mult, op1=ALU.add), [brow, self.bconst], [brow])
        S.op("dve", lambda e, L=L, A=A: e.tensor_tensor_scan(out=rho[32:64, L - 1::-1], data0=self.ones_row[32:64, 0:L], data1=r2[32:64, L - 1::-1], initial=0.0,
                                                        op0=ALU.mult, op1=ALU.add), [brow, self.bconst], [brow])
        S.op("dve", lambda e, L=L, A=A: e.tensor_scalar(out=A(r2), in0=A(rho), scalar1=-1.0, scalar2=None, op0=ALU.mult), [brow], [brow])
        S.op("dve", lambda e, L=L, A=A: e.tensor_tensor(out=A(r1), in0=A(r1), in1=A(rho), op=ALU.add), [brow], [brow])
        S.op("dve", lambda e, L=L, A=A: e.tensor_tensor_scan(out=rho[0:32, 0:L], data0=self.ones_row[0:32, 0:L], data1=r1[0:32, 0:L], initial=cst[0:32, 2:3],
                                                        op0=ALU.mult, op1=ALU.max), [brow, bcst, self.bconst], [brow])
        S.op("dve", lambda e, L=L, A=A: e.tensor_tensor_scan(out=rho[32:64, L - 1::-1], data0=self.ones_row[32:64, 0:L], data1=r1[32:64, L - 1::-1], initial=cst[32:64, 2:3],
                                                        op0=ALU.mult, op1=ALU.max), [brow, bcst, self.bconst], [brow])
        S.op("dve", lambda e, L=L, A=A: e.tensor_scalar(out=A(rho), in0=A(rho), scalar1=-1.0, scalar2=None, op0=ALU.mult), [brow], [brow])
        S.op("dve", lambda e, L=L, A=A: e.tensor_tensor(out=cst[0:32, 4:5], in0=r2[0:32, L - 1:L], in1=rho[0:32, L - 1:L], op=ALU.subtract), [brow, bcst], [bcst])
        S.op("dve", lambda e, L=L, A=A: e.tensor_tensor(out=cst[32:64, 4:5], in0=r2[32:64, 0:1], in1=rho[32:64, 0:1], op=ALU.subtract), [brow, bcst], [bcst])
        if g == 1:
            pi = si - 1
            for d in range(2):
                q0 = ((pi * 2 + e_) * 2 + d) * 4
                S.dma("sp", self.o_m[0:1, q0:q0 + 4].rearrange("one q -> q one"), cst[d * 32:d * 32 + 4, 4:5], self.bso, [bcst])
        S.op("dve", lambda e, L=L, A=A: e.tensor_tensor(out=A(r2), in0=A(rho), in1=A(r2), op=ALU.subtract), [brow], [brow])
        S.op("act", lambda e, L=L, A=A: e.activation(out=A(r2), in_=A(r2), func=AF.Exp), [brow], [brow])
        r3 = ar.alloc([Lmax])
        bR = scan_rows_prep(self, rho, r1, 64, L, cols, bcols, r3, None, brow, sdecT, extra=r2, rowsets=rowsets, rp_init=cst[:, 3:4], brp=bcst)
        bcast_rows(self, sdecT, bR, 64, nc_, SD, bSD)
        S.barrier()
        ar.off = mark
        hacc = ar.alloc([ncm, 128])
        bh = self.B("hacc")
        og = ar.alloc([ncm, 128])
        bog = self.B("og", True)
        sq = ar.alloc([ncm, 128])
        ms = ar.alloc([ncm, 2])
        bsq = self.B("sq")
        for h in range(4):
            S.dma("pool", QT[:, 0:L], self.PF[(12 + h) * 128:(13 + h) * 128, s0:s0 + L], bQT, [self.bPF])
            S.dma("pool", KT[:, 0:L], self.PF[(16 + h) * 128:(17 + h) * 128, s0:s0 + L], bKT, [self.bPF])
            S.dma("pool", Ktm[:, 0:nc_, :], self.PT[s0:s0 + L, (16 + h) * 128:(17 + h) * 128].rearrange("(c p) k -> p c k", p=128), bKtm, [self.bPT])
            S.op("act", lambda e, nc_=nc_: e.activation(out=Vtm[:, 0:nc_, 128:129], in_=self.onecol.unsqueeze(1).to_broadcast([128, nc_, 1]), func=AF.Copy),
                 [self.bconst], [bVtm])
            S.op("act", lambda e, nc_=nc_: e.activation(out=Vtm[:, 0:nc_, 129:130], in_=self.onecol.unsqueeze(1).to_broadcast([128, nc_, 1]), func=AF.Copy, scale=0.0),
                 [self.bconst], [bVtm])
            S.dma("pool", Vtm[:, 0:nc_, 0:128], self.PT[s0:s0 + L, (20 + h) * 128:(21 + h) * 128].rearrange("(c p) k -> p c k", p=128), bVtm, [self.bPT])
            S.dma("sp", og[:, 0:nc_, :], self.PT[s0:s0 + L, (24 + h) * 128:(25 + h) * 128].rearrange("(c p) k -> p c k", p=128), bog, [self.bPT])
            for d in range(2):
                S.op("pool", lambda e, d=d: e.memset(S32[d], 0.0), [], [bS[d]])
                if g == 0:
                    r0 = ((e_ * 2 + d) * 4 + h)
                    S.dma("sp", S32[d][:, 0:128], self.st_C[r0 * 128:(r0 + 1) * 128, :], bS[d])
                    S.dma("sp", S32[d][:, 128:129], self.st_n[r0:r0 + 1, :].rearrange("one q -> q one"), bS[d])
                S.op("act", lambda e, d=d: e.activation(out=Sr[d], in_=S32[d], func=AF.Copy), [bS[d]], [bS[d]])
                order = range(nc_) if d == 0 else range(nc_ - 1, -1, -1)
                for c in order:
                    bkG = self.g_bank()
                    S.op("pe", lambda e, c=c, bkG=bkG: e.matmul(self.banks[bkG][:, 0:128], lhsT=KT[:, c * 128:(c + 1) * 128], rhs=QT[:, c * 128:(c + 1) * 128],
                                                              start=True, stop=True), [bKT, bQT], [self.bbank[bkG]])
                    i = cnt % 2
                    cnt += 1
                    scan_unit(self, d, c, (self.banks[bkG][:, 0:128], self.bbank[bkG]), rho, brow, d * 32 + h, cols, bcols, d * 4 + h,
                              Vtm[:, c, :], bVtm, QT[:, c * 128:(c + 1) * 128], bQT, Ktm[:, c, :], bKtm,
                              S32[d], Sr[d], bS[d], SD[:, d * 32 + h, c:c + 1], bSD, 130, tot[i][:, 0:130], btot[i], False, w)
                    ci = d * 4 + h
                    S.op("act", lambda e, i=i: e.activation(out=tot[i][:, 132:133], in_=tot[i][:, 128:129], func=AF.Abs), [btot[i]], [btot[i]])
                    S.op("dve", lambda e, i=i, c=c, ci=ci: e.tensor_tensor(out=tot[i][:, 132:133], in0=tot[i][:, 132:133], in1=cols[:, c, 3, ci:ci + 1], op=ALU.max),
                         [btot[i], bcols], [btot[i]])
                    S.op("dve", lambda e, i=i: e.reciprocal(out=tot[i][:, 131:132], in_=tot[i][:, 132:133]), [btot[i]], [btot[i]])
                    if d == 0:
                        S.op("dve", lambda e, i=i, c=c: e.tensor_scalar(out=hacc[:, c, :], in0=tot[i][:, 0:128], scalar1=tot[i][:, 131:132], scalar2=None, op0=ALU.mult),
                             [btot[i]], [bh])
                    else:
                        S.op("dve", lambda e, i=i, c=c: e.scalar_tensor_tensor(out=hacc[:, c, :], in0=tot[i][:, 0:128], scalar=tot[i][:, 131:132], in1=hacc[:, c, :],
                                                                             op0=ALU.mult, op1=ALU.add), [btot[i], bh], [bh])
                if g == 1:
                    pi = si - 1
                    r0 = (((pi * 2 + e_) * 2 + d) * 4 + h)
                    S.dma("sp", self.o_C[r0 * 128:(r0 + 1) * 128, :], S32[d][:, 0:128], self.bso, [bS[d]])
                    S.dma("sp", self.o_n[r0:r0 + 1, :].rearrange("one q -> q one"), S32[d][:, 128:129], self.bso, [bS[d]])
            S.op("pool", lambda e, nc_=nc_: e.tensor_tensor(out=sq[:, 0:nc_, :], in0=hacc[:, 0:nc_, :], in1=hacc[:, 0:nc_, :], op=ALU.mult), [bh], [bsq])
            S.op("dve", lambda e, nc_=nc_: e.tensor_reduce(out=ms[:, 0:nc_, 0], in_=sq[:, 0:nc_, :], axis=AX.X, op=ALU.add), [bsq], [bsq])
            S.op("dve", lambda e, nc_=nc_: e.tensor_scalar(out=ms[:, 0:nc_, 0], in0=ms[:, 0:nc_, 0], scalar1=1.0 / 128, scalar2=EPS, op0=ALU.mult, op1=ALU.add), [bsq], [bsq])
            S.op("act", lambda e, nc_=nc_: e.activation(out=ms[:, 0:nc_, 1], in_=ms[:, 0:nc_, 0], func=AF.Sqrt), [bsq], [bsq])
            S.op("dve", lambda e, nc_=nc_: e.reciprocal(out=ms[:, 0:nc_, 0], in_=ms[:, 0:nc_, 1]), [bsq], [bsq])
            S.op("dve", lambda e, nc_=nc_: e.tensor_tensor(out=hacc[:, 0:nc_, :], in0=hacc[:, 0:nc_, :], in1=ms[:, 0:nc_, 0:1].to_broadcast([128, nc_, 128]), op=ALU.mult),
                 [bh, bsq], [bh])
            S.op("pool", lambda e, nc_=nc_, h=h: e.tensor_tensor(out=hacc[:, 0:nc_, :], in0=hacc[:, 0:nc_, :],
                                                               in1=ngm[:, h * 128:(h + 1) * 128].unsqueeze(1).to_broadcast([128, nc_, 128]), op=ALU.mult), [bh, bcst], [bh])
            S.op("act", lambda e, nc_=nc_: e.activation(out=og[:, 0:nc_, :], in_=og[:, 0:nc_, :], func=AF.Sigmoid), [bog], [bog])
            S.op("dve", lambda e, nc_=nc_: e.tensor_tensor(out=hacc[:, 0:nc_, :], in0=hacc[:, 0:nc_, :], in1=og[:, 0:nc_, :], op=ALU.mult), [bh, bog], [bh])
            S.dma("sp", self.YT[s0:s0 + L, 512 + h * 128:512 + (h + 1) * 128].rearrange("(c p) k -> p c k", p=128), hacc[:, 0:nc_, :], self.bYT, [bh])
    self.nrot = 8


MK.phase_mlstm = phase_mlstm


def hy_consts(self):
    self.hyc = {}
    for L in sorted(set(s[1] for s in self.seqs)):
        NB = L // 128 + 1
        self.hyc[L] = dict(
            NB=NB, CP=self.din("CP%d" % L, [NB * 128, NB * 128]), SP=self.din("SP%d" % L, [NB * 128, NB * 128]),
            featsT=self.din("featsT%d" % L, [33, L]), tcol=self.din("tcol%d" % L, [128, L // 128]),
            wcol=self.din("wcol%d" % L, [128, NB]), KS=self.dscr("KS%d" % L, [NB * 128, 2, 1024]), bKS=self.B("KS%d" % L, True, True))
    self.deltas = self.din("deltas", [512])


def sin_act(self, t, bt, tmp, n, L):
    S = self.S
    PI = math.pi
    v = t[0:n, 0:L]
    u = tmp[0:n, 0:L]
    S.op("dve", lambda e: e.tensor_single_scalar(out=u, in_=v, scalar=PI, op=ALU.is_gt), [bt], [bt])
    S.op("dve", lambda e: e.scalar_tensor_tensor(out=v, in0=u, scalar=-2 * PI, in1=v, op0=ALU.mult, op1=ALU.add), [bt], [bt])
    S.op("dve", lambda e: e.tensor_single_scalar(out=u, in_=v, scalar=-PI, op=ALU.is_lt), [bt], [bt])
    S.op("dve", lambda e: e.scalar_tensor_tensor(out=v, in0=u, scalar=2 * PI, in1=v, op0=ALU.mult, op1=ALU.add), [bt], [bt])
    S.op("act", lambda e: e.activation(out=v, in_=v, func=AF.Sin), [bt], [bt])


def dft_apply(self, hc, mats, rhs_list, nkc, nout, ncols, evac):
    S, ar = self.S, self.ar
    nm = len(mats)
    KG = 4
    NOB = 4 // nm
    mt = [[ar.alloc([KG, NOB * 128], F32R) for _ in range(2)] for _ in range(nm)]
    bmt = [[self.B("mt", True) for _ in range(2)] for _ in range(nm)]
    cnt = 0
    for ob0 in range(0, nout, NOB):
        nob = min(NOB, nout - ob0)
        acc = {(m, j): (self.next_bank(), 0) for m in range(nm) for j in range(nob)}
        for k0 in range(0, nkc, KG):
            nk = min(KG, nkc - k0)
            s = cnt % 2
            cnt += 1
            for m in range(nm):
                Mv = mats[m].rearrange("(kc p) n -> p kc n", p=128)
                S.dma("pool", mt[m][s][:, 0:nk, 0:nob * 128], Mv[:, k0:k0 + nk, ob0 * 128:(ob0 + nob) * 128], bmt[m][s])
            for m in range(nm):
                rt, brt = rhs_list[m]
                for j in range(nob):
                    bk, off = acc[(m, j)]
                    for kk in range(nk):
                        kc = k0 + kk
                        S.op("pe", lambda e, m=m, j=j, kk=kk, kc=kc, bk=bk, off=off, s=s, rt=rt: e.matmul(
                            self.banks[bk][:, off:off + ncols], lhsT=mt[m][s][:, kk, j * 128:(j + 1) * 128], rhs=rt[:, kc, 0:ncols],
                            start=(kc == 0), stop=(kc == nkc - 1), skip_group_check=True), [bmt[m][s], brt], [self.bbank[bk]])
        for j in range(nob):
            evac(ob0 + j, [self.banks[acc[(m, j)][0]][:, acc[(m, j)][1]:acc[(m, j)][1] + ncols] for m in range(nm)],
                 [self.bbank[acc[(m, j)][0]] for m in range(nm)])


def phase_hyena_filters(self, l, L):
    S, ar, W = self.S, self.ar, self.W
    e_ = l // 2
    hc = self.hyc[L]
    NB, nkc = hc["NB"], L // 128
    S.barrier()
    ar.reset()
    self.nrot = 8
    cst = ar.alloc([8])
    bcst = self.B("cst", True)
    for j, nm_ in enumerate(("hy_b1", "hy_b2", "hy_freq")):
        S.dma("sp", cst[0:64, j:j + 1], W[nm_][e_].rearrange("(q one) -> q one", one=1), bcst)
    dl = ar.alloc([512])
    S.dma("sp", dl, self.deltas.partition_broadcast(128), bcst)
    tcol = ar.alloc([nkc])
    S.dma("sp", tcol, hc["tcol"], bcst)
    S.op("dve", lambda e: e.tensor_scalar(out=tcol, in0=tcol, scalar1=-1.0, scalar2=None, op0=ALU.mult), [bcst], [bcst])
    wcol = ar.alloc([NB])
    S.dma("sp", wcol, hc["wcol"], bcst)
    ones = ar.alloc([128], F32R)
    bw = self.B("hw", True)
    S.op("act", lambda e: e.activation(out=ones, in_=self.onecol.to_broadcast([128, 128]), func=AF.Copy), [self.bconst], [bw])
    w3 = ar.alloc([2048], F32R)
    S.dma("pool", w3[0:64, :], W["hy_w3"][e_], bw)
    h2T = ar.alloc([L], F32R)
    bh = self.B("hT")
    rinv = ar.alloc([512])
    brinv = self.B("rinv")
    mark = ar.off
    fT = ar.alloc([L], F32R)
    w1 = ar.alloc([128], F32R)
    w2 = ar.alloc([128], F32R)
    S.op("act", lambda e: e.activation(out=w1, in_=self.onecol.to_broadcast([128, 128]), func=AF.Copy, scale=0.0), [self.bconst], [bw])
    S.op("act", lambda e: e.activation(out=w2, in_=self.onecol.to_broadcast([128, 128]), func=AF.Copy, scale=0.0), [self.bconst], [bw])
    h1T = ar.alloc([L], F32R)
    pre = ar.alloc([L])
    tmp = ar.alloc([L])
    bpre = self.B("pre")
    S.dma("pool", fT[0:33, :], hc["featsT"], bw)
    S.dma("pool", w1[0:33, 0:64], W["hy_w1"][e_], bw)
    S.dma("pool", w2[0:64, 0:64], W["hy_w2"][e_], bw)
    for (lhs, kk, src, bcol, dst) in ((w1, 33, fT, 0, h1T), (w2, 64, h1T, 1, h2T)):
        for t0 in range(0, L, 512):
            tw = min(512, L - t0)
            bk = self.next_bank()
            S.op("pe", lambda e, lhs=lhs, kk=kk, src=src, t0=t0, tw=tw, bk=bk: e.matmul(self.banks[bk][:, 0:tw], lhsT=lhs[0:kk, :], rhs=src[0:kk, t0:t0 + tw],
                                                                                   start=True, stop=True), [bw, bh], [self.bbank[bk]])
            S.op("dve", lambda e, t0=t0, tw=tw, bk=bk, bcol=bcol: e.tensor_scalar(out=pre[0:64, t0:t0 + tw], in0=self.banks[bk][0:64, 0:tw], scalar1=cst[0:64, bcol:bcol + 1],
                                                                               scalar2=cst[0:64, 2:3], op0=ALU.add, op1=ALU.mult), [self.bbank[bk], bcst], [bpre])
        sin_act(self, pre, bpre, tmp, 64, L)
        S.op("act", lambda e, dst=dst: e.activation(out=dst[0:64, 0:L], in_=pre[0:64, 0:L], func=AF.Copy), [bpre], [bh])
    S.barrier()
    ar.off = mark
    kbuf = ar.alloc([nkc, 512], F32R)
    bkb = self.B("kbuf")
    dec = ar.alloc([512])
    hf = ar.alloc([512])
    hb = ar.alloc([512])
    ab = ar.alloc([512], F32R)
    bwk = self.B("wk")
    og = [ar.alloc([2, 512]) for _ in range(2)]
    bog = [self.B("og") for _ in range(2)]
    for o in range(2):
        for sign, mat, slot in ((1.0, hc["CP"], 0), (-1.0, hc["SP"], 1)):
            bkn = self.next_bank()
            for c in range(nkc):
                bka, bkb_ = self.next_bank(), self.next_bank()
                if bka == bkn or bkb_ == bkn:
                    bka, bkb_ = self.next_bank(), self.next_bank()
                for d, bk in ((0, bka), (1, bkb_)):
                    c0 = (d * 2 + o) * 512
                    S.op("pe", lambda e, c=c, bk=bk, c0=c0: e.matmul(self.banks[bk][:, :], lhsT=h2T[0:64, c * 128:(c + 1) * 128], rhs=w3[0:64, c0:c0 + 512],
                                                                  start=True, stop=True), [bh, bw], [self.bbank[bk]])
                S.op("act", lambda e, c=c: e.activation(out=dec, in_=dl, func=AF.Exp, scale=tcol[:, c:c + 1]), [bcst], [bwk])
                S.op("dve", lambda e, bka=bka: e.tensor_tensor(out=hf, in0=self.banks[bka][:, :], in1=dec, op=ALU.mult), [self.bbank[bka], bwk], [bwk])
                S.op("dve", lambda e, bkb_=bkb_: e.tensor_tensor(out=hb, in0=self.banks[bkb_][:, :], in1=dec, op=ALU.mult), [self.bbank[bkb_], bwk], [bwk])
                if c == 0:
                    S.op("pool", lambda e: e.memset(hb[0:1, :], 0.0), [bwk], [bwk])
                if sign > 0:
                    S.op("pool", lambda e, c=c: e.tensor_tensor(out=kbuf[:, c, :], in0=hb, in1=hf, op=ALU.add), [bwk], [bkb])
                    S.op("act", lambda e: e.activation(out=hf, in_=hf, func=AF.Abs), [bwk], [bwk])
                    S.op("act", lambda e: e.activation(out=hb, in_=hb, func=AF.Abs), [bwk], [bwk])
                    S.op("pool", lambda e: e.tensor_tensor(out=ab, in0=hb, in1=hf, op=ALU.add), [bwk], [bwk])
                    S.op("pe", lambda e, c=c, bkn=bkn: e.matmul(self.banks[bkn][:, :], lhsT=ones, rhs=ab, start=(c == 0), stop=(c == nkc - 1), skip_group_check=True),
                         [bwk, bw], [self.bbank[bkn]])
                else:
                    S.op("pool", lambda e, c=c: e.tensor_tensor(out=kbuf[:, c, :], in0=hb, in1=hf, op=ALU.subtract), [bwk], [bkb])
            if sign > 0:
                S.op("dve", lambda e, bkn=bkn: e.reciprocal(out=rinv, in_=self.banks[bkn][:, :]), [self.bbank[bkn]], [brinv])

            def evac(ob, aps, bbs, o=o, slot=slot):
                i = ob % 2
                S.op("dve", lambda e: e.scalar_tensor_tensor(out=og[i][:, slot, :], in0=aps[0], scalar=wcol[:, ob:ob + 1], in1=rinv, op0=ALU.mult, op1=ALU.mult),
                     [bbs[0], bcst, brinv], [bog[i]])
                S.dma("sp", hc["KS"][ob * 128:(ob + 1) * 128, slot, o * 512:(o + 1) * 512], og[i][:, slot, :], hc["bKS"], [bog[i]])
            m2 = ar.off
            dft_apply(self, hc, [mat], [(kbuf, bkb)], nkc, NB, 512, evac)
            S.barrier()
            ar.off = m2


def phase_hyena(self, l):
    S, ar, W = self.S, self.ar, self.W
    e_ = l // 2
    for L in sorted(self.hyc):
        phase_hyena_filters(self, l, L)
    S.barrier()
    ar.reset()
    self.nrot = 8
    for si, (s0, L, g) in enumerate(self.seqs):
        for grp in range(2):
            hyena_group(self, e_, s0, L, grp)


def hyena_group(self, e_, s0, L, grp):
    S, ar, W = self.S, self.ar, self.W
    hc = self.hyc[L]
    NB, nkc = hc["NB"], L // 128
    if True:
        if True:
            S.barrier()
            ar.reset()
            z = ar.alloc([nkc, 256], F32R)
            bz = self.B("z", True)
            Y = ar.alloc([NB, 2, 256], F32R)
            bY = self.B("Y")
            gate = ar.alloc([nkc, 256])
            bgate = self.B("gate", True)
            bias = ar.alloc([2, 256])
            bbias = self.B("bias", True)
            for o in range(2):
                S.dma("sp", bias[:, o, :], W["hy_bias"][e_, o, grp * 256:(grp + 1) * 256].partition_broadcast(128), bbias)
            kt = [ar.alloc([2, 256]) for _ in range(2)]
            bkt = [self.B("kt", True) for _ in range(2)]
            t1 = [ar.alloc([256]) for _ in range(4)]
            bt1 = self.B("t1")
            S.dma("pool", z, self.PT[s0:s0 + L, grp * 256:(grp + 1) * 256].rearrange("(c p) k -> p c k", p=128), bz, [self.bPT])
            m2 = ar.off
            for o in range(2):
                S.dma("sp", gate, self.PT[s0:s0 + L, 512 * (o + 1) + grp * 256:512 * (o + 1) + (grp + 1) * 256].rearrange("(c p) k -> p c k", p=128),
                      bgate, [self.bPT])

                def evac_f(ob, aps, bbs, o=o, grp=grp):
                    i = ob % 2
                    S.dma("sp", kt[i], hc["KS"][ob * 128:(ob + 1) * 128, :, o * 512 + grp * 256:o * 512 + (grp + 1) * 256], bkt[i], [hc["bKS"]])
                    zre, zs = aps
                    S.op("dve", lambda e: e.tensor_tensor(out=t1[0], in0=zre, in1=kt[i][:, 0, :], op=ALU.mult), [bbs[0], bkt[i], bt1], [bt1])
                    S.op("dve", lambda e: e.tensor_tensor(out=t1[1], in0=zs, in1=kt[i][:, 1, :], op=ALU.mult), [bbs[1], bkt[i], bt1], [bt1])
                    S.op("pool", lambda e: e.tensor_tensor(out=Y[:, ob, 0, :], in0=t1[0], in1=t1[1], op=ALU.add), [bt1], [bY, bt1])
                    S.op("dve", lambda e: e.tensor_tensor(out=t1[2], in0=zs, in1=kt[i][:, 0, :], op=ALU.mult), [bbs[1], bkt[i], bt1], [bt1])
                    S.op("dve", lambda e: e.tensor_tensor(out=t1[3], in0=zre, in1=kt[i][:, 1, :], op=ALU.mult), [bbs[0], bkt[i], bt1], [bt1])
                    S.op("pool", lambda e: e.tensor_tensor(out=Y[:, ob, 1, :], in0=t1[2], in1=t1[3], op=ALU.subtract), [bt1], [bY, bt1])
                dft_apply(self, hc, [hc["CP"], hc["SP"]], [(z, bz), (z, bz)], nkc, NB, 256, evac_f)
                S.barrier()
                ar.off = m2
                Yre = Y[:, :, 0, :]
                Yim = Y[:, :, 1, :]

                def evac_i(tb, aps, bbs, o=o):
                    S.op("pool", lambda e: e.tensor_tensor(out=t1[0], in0=z[:, tb, :].bitcast(F32), in1=bias[:, o, :], op=ALU.mult), [bz, bbias, bt1], [bt1])
                    S.op("dve", lambda e: e.tensor_tensor(out=t1[0], in0=t1[0], in1=aps[0], op=ALU.add), [bt1, bbs[0]], [bt1])
                    S.op("dve", lambda e: e.tensor_tensor(out=t1[0], in0=t1[0], in1=aps[1], op=ALU.add), [bt1, bbs[1]], [bt1])
                    S.op("pool", lambda e: e.tensor_tensor(out=z[:, tb, :], in0=t1[0], in1=gate[:, tb, :], op=ALU.mult), [bt1, bgate], [bz, bt1])
                dft_apply(self, hc, [hc["CP"], hc["SP"]], [(Yre, bY), (Yim, bY)], NB, nkc, 256, evac_i)
                S.barrier()
                ar.off = m2
            S.dma("sp", self.YT[s0:s0 + L, grp * 256:(grp + 1) * 256].rearrange("(c p) k -> p c k", p=128), z.bitcast(F32), self.bYT, [bz])


MK.phase_hyena = phase_hyena


def hy_host_consts(Ls):
    out = {}
    for L in Ls:
        N = 2 * L
        NB = L // 128 + 1
        a = np.arange(NB * 128, dtype=np.int64)
        ph = (a[:, None] * a[None, :]) % N
        valid = (a[:, None] <= L) & (a[None, :] <= L)
        ang = ph.astype(np.float64) * (2.0 * np.pi / N)
        out["CP%d" % L] = np.where(valid, np.cos(ang), 0.0).astype(np.float32)
        out["SP%d" % L] = np.where(valid, np.sin(ang), 0.0).astype(np.float32)
        t = np.linspace(0.0, 1.0, L, dtype=np.float32)[:, None]
        angf = (np.float32(2.0 * math.pi / L) * np.arange(L, dtype=np.float32)[:, None] * np.linspace(1e-4, 15, 16, dtype=np.float32)[None])
        feats = np.concatenate([t, np.cos(angf), -np.sin(angf)], -1).astype(np.float32)
        out["featsT%d" % L] = np.ascontiguousarray(feats.T)
        out["tcol%d" % L] = np.ascontiguousarray(t[:, 0].reshape(L // 128, 128).T)
        f = np.arange(NB * 128)
        w = np.where(f > L, 0.0, np.where((f == 0) | (f == L), 1.0, 2.0)) / N
        out["wcol%d" % L] = np.ascontiguousarray(w.reshape(NB, 128).T.astype(np.float32))
    out["deltas"] = np.abs(np.linspace(math.log(1e-2) / 1.5, math.log(1e-2) / 0.3, 512, dtype=np.float32)).astype(np.float32)
    return out


def batch_work(self, n, dvmax=192):
    ar = self.ar
    return dict(E=[ar.alloc([128]) for _ in range(n)], bE=[self.B("E") for _ in range(n)],
                ST=[ar.alloc([128], F32R) for _ in range(n)], bST=[self.B("ST") for _ in range(n)],
                Aw=[ar.alloc([128], F32R) for _ in range(n)], bAw=[self.B("Aw") for _ in range(n)],
                tmp=[ar.alloc([dvmax]) for _ in range(n)], btmp=[self.B("tmp") for _ in range(n)])


def scan_step(self, U, w):
    S = self.S
    bank, bb = self.banks, self.bbank
    for u in U:
        bk, off = u["D"]
        S.op("pe", lambda e, u=u, bk=bk, off=off: e.matmul(bank[bk][:, off:off + 128], lhsT=self.sel[0:self.selrows, u["row"], :],
                                                        rhs=u["rho"][0:self.selrows, u["c"] * 128:(u["c"] + 1) * 128], start=True, stop=False),
             [u["brow"], self.bconst], [bb[bk]])
        S.op("pe", lambda e, u=u, bk=bk, off=off: e.matmul(bank[bk][:, off:off + 128], lhsT=self.ident_bf, rhs=self.negmask_bf[:, u["d"], :], start=False, stop=True),
             [self.bconst], [bb[bk]])
    for u in U:
        j = u["j"]
        bk, off = u["D"]
        S.op("act", lambda e, u=u, j=j, bk=bk, off=off: e.activation(out=w["E"][j], in_=bank[bk][:, off:off + 128], func=AF.Exp,
                                                                   bias=u["cols"][:, u["c"], 0, u["ci"]:u["ci"] + 1]), [bb[bk], u["bcols"]], [w["bE"][j]])
    for u in U:
        j = u["j"]
        gk, goff = u["G"]
        S.op("dve", lambda e, u=u, j=j, gk=gk, goff=goff: e.tensor_tensor(out=w["ST"][j], in0=bank[gk][:, goff:goff + 128], in1=w["E"][j], op=ALU.mult),
             [bb[gk], w["bE"][j]], [w["bST"][j]])
    for u in U:
        j, dv = u["j"], u["dv"]
        bk, o1, bk2, o2 = u["O"]
        S.op("pe", lambda e, u=u, j=j, dv=dv, bk=bk, o1=o1: e.matmul(bank[bk][:, o1:o1 + dv], lhsT=w["ST"][j], rhs=u["Xr"], start=True, stop=True),
             [w["bST"][j], u["bX"]], [bb[bk]])
        S.op("pe", lambda e, u=u, dv=dv, bk2=bk2, o2=o2: e.matmul(bank[bk2][:, o2:o2 + dv], lhsT=u["CTc"], rhs=u["Sr"], start=True, stop=True),
             [u["bCT"], u["bS"]], [bb[bk2]])
    for u in U:
        j, dv = u["j"], u["dv"]
        bk, o1, bk2, o2 = u["O"]
        S.op("act", lambda e, j=j, dv=dv, bk=bk, o1=o1: e.activation(out=w["tmp"][j][:, 0:dv], in_=bank[bk][:, o1:o1 + dv], func=AF.Copy), [bb[bk]], [w["btmp"][j]])
    for u in U:
        j, dv = u["j"], u["dv"]
        bk_, o1, bk, o2 = u["O"]
        fac = u["cols"][:, u["c"], 1, u["ci"]:u["ci"] + 1]
        if not u["acc"]:
            S.op("dve", lambda e, u=u, j=j, dv=dv, bk=bk, o2=o2, fac=fac: e.scalar_tensor_tensor(out=u["yout"], in0=bank[bk][:, o2:o2 + dv], scalar=fac, in1=w["tmp"][j][:, 0:dv],
                                                                                          op0=ALU.mult, op1=ALU.add), [bb[bk], u["bcols"], w["btmp"][j]], [u["byout"]])
        else:
            S.op("dve", lambda e, u=u, j=j, dv=dv, bk=bk, o2=o2, fac=fac: e.scalar_tensor_tensor(out=w["tmp"][j][:, 0:dv], in0=bank[bk][:, o2:o2 + dv], scalar=fac, in1=w["tmp"][j][:, 0:dv],
                                                                                          op0=ALU.mult, op1=ALU.add), [bb[bk], u["bcols"], w["btmp"][j]], [w["btmp"][j]])
            S.op("pool", lambda e, u=u, j=j, dv=dv: e.tensor_tensor(out=u["yout"], in0=u["yout"], in1=w["tmp"][j][:, 0:dv], op=ALU.add), [w["btmp"][j], u["byout"]], [u["byout"]])
    for u in U:
        if u.get("post"):
            u["post"](u)
    for u in U:
        j = u["j"]
        S.op("pool", lambda e, u=u, j=j: e.tensor_scalar(out=w["Aw"][j], in0=u["Btm"], scalar1=u["cols"][:, u["c"], 2, u["ci"]:u["ci"] + 1], scalar2=None, op0=ALU.mult),
             [u["bB"], u["bcols"]], [w["bAw"][j]])
    for u in U:
        j, dv = u["j"], u["dv"]
        bk, off = u["Sb"]
        S.op("pe", lambda e, u=u, j=j, dv=dv, bk=bk, off=off: e.matmul(bank[bk][:, off:off + dv], lhsT=w["Aw"][j], rhs=u["Xr"], start=True, stop=True),
             [w["bAw"][j], u["bX"]], [bb[bk]])
    for u in U:
        dv = u["dv"]
        bk, off = u["Sb"]
        S.op("dve", lambda e, u=u, dv=dv, bk=bk, off=off: e.scalar_tensor_tensor(out=u["S32"], in0=u["S32"], scalar=u["SDcol"], in1=bank[bk][:, off:off + dv], op0=ALU.mult, op1=ALU.add),
             [u["bS"], u["bSD"], bb[bk]], [u["bS"]])
    for u in U:
        S.op("act", lambda e, u=u: e.activation(out=u["Sr"], in_=u["S32"], func=AF.Copy), [u["bS"]], [u["bS"]])


FULL_SEQS = [(0, 4096, 0), (4096, 256, 1), (4352, 256, 1), (4608, 256, 1), (4864, 256, 1)]
_CACHE = {}


def _pack_cw(w, b):
    n = w.shape[1] // 128
    a = np.concatenate([w, b[None]], 0)
    return np.ascontiguousarray(a.reshape(4, n, 128).transpose(2, 1, 0).reshape(128, n * 4))


def build_full(seqs=FULL_SEQS, depth=4):
    m = MK(dict(seqs=seqs))
    m.setup()
    hy_consts(m)
    S = m.S
    for l in range(depth):
        m.phase_mod(l)
        m.phase_in(l)
        m.phase_prep(l)
        if l % 2 == 0:
            m.phase_hyena(l)
            m.phase_mlstm(l)
        else:
            m.phase_ssd(l)
        m.phase_mod(l)
        m.phase_out(l)
        m.phase_ffn(l, final=(l == depth - 1), perm=((l == 1 or l == 3) and seqs[0][1] == 4096))
    S.barrier()
    S.finish("sp", [m.bYO, m.bso])
    S.emit()
    m.st.close()
    return m


def host_consts(seqs):
    f32 = np.float32
    c = hy_host_consts(sorted(set(s[1] for s in seqs)))
    c["ident"] = np.eye(128, dtype=f32)
    s_i, l_i = np.arange(128)[:, None], np.arange(128)[None, :]
    c["negmask"] = np.concatenate([np.where(s_i > l_i, NEG, 0.0), np.where(s_i < l_i, NEG, 0.0)], 1).astype(f32)
    return c


def kernel(**inp):
    f32 = np.float32
    if "m" not in _CACHE:
        _CACHE["m"] = build_full()
        _CACHE["c"] = host_consts(FULL_SEQS)
    m = _CACHE["m"]
    xp = np.asarray(inp["x_prompt"], f32)
    xs = np.asarray(inp["x_sample"], f32)
    c = np.asarray(inp["c"], f32)
    cctx = np.asarray(inp["c_ctx"], f32)
    shared = {k: np.ascontiguousarray(np.asarray(inp[k], f32)) for k in WSHAPES if k in inp}
    shared["ev_cw"] = np.stack([_pack_cw(shared["ev_conv_w"][e], shared["ev_conv_b"][e]) for e in range(2)])
    shared["od_cw"] = np.stack([_pack_cw(shared["od_conv_w"][e], shared["od_conv_b"][e]) for e in range(2)])
    shared.update(_CACHE["c"])
    sC = np.asarray(inp["state_mlstm_C"], f32)
    sn = np.asarray(inp["state_mlstm_n"], f32)
    sm = np.asarray(inp["state_mlstm_m"], f32)
    ss = np.asarray(inp["state_ssd"], f32)
    in_maps = []
    for b in range(8):
        d = dict(shared)
        d["xin"] = np.ascontiguousarray(np.concatenate([xs[b], xp[4 * b:4 * b + 4].reshape(1024, 1024)], 0))
        cond = np.stack([c[b], cctx], 0)
        d["condT"] = np.ascontiguousarray(cond.reshape(2, 8, 128).transpose(2, 0, 1).reshape(128, 16))
        d["st_ssd"] = np.ascontiguousarray(ss[b].reshape(-1, 128))
        d["st_C"] = np.ascontiguousarray(sC[b].reshape(-1, 128))
        d["st_n"] = np.ascontiguousarray(sn[b].reshape(-1, 128))
        d["st_m"] = np.ascontiguousarray(sm[b].reshape(1, 16))
        in_maps.append({k: v for k, v in d.items() if k in m.ins})
    res = run_bass_kernel_spmd(m.nc, in_maps, core_ids=list(range(8)))
    outs = res.results
    y_sample = np.stack([outs[b]["yout"][:4096] for b in range(8)], 0)
    y_prompt = np.concatenate([outs[b]["yout"][4096:].reshape(4, 256, 1024) for b in range(8)], 0)
    newC = np.concatenate([outs[b]["newC"].reshape(4, 2, 2, 4, 128, 128) for b in range(8)], 0)
    newn = np.concatenate([outs[b]["newn"].reshape(4, 2, 2, 4, 128) for b in range(8)], 0)
    newm = np.concatenate([outs[b]["newm"].reshape(4, 2, 2, 4) for b in range(8)], 0)
    newssd = np.concatenate([outs[b]["newssd"].reshape(4, 2, 2, 32, 64, 128) for b in range(8)], 0)
    return (y_prompt.astype(f32), y_sample.astype(f32), newC.astype(f32), newn.astype(f32), newm.astype(f32), newssd.astype(f32))
```
